# Optimizing a Trainium2 kernel written in Bass

```python
import math
import jax, jax.numpy as jnp
from jax import lax
import numpy as np

D_MODEL = 2048
BATCH = 4
SEQ = 4096
DEPTH = 2

HEAD_DIM = 128
N_HEADS = D_MODEL // HEAD_DIM
A_Q_HEADS = 6
A_KV_HEADS = 2
CMP_BLOCK = 32
CMP_STRIDE = 16
SLC_BLOCK = 64
SLC_TOPK = 16
NSA_WINDOW = 512
SLC_Q_CHUNK = 64
FORCED_SCORE = 1.0e4
B_Q_HEADS = 4
B_KV_HEADS = 2
SWA_WINDOW = 128
DIL_PAIRS = ((128, 1), (512, 4), (2048, 16))
C_HEADS_PER_PAIR = 2
C_Q_HEADS = C_HEADS_PER_PAIR * len(DIL_PAIRS)
C_KV_HEADS = len(DIL_PAIRS)
REL_BUCKETS = 32
REL_MAX_EXACT = 16
REL_MAX_DIST = 2048
BAND_BLOCK = 128
SCALE = HEAD_DIM ** -0.5
N_QK_NORMS = 8
D_FF = -(-8 * D_MODEL // (3 * 256)) * 256
SPLIT_SIZES = ((A_Q_HEADS * HEAD_DIM,) + (A_KV_HEADS * HEAD_DIM,) * 6 + (A_Q_HEADS * 3,)
               + (B_Q_HEADS * HEAD_DIM, B_KV_HEADS * HEAD_DIM, B_KV_HEADS * HEAD_DIM)
               + (C_Q_HEADS * HEAD_DIM, C_KV_HEADS * HEAD_DIM, C_KV_HEADS * HEAD_DIM))
N_IN = sum(SPLIT_SIZES)

kernel_name = "hybrid_nsa_swa_sink_dilated_block"


def rms_norm(x, g, eps=1e-6):
    xf = x.astype(jnp.float32)
    y = xf * lax.rsqrt(jnp.mean(xf * xf, axis=-1, keepdims=True) + eps)
    return (y * g.astype(jnp.float32)).astype(x.dtype)


def rel_bucket(dist):
    dist = jnp.maximum(dist, 0)
    far = jnp.maximum(dist, REL_MAX_EXACT).astype(jnp.float32)
    log_b = REL_MAX_EXACT + (jnp.log(far / REL_MAX_EXACT) / math.log(REL_MAX_DIST / REL_MAX_EXACT)
                             * (REL_BUCKETS - REL_MAX_EXACT)).astype(jnp.int32)
    log_b = jnp.minimum(log_b, REL_BUCKETS - 1)
    return jnp.where(dist < REL_MAX_EXACT, dist, log_b)


def banded_attention(q, k, v, head_bias, max_dist, dist_scale=1, sinks=None):
    n, L, hq, hd = q.shape
    hk = k.shape[2]
    grp = hq // hk
    nb = -(-L // BAND_BLOCK)
    Lp = nb * BAND_BLOCK
    n_prev = -(-max_dist // BAND_BLOCK)
    W = (n_prev + 1) * BAND_BLOCK
    pad_end = Lp - L
    qb = jnp.pad(q, ((0, 0), (0, pad_end), (0, 0), (0, 0))).reshape(n, nb, BAND_BLOCK, hk, grp, hd)
    kv_pad = ((0, 0), (n_prev * BAND_BLOCK, pad_end), (0, 0), (0, 0))
    kb = jnp.pad(k, kv_pad).reshape(n, nb + n_prev, BAND_BLOCK, hk, hd)
    vb = jnp.pad(v, kv_pad).reshape(n, nb + n_prev, BAND_BLOCK, hk, hd)
    kw = jnp.concatenate([kb[:, s:s + nb] for s in range(n_prev + 1)], axis=2)
    vw = jnp.concatenate([vb[:, s:s + nb] for s in range(n_prev + 1)], axis=2)
    qpos = jnp.arange(nb)[:, None] * BAND_BLOCK + jnp.arange(BAND_BLOCK)[None, :]
    kpos = (jnp.arange(nb)[:, None] - n_prev) * BAND_BLOCK + jnp.arange(W)[None, :]
    dist = qpos[:, :, None] - kpos[:, None, :]
    valid = (dist >= 0) & (dist <= max_dist) & (kpos[:, None, :] >= 0)
    bias = head_bias.astype(jnp.float32)[rel_bucket(dist * dist_scale)]
    bias = bias.reshape(nb, BAND_BLOCK, W, hk, grp).transpose(0, 3, 4, 1, 2)
    s = jnp.einsum('nbqkgd,nbckd->nbkgqc', qb, kw, preferred_element_type=jnp.float32) * SCALE + bias
    s = jnp.where(valid[:, None, None], s, -jnp.inf)
    lse = jax.nn.logsumexp(s, axis=-1)
    if sinks is None:
        total = lse
    else:
        total = jnp.logaddexp(lse, sinks.astype(jnp.float32).reshape(1, 1, hk, grp, 1))
    p = jnp.exp(s - total[..., None])
    out = jnp.einsum('nbkgqc,nbckd->nbqkgd', p.astype(v.dtype), vw).reshape(n, Lp, hq, hd)[:, :L]
    lse = lse.transpose(0, 1, 4, 2, 3).reshape(n, Lp, hq)[:, :L]
    return out.astype(q.dtype), lse


def nsa_compress(x, pe, w1, w2):
    B, T, hk, hd = x.shape
    n_cmp = (T - CMP_BLOCK) // CMP_STRIDE + 1
    tok = jnp.arange(n_cmp)[:, None] * CMP_STRIDE + jnp.arange(CMP_BLOCK)[None, :]
    blk = x[:, tok] + pe[:, None, :].astype(x.dtype)
    blk = blk.transpose(0, 1, 3, 2, 4).reshape(B, n_cmp, hk, CMP_BLOCK * hd)
    return jax.nn.gelu(blk @ w1) @ w2


def nsa_mixer(q, k_cmp, v_cmp, k_slc, v_slc, k_win, v_win, gates, k_cmp_gain, cmp_pe, cmp_w1, cmp_w2, head_bias):
    B, T, hq, hd = q.shape
    hk = k_cmp.shape[2]
    grp = hq // hk
    qg = q.reshape(B, T, hk, grp, hd)
    t_pos = jnp.arange(T)
    kc = rms_norm(nsa_compress(k_cmp, cmp_pe[0], cmp_w1[0], cmp_w2[0]), k_cmp_gain)
    vc = nsa_compress(v_cmp, cmp_pe[1], cmp_w1[1], cmp_w2[1])
    n_cmp = kc.shape[1]
    c_end = jnp.arange(n_cmp) * CMP_STRIDE + CMP_BLOCK - 1
    dist_c = t_pos[:, None] - c_end[None, :]
    bias_c = head_bias.astype(jnp.float32)[rel_bucket(dist_c)].transpose(2, 0, 1).reshape(hk, grp, T, n_cmp)
    s_c = jnp.einsum('btkgd,bnkd->bkgtn', qg, kc, preferred_element_type=jnp.float32) * SCALE + bias_c
    s_c = jnp.where(dist_c >= 0, s_c, -jnp.inf)
    m = jnp.max(s_c, axis=-1, keepdims=True)
    e = jnp.exp(s_c - jnp.where(jnp.isfinite(m), m, 0.0))
    den = jnp.sum(e, axis=-1, keepdims=True)
    p_c = e / jnp.where(den > 0, den, 1.0)
    o_cmp = jnp.einsum('bkgtn,bnkd->btkgd', p_c, vc.astype(jnp.float32)).reshape(B, T, hq, hd)
    n_slc = T // SLC_BLOCK
    k_sel = min(SLC_TOPK, n_slc)
    c0 = np.arange(n_cmp)[:, None] * CMP_STRIDE
    s0 = np.arange(n_slc)[None, :] * SLC_BLOCK
    overlap = np.clip(np.minimum(c0 + CMP_BLOCK, s0 + SLC_BLOCK) - np.maximum(c0, s0), 0, None) / CMP_BLOCK
    imp = jnp.einsum('bkgtn,nj->bktj', p_c, jnp.asarray(overlap, dtype=jnp.float32))
    blk_id = jnp.arange(n_slc)[None, :]
    cur = (t_pos // SLC_BLOCK)[:, None]
    forced = (blk_id == 0) | (blk_id == cur) | (blk_id == cur - 1)
    causal_blk = blk_id * SLC_BLOCK <= t_pos[:, None]
    imp = jnp.where(causal_blk, jnp.where(forced, FORCED_SCORE, imp), -jnp.inf)
    _, idx = lax.top_k(imp, k_sel)
    kb = k_slc.reshape(B, n_slc, SLC_BLOCK, hk, hd).transpose(0, 3, 1, 2, 4)
    vb = v_slc.reshape(B, n_slc, SLC_BLOCK, hk, hd).transpose(0, 3, 1, 2, 4)
    n_chunk = T // SLC_Q_CHUNK
    q_ch = qg.reshape(B, n_chunk, SLC_Q_CHUNK, hk, grp, hd).transpose(1, 0, 3, 2, 4, 5)
    idx_ch = idx.reshape(B, hk, n_chunk, SLC_Q_CHUNK, k_sel).transpose(2, 0, 1, 3, 4)
    starts = jnp.arange(n_chunk) * SLC_Q_CHUNK
    bi = jnp.arange(B)[:, None, None, None]
    ki = jnp.arange(hk)[None, :, None, None]
    hb = head_bias.astype(jnp.float32).reshape(REL_BUCKETS, hk, grp).transpose(1, 0, 2)

    def chunk_fn(args):
        qc, ic, t0 = args
        kg = kb[bi, ki, ic]
        vg = vb[bi, ki, ic]
        tq = t0 + jnp.arange(SLC_Q_CHUNK)
        kpos = ic[..., None] * SLC_BLOCK + jnp.arange(SLC_BLOCK)
        dist = tq[None, None, :, None, None] - kpos
        bias = hb[ki[..., None], rel_bucket(dist)].transpose(0, 1, 2, 5, 3, 4)
        s = jnp.einsum('bkqgd,bkqjpd->bkqgjp', qc, kg, preferred_element_type=jnp.float32) * SCALE + bias
        s = jnp.where(dist[:, :, :, None] >= 0, s, -jnp.inf)
        p = jax.nn.softmax(s.reshape(s.shape[:4] + (k_sel * SLC_BLOCK,)), axis=-1).reshape(s.shape)
        return jnp.einsum('bkqgjp,bkqjpd->bkqgd', p.astype(vg.dtype), vg)

    o_slc = lax.map(chunk_fn, (q_ch, idx_ch, starts))
    o_slc = o_slc.transpose(1, 0, 3, 2, 4, 5).reshape(B, T, hq, hd)
    o_win, _ = banded_attention(q, k_win, v_win, head_bias, NSA_WINDOW - 1)
    g = jax.nn.sigmoid(gates.astype(jnp.float32))
    o = (g[..., 0:1] * o_cmp + g[..., 1:2] * o_slc.astype(jnp.float32) + g[..., 2:3] * o_win.astype(jnp.float32))
    return o.astype(q.dtype)


def strided_window_attention(q, k, v, head_bias, window, dilation):
    B, T, hq, hd = q.shape
    Tp = -(-T // dilation) * dilation
    Ls = Tp // dilation

    def to_sub(t):
        t = jnp.pad(t, ((0, 0), (0, Tp - T), (0, 0), (0, 0)))
        h = t.shape[2]
        return t.reshape(B, Ls, dilation, h, hd).transpose(0, 2, 1, 3, 4).reshape(B * dilation, Ls, h, hd)

    o, lse = banded_attention(to_sub(q), to_sub(k), to_sub(v), head_bias, window // dilation, dist_scale=dilation)
    o = o.reshape(B, dilation, Ls, hq, hd).transpose(0, 2, 1, 3, 4).reshape(B, Tp, hq, hd)[:, :T]
    lse = lse.reshape(B, dilation, Ls, hq).transpose(0, 2, 1, 3).reshape(B, Tp, hq)[:, :T]
    return o, lse


def dilated_mixer(q, k, v, head_bias):
    B, T, _, hd = q.shape
    outs, lses = [], []
    for g, (w, d) in enumerate(DIL_PAIRS):
        sl = slice(g * C_HEADS_PER_PAIR, (g + 1) * C_HEADS_PER_PAIR)
        o, lse = strided_window_attention(q[:, :, sl], k[:, :, g:g + 1], v[:, :, g:g + 1], head_bias[:, sl], w, d)
        outs.append(o)
        lses.append(lse)
    alpha = jax.nn.softmax(jnp.stack(lses, axis=2), axis=2)
    o = jnp.stack(outs, axis=2).astype(jnp.float32) * alpha[..., None]
    return o.reshape(B, T, C_Q_HEADS, hd).astype(q.dtype)


def setup_inputs(seed: int = 0) -> dict:
    key = jax.random.key(seed)
    ks = jax.random.split(key, 14)
    f = jnp.float32
    nrm = jax.random.normal
    return {
        "x": nrm(ks[0], (BATCH, SEQ, D_MODEL), f),
        "norm_attn": 1.0 + 0.01 * nrm(ks[1], (DEPTH, D_MODEL), f),
        "w_in": nrm(ks[2], (DEPTH, D_MODEL, N_IN), f) * D_MODEL ** -0.5,
        "qk_gain": 1.0 + 0.01 * nrm(ks[3], (DEPTH, N_QK_NORMS, HEAD_DIM), f),
        "cmp_pe": 0.02 * nrm(ks[4], (DEPTH, 2, CMP_BLOCK, HEAD_DIM), f),
        "cmp_w1": nrm(ks[5], (DEPTH, 2, CMP_BLOCK * HEAD_DIM, HEAD_DIM), f) * (CMP_BLOCK * HEAD_DIM) ** -0.5,
        "cmp_w2": nrm(ks[6], (DEPTH, 2, HEAD_DIM, HEAD_DIM), f) * HEAD_DIM ** -0.5,
        "sinks": 0.5 * nrm(ks[7], (DEPTH, B_Q_HEADS), f),
        "rel_bias": 0.5 * nrm(ks[8], (REL_BUCKETS, N_HEADS), f),
        "w_out": nrm(ks[9], (DEPTH, D_MODEL, D_MODEL), f) * D_MODEL ** -0.5,
        "norm_ffn": 1.0 + 0.01 * nrm(ks[10], (DEPTH, D_MODEL), f),
        "w_gate": nrm(ks[11], (DEPTH, D_MODEL, D_FF), f) * D_MODEL ** -0.5,
        "w_up": nrm(ks[12], (DEPTH, D_MODEL, D_FF), f) * D_MODEL ** -0.5,
        "w_down": nrm(ks[13], (DEPTH, D_FF, D_MODEL), f) * D_FF ** -0.5,
    }


def reference(x, norm_attn, w_in, qk_gain, cmp_pe, cmp_w1, cmp_w2, sinks, rel_bias, w_out, norm_ffn, w_gate, w_up, w_down):
    B, T, _ = x.shape
    offsets = np.cumsum(SPLIT_SIZES)[:-1].tolist()
    bias_a = rel_bias[:, :A_Q_HEADS]
    bias_b = rel_bias[:, A_Q_HEADS:A_Q_HEADS + B_Q_HEADS]
    bias_c = rel_bias[:, A_Q_HEADS + B_Q_HEADS:]

    def heads(t, n):
        return t.reshape(B, T, n, HEAD_DIM)

    for l in range(DEPTH):
        g = qk_gain[l]
        h = rms_norm(x, norm_attn[l])
        proj = jnp.einsum('btd,dn->btn', h, w_in[l])
        (qa, kca, vca, ksa, vsa, kwa, vwa, ga, qb, kb, vb, qc, kc, vc) = jnp.split(proj, offsets, axis=-1)
        o_a = nsa_mixer(rms_norm(heads(qa, A_Q_HEADS), g[0]),
                        heads(kca, A_KV_HEADS), heads(vca, A_KV_HEADS),
                        rms_norm(heads(ksa, A_KV_HEADS), g[2]), heads(vsa, A_KV_HEADS),
                        rms_norm(heads(kwa, A_KV_HEADS), g[3]), heads(vwa, A_KV_HEADS),
                        ga.reshape(B, T, A_Q_HEADS, 3), g[1], cmp_pe[l], cmp_w1[l], cmp_w2[l], bias_a)
        o_b, _ = banded_attention(rms_norm(heads(qb, B_Q_HEADS), g[4]), rms_norm(heads(kb, B_KV_HEADS), g[5]),
                                  heads(vb, B_KV_HEADS), bias_b, SWA_WINDOW - 1, sinks=sinks[l])
        o_c = dilated_mixer(rms_norm(heads(qc, C_Q_HEADS), g[6]), rms_norm(heads(kc, C_KV_HEADS), g[7]),
                            heads(vc, C_KV_HEADS), bias_c)
        mix = jnp.concatenate([o_a.reshape(B, T, -1), o_b.reshape(B, T, -1), o_c.reshape(B, T, -1)], axis=-1)
        x = x + jnp.einsum('btm,md->btd', mix.astype(x.dtype), w_out[l])
        h = rms_norm(x, norm_ffn[l])
        x = x + jnp.einsum('btf,fd->btd', jax.nn.silu(h @ w_gate[l]) * (h @ w_up[l]), w_down[l])
    return x
```

```python
import math
from contextlib import ExitStack

import numpy as np
import ml_dtypes
import concourse.bass as bass
import concourse.mybir as mybir
from concourse.bass_utils import run_bass_kernel_spmd

F32 = mybir.dt.float32
BF16 = mybir.dt.bfloat16
AF = mybir.ActivationFunctionType
ALU = mybir.AluOpType

T = 4096
D = 2048
NT = T // 128
GT = 512
NG = T // GT
DFF = 5632
NIN = 4882
HD = 128
SCALE = HD ** -0.5
EPS = 1e-6
NEG = -30000.0

C_QA, C_KCA, C_VCA, C_KSA, C_VSA, C_KWA, C_VWA, C_GA = 0, 768, 1024, 1280, 1536, 1792, 2048, 2304
C_QB, C_KB, C_VB, C_QC, C_KC, C_VC = 2322, 2834, 3090, 3346, 4114, 4498
KT_KCMP, KT_VCMP, KT_KSLC, KT_KWIN, KT_KB, KT_KC = 0, 2, 4, 6, 8, 10
V_SLC, V_WIN, V_B, V_C = 0, 2, 4, 6
DIL = ((128, 1), (512, 4), (2048, 16))
C_NPREV = (1, 4, 16)


def _rel_bucket(dist):
    dist = np.maximum(dist, 0)
    far = np.maximum(dist, 16).astype(np.float32)
    lb = 16 + (np.log(far / np.float32(16)) / np.float32(math.log(2048 / 16)) * np.float32(16)).astype(np.int32)
    lb = np.minimum(lb, 31)
    return np.where(dist < 16, dist, lb).astype(np.int64)


def _table_plan():
    plan = {}
    tab_off = 0
    ecol = 0
    specs = []
    for h in range(6):
        specs.append((("slc", h), 127 + 4096, 1, 4096, 0))
    for h in range(6):
        specs.append((("win", h), 127 + 640, 1, 640, 0))
    for h in range(4):
        specs.append((("b", h), 127 + 256, 1, 256, 0))
    for g in range(3):
        for s in range(2):
            w = 128 * (C_NPREV[g] + 1)
            specs.append((("c", g, s), 127 + w, 1, w, 0))
    for h in range(6):
        specs.append((("cmp0", h), 8176, 16, 4096, 2048))
        specs.append((("cmp1", h), -1, 16, 2048, 2048))
    for (name, tlen, pstep, width, start) in specs:
        if tlen < 0:
            toff = plan[("cmp0", name[1])][0]
        else:
            toff = tab_off
            tab_off += tlen
        plan[name] = (toff, pstep, width, ecol, start)
        ecol += width
    return plan, tab_off, ecol


PLAN, TAB_LEN, ETOT = _table_plan()


def _build_tabs(rel_bias):
    tabs = np.full((TAB_LEN,), NEG, dtype=np.float32)

    def fill(name, col, dvals, valid):
        toff = PLAN[name][0]
        idx = _rel_bucket(np.where(valid, dvals, 0))
        vals = rel_bias[idx, col]
        seg = tabs[toff:toff + len(dvals)]
        seg[valid] = vals[valid]

    for h in range(6):
        d = np.arange(127 + 4096) - 127
        fill(("slc", h), h, d, d >= 0)
        d = np.arange(127 + 640) - 127
        fill(("win", h), h, d, (d >= 0) & (d <= 511))
        d = np.arange(8176) - 4111
        fill(("cmp0", h), h, d, d >= 0)
    for h in range(4):
        d = np.arange(127 + 256) - 127
        fill(("b", h), 6 + h, d, (d >= 0) & (d <= 127))
    for g in range(3):
        win, dil = DIL[g]
        for s in range(2):
            w = 128 * (C_NPREV[g] + 1)
            d = np.arange(127 + w) - 127
            fill(("c", g, s), 10 + 2 * g + s, d, (d >= 0) & (d <= win) & (d % dil == 0))
    return tabs


def _build_consts():
    bf = ml_dtypes.bfloat16
    ident = np.eye(128, dtype=np.float32).astype(bf)
    jrev = np.eye(128, dtype=np.float32)[::-1].copy().astype(bf)
    ones = np.ones((128, 128), dtype=np.float32).astype(bf)
    expn = np.zeros((64, 4096), dtype=np.float32)
    for j in range(64):
        expn[j, 64 * j:64 * j + 64] = -1.0
    expn = expn.astype(bf)
    c0 = np.arange(255)[:, None] * 16
    s0 = np.arange(64)[None, :] * 64
    ov = np.clip(np.minimum(c0 + 32, s0 + 64) - np.maximum(c0, s0), 0, None) / 32.0
    ovl = np.zeros((256, 64), dtype=np.float32)
    ovl[:255] = ov
    ovl = ovl.reshape(2, 128, 64).transpose(1, 0, 2).copy().astype(bf)
    tpos = np.arange(T)
    blk = np.arange(64)[None, :]
    cur = (tpos // 64)[:, None]
    forced = (blk == 0) | (blk == cur) | (blk == cur - 1)
    causal = blk * 64 <= tpos[:, None]
    keep = (causal & ~forced).astype(np.float32)
    add = np.where(causal, np.where(forced, 1.0e4, 0.0), -1.0e30).astype(np.float32)
    keep = keep.reshape(NT, 128, 64).transpose(1, 0, 2).reshape(128, NT * 64).copy()
    add = add.reshape(NT, 128, 64).transpose(1, 0, 2).reshape(128, NT * 64).copy()
    return dict(ident=ident, jrev=jrev, ones=ones, expn=expn, ovl=ovl, keepm=keep, addm=add)


class Buf:
    __slots__ = ("name", "w", "r", "excl")

    def __init__(self, name, excl=False):
        self.name = name
        self.w = {}
        self.r = {}
        self.excl = excl


class Prog:
    ENG = ("pe", "act", "dve", "pool", "sp")

    def __init__(self, nc, es):
        self.nc = nc
        self.eng = {"pe": nc.tensor, "act": nc.scalar, "dve": nc.vector, "pool": nc.gpsimd, "sp": nc.sync}
        self.streams = {k: [] for k in self.ENG}
        self.cnt = {}
        self.waited = {}
        self.sem = {}
        for k in list(self.ENG):
            self.sem[k] = es.enter_context(nc.semaphore("s_" + k))
            self.cnt[k] = 0
        self.NS = 24
        self.dnext = {"sp": 0, "pool": 0}
        for q in ("sp", "pool"):
            for i in range(self.NS):
                self.sem[(q, i)] = es.enter_context(nc.semaphore(f"d_{q}{i}"))
                self.cnt[(q, i)] = 0

    def _collect(self, me, reads, writes):
        need = {}
        for b in reads:
            for t, v in b.w.items():
                if v > need.get(t, 0):
                    need[t] = v
        for b in writes:
            for t, v in b.w.items():
                if v > need.get(t, 0):
                    need[t] = v
            for t, v in b.r.items():
                if v > need.get(t, 0):
                    need[t] = v
        out = []
        for t, v in need.items():
            if me == "pe" and t == "pe":
                continue
            if self.waited.get((me, t), 0) >= v:
                continue
            self.waited[(me, t)] = v
            out.append((t, v))
        return out

    def _mark(self, tgt, idx, reads, writes):
        for b in reads:
            if b.r.get(tgt, 0) < idx:
                b.r[tgt] = idx
        for b in writes:
            b.w = {tgt: idx}
            b.r = {}

    def op(self, e, fn, r=(), w=()):
        if any(b.excl for b in r):
            w = list(w) + [b for b in r if b.excl]
            r = [b for b in r if not b.excl]
        waits = self._collect(e, r, w)
        self.cnt[e] += 1
        idx = self.cnt[e]
        sem = self.sem

        def run(eng, waits=waits, fn=fn, s=sem[e]):
            for t, v in waits:
                eng.wait_ge(sem[t], v)
            fn(eng).then_inc(s, 1)

        self.streams[e].append(run)
        self._mark(e, idx, r, w)

    def dma(self, q, out, in_, r=(), w=(), merge=(), **kw):
        tgt = (q, self.dnext[q] % self.NS)
        self.dnext[q] += 1
        waits = self._collect(q, r, w)
        prev = self.cnt[tgt]
        if prev > 0 and self.waited.get((q, tgt), 0) < prev:
            self.waited[(q, tgt)] = prev
            waits.append((tgt, prev))
        self.cnt[tgt] += 16
        idx = self.cnt[tgt]
        sem = self.sem

        def run(eng, waits=waits, s=sem[tgt]):
            for t, v in waits:
                eng.wait_ge(sem[t], v)
            eng.dma_start(out=out, in_=in_, **kw).then_inc(s, 16)

        self.streams[q].append(run)
        self._mark(tgt, idx, r, w)
        for b in merge:
            b.w[tgt] = idx

    def barrier(self):
        for e in self.ENG:
            waits = []
            for t, v in self.cnt.items():
                if v == 0 or t == e:
                    continue
                if self.waited.get((e, t), 0) >= v:
                    continue
                self.waited[(e, t)] = v
                waits.append((t, v))
            if waits:
                sem = self.sem

                def run(eng, waits=waits):
                    for t, v in waits:
                        eng.wait_ge(sem[t], v)

                self.streams[e].append(run)

    def emit(self):
        with self.nc.Block() as block:
            @block.tensor
            def _(e):
                for f in self.streams["pe"]:
                    f(e)

            @block.scalar
            def _(e):
                for f in self.streams["act"]:
                    f(e)

            @block.vector
            def _(e):
                for f in self.streams["dve"]:
                    f(e)

            @block.gpsimd
            def _(e):
                for f in self.streams["pool"]:
                    f(e)

            @block.sync
            def _(e):
                for f in self.streams["sp"]:
                    f(e)


class Arena:
    def __init__(self, ap, nelem):
        self.ap = ap
        self.n = nelem
        self.off = 0

    def alloc(self, nfree, dtype=BF16):
        nb = nfree * (4 if dtype == F32 else 2)
        nb = (nb + 63) // 64 * 64
        ne = nb // 2
        assert self.off + ne <= self.n, ("arena overflow", self.off, ne, self.n)
        a = self.ap[:, self.off:self.off + ne]
        self.off += ne
        if dtype == F32:
            return a.bitcast(F32)[:, :nfree]
        return a[:, :nfree]


class Rot:
    def __init__(self, items):
        self.items = items
        self.i = 0

    def next(self):
        it = self.items[self.i % len(self.items)]
        self.i += 1
        return it


def run_pipeline(stages):
    handles = [None] * len(stages)
    if stages:
        handles[0] = stages[0][0]()
    for n in range(len(stages)):
        if n + 1 < len(stages):
            handles[n + 1] = stages[n + 1][0]()
        stages[n][1](handles[n])


def build_program(n_layers=2, dbg=False, phases=("p0", "p1", "p2", "p3")):
    nc = bass.Bass("TRN2", target_bir_lowering=False)
    es = ExitStack()
    P = Prog(nc, es)

    def din(name, shape, dt=F32):
        return nc.dram_tensor(name, list(shape), dt, kind="ExternalInput")

    x_in = din("x", [T, D]).ap()
    w_in_f = din("w_in", [2, D, NIN]).ap()
    w_out_f = din("w_out", [2, D, D]).ap()
    w_g_f = din("w_gate", [2, D, DFF]).ap()
    w_u_f = din("w_up", [2, D, DFF]).ap()
    w_d_f = din("w_down", [2, DFF, D]).ap()
    norm_attn = din("norm_attn", [2, D]).ap()
    norm_ffn = din("norm_ffn", [2, D]).ap()
    qk_gain = din("qk_gain", [2, 8, HD]).ap()
    cmp_pe = din("cmp_pe", [2, 2, 32, HD]).ap()
    cmp_w1 = din("cmp_w1", [2, 2, 4096, HD]).ap()
    cmp_w2 = din("cmp_w2", [2, 2, HD, HD]).ap()
    sinks = din("sinks", [2, 4]).ap()
    tabs_t = din("tabs", [TAB_LEN])
    c_ident = din("c_ident", [128, 128], BF16).ap()
    c_jrev = din("c_jrev", [128, 128], BF16).ap()
    c_ones = din("c_ones", [128, 128], BF16).ap()
    c_expn = din("c_expn", [64, 4096], BF16).ap()
    c_ovl = din("c_ovl", [128, 2, 64], BF16).ap()
    c_keep = din("c_keepm", [128, NT * 64]).ap()
    c_add = din("c_addm", [128, NT * 64]).ap()

    okind = "ExternalOutput" if dbg else "Internal"
    y_out = nc.dram_tensor("y", [T, D], F32, kind="ExternalOutput").ap()
    xres = nc.dram_tensor("xres", [T, D], F32, kind="Internal").ap()
    wb_in = nc.dram_tensor("wb_in", [2, D, NIN], BF16, kind="Internal").ap()
    wb_out = nc.dram_tensor("wb_out", [2, D, D], BF16, kind="Internal").ap()
    wb_g = nc.dram_tensor("wb_g", [2, D, DFF], BF16, kind="Internal").ap()
    wb_u = nc.dram_tensor("wb_u", [2, D, DFF], BF16, kind="Internal").ap()
    wb_d = nc.dram_tensor("wb_d", [2, DFF, D], BF16, kind="Internal").ap()
    etab = nc.dram_tensor("etab", [128, ETOT], BF16, kind="Internal").ap()
    qt = nc.dram_tensor("qt", [16, 128, T], BF16, kind=okind).ap()
    kt = nc.dram_tensor("kt", [13, 128, T], BF16, kind=okind).ap()
    vv = nc.dram_tensor("vv", [9, T, 128], BF16, kind=okind).ap()
    gates = nc.dram_tensor("gates", [T, 18], F32, kind=okind).ap()
    mix = nc.dram_tensor("mix", [T, D], BF16, kind=okind).ap()
    dbgA = nc.dram_tensor("dbgA", [3, T, 128], F32, kind=okind).ap() if dbg else None
    B_dbgA = Buf("dbgA")

    B_wb = {(n, l): Buf(f"{n}{l}") for n in ("in", "out", "g", "u", "d") for l in range(2)}
    B_etab = Buf("etab")
    B_qt = [Buf(f"qt{h}") for h in range(16)]
    B_kt = [Buf(f"kt{h}") for h in range(13)]
    B_vv = [Buf(f"vv{h}") for h in range(9)]
    B_gates = Buf("gates")
    B_mix = Buf("mix")
    B_xres = Buf("xres")
    B_y = Buf("y")

    ARENA_N = 100352
    arena_t = es.enter_context(nc.sbuf_tensor("arena", [128, ARENA_N], BF16))
    ident = es.enter_context(nc.sbuf_tensor("ident", [128, 128], BF16))
    jrev = es.enter_context(nc.sbuf_tensor("jrev", [128, 128], BF16))
    ones = es.enter_context(nc.sbuf_tensor("ones", [128, 128], BF16))
    prm = es.enter_context(nc.sbuf_tensor("prm", [128, 64], F32))
    B_const = Buf("const")
    B_prm = Buf("prm")
    A = Arena(arena_t[:], ARENA_N)

    ps_t = [es.enter_context(nc.psum_tensor(f"ps{i}", [128, 512], F32)) for i in range(6)]
    pst_t = [es.enter_context(nc.psum_tensor(f"pst{i}", [128, 1024], BF16)) for i in range(2)]
    ps = [(t[:], Buf(f"ps{i}", excl=True)) for i, t in enumerate(ps_t)]
    pst = [(pst_t[i][:, 0:512], Buf(f"pst{i}", excl=True)) for i in range(2)]

    P.dma("sp", ident[:], c_ident, w=[B_const])
    P.dma("sp", jrev[:], c_jrev, w=[B_const])
    P.dma("sp", ones[:], c_ones, w=[B_const])

    cast_thr = [Buf(f"castthr{i}") for i in range(2)]
    cast_n = [0]

    def cast_weights(l):
        for (nm, src, dst, rows) in (("in", w_in_f, wb_in, D), ("out", w_out_f, wb_out, D), ("g", w_g_f, wb_g, D),
                                     ("u", w_u_f, wb_u, D), ("d", w_d_f, wb_d, DFF)):
            for r0 in range(0, rows, 256):
                cast_n[0] += 1
                P.dma("pool", dst[l, r0:r0 + 256, :], src[l, r0:r0 + 256, :], w=[cast_thr[cast_n[0] % 2]],
                      merge=[B_wb[(nm, l)]])

    for l in range(n_layers):
        cast_weights(l)

    def phase0():
        A.off = 0
        CH = 2048
        Hs = [(A.alloc(CH, F32), Buf(f"H{i}")) for i in range(2)]
        Hbs = [(A.alloc(CH), Buf(f"Hb{i}")) for i in range(2)]
        Ebs = [(A.alloc(CH), Buf(f"Eb{i}")) for i in range(2)]
        psr = Rot(ps[0:4])
        k = 0
        for name, (toff, pstep, width, ecol, start) in PLAN.items():
            for c0 in range(0, width, CH):
                w = min(CH, width - c0)
                H, bH = Hs[k % 2]
                Hb, bHb = Hbs[k % 2]
                Eb, bEb = Ebs[k % 2]
                k += 1
                src = bass.AP(tabs_t, toff + start + c0, [[pstep, 128], [1, w]])
                P.dma("sp", H[:, :w], src, w=[bH])
                P.op("act", lambda e, H=H, Hb=Hb, w=w: e.activation(out=Hb[:, :w], in_=H[:, :w], func=AF.Exp),
                     r=[bH], w=[bHb])
                for cc in range(0, w, 512):
                    n = min(512, w - cc)
                    pp, bpp = psr.next()
                    P.op("pe", lambda e, pp=pp, Hb=Hb, cc=cc, n=n: e.matmul(pp[:, :n], jrev[:], Hb[:, cc:cc + n],
                                                                            start=True, stop=True),
                         r=[bHb, B_const], w=[bpp])
                    P.op("dve", lambda e, pp=pp, Eb=Eb, cc=cc, n=n: e.tensor_copy(out=Eb[:, cc:cc + n], in_=pp[:, :n]),
                         r=[bpp], w=[bEb])
                P.dma("pool", etab[:, ecol + c0:ecol + c0 + w], Eb[:, :w], r=[bEb], w=[B_etab])
        P.barrier()

    def load_params(l):
        P.dma("sp", prm[:, 0:16], norm_attn[l].rearrange("(c p) -> p c", p=128), w=[B_prm],
              allow_slow_non_contiguous=True)
        P.dma("sp", prm[:, 16:32], norm_ffn[l].rearrange("(c p) -> p c", p=128), w=[B_prm],
              allow_slow_non_contiguous=True)
        P.dma("sp", prm[:, 32:40], qk_gain[l].rearrange("n d -> d n"), w=[B_prm], allow_slow_non_contiguous=True)
        P.dma("sp", prm[:, 40:44], sinks[l].partition_broadcast(128), w=[B_prm], allow_slow_non_contiguous=True)
        P.op("dve", lambda e: e.tensor_scalar_mul(out=prm[:, 0:32], in0=prm[:, 0:32], scalar1=float(math.sqrt(D))),
             r=[B_prm], w=[B_prm])
        P.op("dve", lambda e: e.tensor_scalar_mul(out=prm[:, 32:40], in0=prm[:, 32:40], scalar1=float(math.sqrt(HD))),
             r=[B_prm], w=[B_prm])
        P.op("act", lambda e: e.activation(out=prm[:, 40:44], in_=prm[:, 40:44], func=AF.Exp), r=[B_prm], w=[B_prm])

    def norm_transpose(xts, xns, junk, ssb, hT, bhT, gcol):
        junk_ap, bjunk = junk
        ss_ap, bss = ssb
        P.op("dve", lambda e: e.memset(ss_ap, 0.0), w=[bss])
        for j in range(4):
            xt, bxt = xts[j]
            xn, bxn = xns[j]
            P.op("act", lambda e, xt=xt, j=j: e.activation(out=junk_ap, in_=xt, func=AF.Square,
                                                          accum_out=ss_ap[:, j:j + 1]),
                 r=[bxt], w=[bjunk, bss])
            P.op("act", lambda e, j=j: e.activation(out=ss_ap[:, 4 + j:5 + j], in_=ss_ap[:, j:j + 1], func=AF.Sqrt,
                                                    bias=float(D * EPS)), r=[bss], w=[bss])
            P.op("dve", lambda e, j=j: e.reciprocal(out=ss_ap[:, 4 + j:5 + j], in_=ss_ap[:, 4 + j:5 + j]),
                 r=[bss], w=[bss])
            P.op("dve", lambda e, xt=xt, xn=xn, j=j: e.tensor_scalar(out=xn, in0=xt, scalar1=ss_ap[:, 4 + j:5 + j],
                                                                     scalar2=None, op0=ALU.mult),
                 r=[bxt, bss], w=[bxn])
        transpose_group([x[0] for x in xns], [x[1] for x in xns], hT, bhT, gcol)

    def transpose_group(srcs, bsrcs, hT, bhT, gcol):
        for c in range(16):
            pp, bpp = pst[c % 2]

            def tr(e, pp=pp, c=c):
                ins = None
                for j in range(4):
                    ins = e.transpose(pp[:, j * 128:(j + 1) * 128], srcs[j][:, c * 128:(c + 1) * 128], ident[:])
                return ins

            P.op("pe", tr, r=list(bsrcs) + [B_const], w=[bpp])
            if gcol is None:
                if c % 2 == 0:
                    P.op("act", lambda e, pp=pp, c=c: e.copy(out=hT[:, c, :], in_=pp), r=[bpp], w=[bhT])
                else:
                    P.op("dve", lambda e, pp=pp, c=c: e.tensor_copy(out=hT[:, c, :], in_=pp), r=[bpp], w=[bhT])
            else:
                if c % 2 == 0:
                    P.op("act", lambda e, pp=pp, c=c: e.activation(out=hT[:, c, :], in_=pp, func=AF.Copy,
                                                                   scale=prm[:, gcol + c:gcol + c + 1]),
                         r=[bpp, B_prm], w=[bhT])
                else:
                    P.op("dve", lambda e, pp=pp, c=c: e.tensor_scalar(out=hT[:, c, :], in0=pp,
                                                                      scalar1=prm[:, gcol + c:gcol + c + 1],
                                                                      scalar2=None, op0=ALU.mult),
                         r=[bpp, B_prm], w=[bhT])

    def fm(tensor, bufs, h, gain):
        return (tensor, bufs, h, gain)

    BLOCKS = [
        (0, 512, "fm", [(qt, B_qt, 0, 0), (qt, B_qt, 1, 0), (qt, B_qt, 2, 0), (qt, B_qt, 3, 0)]),
        (512, 512, "fm", [(qt, B_qt, 4, 0), (qt, B_qt, 5, 0), (kt, B_kt, KT_KCMP, None), (kt, B_kt, KT_KCMP + 1, None)]),
        (1024, 512, "fm", [(kt, B_kt, KT_VCMP, None), (kt, B_kt, KT_VCMP + 1, None), (kt, B_kt, KT_KSLC, 2),
                           (kt, B_kt, KT_KSLC + 1, 2)]),
        (1536, 256, "tm", (V_SLC, 2, False)),
        (1792, 256, "fm", [(kt, B_kt, KT_KWIN, 3), (kt, B_kt, KT_KWIN + 1, 3)]),
        (2048, 274, "tm", (V_WIN, 2, True)),
        (2322, 512, "fm", [(qt, B_qt, 6, 4), (qt, B_qt, 7, 4), (qt, B_qt, 8, 4), (qt, B_qt, 9, 4)]),
        (2834, 256, "fm", [(kt, B_kt, KT_KB, 5), (kt, B_kt, KT_KB + 1, 5)]),
        (3090, 256, "tm", (V_B, 2, False)),
        (3346, 512, "fm", [(qt, B_qt, 10, 6), (qt, B_qt, 11, 6), (qt, B_qt, 12, 6), (qt, B_qt, 13, 6)]),
        (3858, 512, "fm", [(qt, B_qt, 14, 6), (qt, B_qt, 15, 6), (kt, B_kt, KT_KC, 7), (kt, B_kt, KT_KC + 1, 7)]),
        (4370, 128, "fm", [(kt, B_kt, KT_KC + 2, 7)]),
        (4498, 384, "tm", (V_C, 3, False)),
    ]

    def phase1(l, x_src, B_xsrc):
        A.off = 0
        xts = [(A.alloc(D, F32), Buf(f"xt{j}")) for j in range(4)]
        xns = [(A.alloc(D), Buf(f"xn{j}")) for j in range(4)]
        junk = (A.alloc(D), Buf("junk"))
        ssb = (A.alloc(8, F32), Buf("ss"))
        hT2 = A.alloc(16 * 512)
        hT = hT2.rearrange("p (c t) -> p c t", c=16)
        bhT = Buf("hT")
        wbufs = Rot([(A.alloc(16 * 512).rearrange("p (c n) -> p c n", c=16), Buf(f"wb{i}")) for i in range(3)])
        sqs = Rot([(A.alloc(512), Buf(f"sq{i}")) for i in range(2)])
        rrs = Rot([(A.alloc(512, F32), Buf(f"rr{i}")) for i in range(2)])
        ots = Rot([(A.alloc(512), Buf(f"ot{i}")) for i in range(4)])
        vts = Rot([(A.alloc(384), Buf(f"vt{i}")) for i in range(3)])
        gts = Rot([(A.alloc(18, F32), Buf(f"gt{i}")) for i in range(3)])
        psA = Rot(ps[0:2])
        psB = Rot(ps[2:4])
        psT = Rot(ps[4:6])

        stages = []
        for g in range(NG):
            t0 = g * GT

            def ld_x(t0=t0):
                for j in range(4):
                    P.dma("sp", xts[j][0], x_src[t0 + 128 * j:t0 + 128 * (j + 1), :], r=[B_xsrc], w=[xts[j][1]])

            def cp_x(_h):
                norm_transpose(xts, xns, junk, ssb, hT, bhT, 0)

            stages.append((ld_x, cp_x))
            for (c0, ncols, kind, items) in BLOCKS:
                def ld_w(c0=c0, ncols=ncols):
                    wb, bwb = wbufs.next()
                    P.dma("sp", wb[:, :, :ncols], wb_in[l, :, c0:c0 + ncols].rearrange("(c p) n -> p c n", p=128),
                          r=[B_wb[("in", l)]], w=[bwb])
                    return wb, bwb

                if kind == "fm":
                    def cp(hnd, items=items, t0=t0):
                        wb, bwb = hnd
                        for hi, (tens, tbufs, h, gain) in enumerate(items):
                            pa, bpa = psA.next()

                            def mmf(e, pa=pa, wb=wb, hi=hi):
                                ins = None
                                for kc in range(16):
                                    ins = e.matmul(pa, wb[:, kc, hi * 128:(hi + 1) * 128], hT[:, kc, :],
                                                   start=(kc == 0), stop=(kc == 15))
                                return ins

                            P.op("pe", mmf, r=[bwb, bhT], w=[bpa])
                            ot, bot = ots.next()
                            if gain is None:
                                P.op("act", lambda e, ot=ot, pa=pa: e.copy(out=ot, in_=pa), r=[bpa], w=[bot])
                            else:
                                sq, bsq = sqs.next()
                                rr, brr = rrs.next()
                                pb, bpb = psB.next()
                                P.op("act", lambda e, sq=sq, pa=pa: e.activation(out=sq, in_=pa, func=AF.Square),
                                     r=[bpa], w=[bsq])
                                P.op("pe", lambda e, pb=pb, sq=sq: e.matmul(pb, ones[:], sq, start=True, stop=True),
                                     r=[bsq, B_const], w=[bpb])
                                P.op("act", lambda e, rr=rr, pb=pb: e.activation(out=rr, in_=pb, func=AF.Sqrt,
                                                                                 bias=float(HD * EPS)),
                                     r=[bpb], w=[brr])
                                P.op("dve", lambda e, rr=rr: e.reciprocal(out=rr, in_=rr), r=[brr], w=[brr])
                                P.op("dve", lambda e, ot=ot, pa=pa, rr=rr, gain=gain: e.scalar_tensor_tensor(
                                    out=ot, in0=pa, scalar=prm[:, 32 + gain:33 + gain], in1=rr, op0=ALU.mult,
                                    op1=ALU.mult), r=[bpa, brr, B_prm], w=[bot])
                            P.dma("pool", tens[h, :, t0:t0 + GT], ot, r=[bot], w=[tbufs[h]])
                else:
                    def cp(hnd, items=items, t0=t0, ncols=ncols):
                        wb, bwb = hnd
                        vh0, nh, has_g = items
                        for j in range(4):
                            pt_, bpt = psT.next()

                            def mmt(e, pt_=pt_, wb=wb, j=j, ncols=ncols):
                                ins = None
                                for kc in range(16):
                                    ins = e.matmul(pt_[:, :ncols], hT[:, kc, j * 128:(j + 1) * 128], wb[:, kc, :ncols],
                                                   start=(kc == 0), stop=(kc == 15))
                                return ins

                            P.op("pe", mmt, r=[bwb, bhT], w=[bpt])
                            vt, bvt = vts.next()
                            P.op("act", lambda e, vt=vt, pt_=pt_, nh=nh: e.copy(out=vt[:, :nh * 128],
                                                                                in_=pt_[:, :nh * 128]),
                                 r=[bpt], w=[bvt])
                            r0 = t0 + 128 * j
                            P.dma("pool", vv[vh0:vh0 + nh, r0:r0 + 128, :].rearrange("h p d -> p h d"),
                                  vt[:, :nh * 128].rearrange("p (h d) -> p h d", h=nh), r=[bvt],
                                  w=[B_vv[vh0 + i] for i in range(nh)])
                            if has_g:
                                gt, bgt = gts.next()
                                P.op("act", lambda e, gt=gt, pt_=pt_: e.activation(out=gt, in_=pt_[:, 256:274],
                                                                                    func=AF.Sigmoid),
                                     r=[bpt], w=[bgt])
                                P.dma("pool", gates[r0:r0 + 128, :], gt, r=[bgt], w=[B_gates])
                stages.append((ld_w, cp))
        run_pipeline(stages)
        P.barrier()

    def phase2(l):
        A.off = 0
        kcT = A.alloc(2 * 256).rearrange("p (k n) -> p k n", k=2)
        b_kcT = Buf("kcT")
        VC1 = A.alloc(2 * 2 * 200).rearrange("p (k a n) -> p k a n", k=2, a=2)
        b_VC1 = Buf("VC1")
        expn = A.alloc(4096)
        b_expn = Buf("expn")
        keepm = A.alloc(NT * 64, F32)
        addm = A.alloc(NT * 64, F32)
        b_km = Buf("km")
        gsb = A.alloc(NT * 18, F32).rearrange("p (i c) -> p i c", c=18)
        b_gsb = Buf("gsb")
        mark0 = A.off
        P.dma("sp", expn[0:64, :], c_expn, w=[b_expn])
        P.dma("sp", keepm, c_keep, w=[b_km])
        P.dma("sp", addm, c_add, w=[b_km])
        P.dma("sp", gsb, gates.rearrange("(i p) c -> p i c", p=128), r=[B_gates], w=[b_gsb])

        XTs = [(A.alloc(T), Buf(f"XT{i}")) for i in range(2)]
        W1s = [(A.alloc(32 * 128).rearrange("p (l f) -> p l f", l=32), Buf(f"W1{i}")) for i in range(2)]
        peTf = A.alloc(32, F32)
        peT = A.alloc(32)
        b_peT = Buf("peT")
        W2 = A.alloc(128)
        b_W2 = Buf("W2")
        cb = A.alloc(1, F32)
        b_cb = Buf("cb")
        uu = A.alloc(256, F32)
        t1 = A.alloc(256, F32)
        sg = A.alloc(256, F32)
        b_u = Buf("u")
        GT_ = A.alloc(256)
        b_G = Buf("G")
        sq = A.alloc(256)
        b_sq = Buf("sqc")
        rr = A.alloc(256, F32)
        b_rr = Buf("rrc")
        P.op("dve", lambda e: e.memset(GT_, 0.0), w=[b_G])
        P.op("dve", lambda e: e.memset(VC1.rearrange("p k a n -> p (k a n)"), 0.0), w=[b_VC1])
        it = 0
        for k in range(2):
            for kv in range(2):
                XT, bXT = XTs[it % 2]
                W1, bW1 = W1s[it % 2]
                it += 1
                src_h = (KT_KCMP if kv == 0 else KT_VCMP) + k
                P.dma("sp", XT, kt[src_h], r=[B_kt[src_h]], w=[bXT])
                P.dma("pool", W1, cmp_w1[l, kv].rearrange("(l d) f -> d l f", d=128), w=[bW1])
                P.dma("pool", W2, cmp_w2[l, kv], w=[b_W2])
                P.dma("sp", peTf, cmp_pe[l, kv].rearrange("l d -> d l"), w=[b_peT], allow_slow_non_contiguous=True)
                P.op("dve", lambda e: e.tensor_copy(out=peT, in_=peTf), r=[b_peT], w=[b_peT])
                p0, bp0 = ps[0]
                p1, bp1 = ps[1]
                p2, bp2 = ps[2]

                def mm_h(e, XT=XT, W1=W1):
                    ins = None
                    for li in range(32):
                        ins = e.matmul(p0[:, 0:255], W1[:, li, :], XT[:, li:li + 16 * 254 + 1:16], start=(li == 0),
                                       stop=(li == 31))
                    return ins

                P.op("pe", mm_h, r=[bXT, bW1], w=[bp0])

                def mm_c(e, W1=W1):
                    ins = None
                    for li in range(32):
                        ins = e.matmul(p1[:, 0:1], W1[:, li, :], peT[:, li:li + 1], start=(li == 0), stop=(li == 31))
                    return ins

                P.op("pe", mm_c, r=[b_peT, bW1], w=[bp1])
                P.op("dve", lambda e: e.tensor_copy(out=cb, in_=p1[:, 0:1]), r=[bp1], w=[b_cb])
                P.op("act", lambda e: e.activation(out=uu[:, 0:255], in_=p0[:, 0:255], func=AF.Identity, bias=cb),
                     r=[bp0, b_cb], w=[b_u])
                P.op("dve", lambda e: e.tensor_tensor(out=t1[:, 0:255], in0=uu[:, 0:255], in1=uu[:, 0:255],
                                                      op=ALU.mult), r=[b_u], w=[b_u])
                P.op("dve", lambda e: e.tensor_scalar(out=t1[:, 0:255], in0=t1[:, 0:255], scalar1=0.044715, scalar2=1.0,
                                                      op0=ALU.mult, op1=ALU.add), r=[b_u], w=[b_u])
                P.op("dve", lambda e: e.tensor_tensor(out=t1[:, 0:255], in0=t1[:, 0:255], in1=uu[:, 0:255],
                                                      op=ALU.mult), r=[b_u], w=[b_u])
                P.op("act", lambda e: e.activation(out=sg[:, 0:255], in_=t1[:, 0:255], func=AF.Sigmoid,
                                                   scale=1.5957691216057308), r=[b_u], w=[b_u])
                P.op("dve", lambda e: e.tensor_tensor(out=GT_[:, 0:255], in0=uu[:, 0:255], in1=sg[:, 0:255],
                                                      op=ALU.mult), r=[b_u], w=[b_G])
                if kv == 0:
                    P.op("pe", lambda e: e.matmul(p2[:, 0:256], W2, GT_, start=True, stop=True), r=[b_W2, b_G],
                         w=[bp2])
                    P.op("act", lambda e: e.activation(out=sq, in_=p2[:, 0:256], func=AF.Square), r=[bp2], w=[b_sq])
                    P.op("pe", lambda e: e.matmul(p1[:, 0:256], ones[:], sq, start=True, stop=True),
                         r=[b_sq, B_const, b_cb], w=[bp1])
                    P.op("act", lambda e: e.activation(out=rr, in_=p1[:, 0:256], func=AF.Sqrt, bias=float(HD * EPS)),
                         r=[bp1], w=[b_rr])
                    P.op("dve", lambda e: e.reciprocal(out=rr, in_=rr), r=[b_rr], w=[b_rr])
                    P.op("dve", lambda e, k=k: e.scalar_tensor_tensor(out=kcT[:, k, :], in0=p2[:, 0:256],
                                                                      scalar=prm[:, 33:34], in1=rr, op0=ALU.mult,
                                                                      op1=ALU.mult),
                         r=[bp2, b_rr, B_prm], w=[b_kcT])
                else:
                    for a in range(2):
                        pa, bpa = ps[3 + a]
                        P.op("pe", lambda e, pa=pa, a=a: e.matmul(pa[:, 0:128], GT_[:, a * 128:(a + 1) * 128], W2,
                                                                  start=True, stop=True), r=[b_W2, b_G], w=[bpa])
                        P.op("act", lambda e, pa=pa, a=a, k=k: e.copy(out=VC1[:, k, a, 0:128], in_=pa[:, 0:128]),
                             r=[bpa], w=[b_VC1])
        for k in range(2):
            for a in range(2):
                P.op("dve", lambda e, k=k, a=a: e.memset(VC1[:, k, a, 128:129], 1.0), w=[b_VC1])
                P.dma("sp", VC1[:, k, a, 129:193], c_ovl[:, a, :], w=[b_VC1])
        P.barrier()

        A.off = mark0
        psS = Rot(ps[0:3])
        psO = Rot(ps[3:6])
        PTs = Rot([(A.alloc(512), Buf(f"PT{i}")) for i in range(4)])
        sm = A.alloc(64, F32)
        b_sm = Buf("sm")
        mark1 = A.off

        def attn(qT, bq, KT, bK, V1, bV, E, bE, i, jlist, po, bpo, ncol, mask=None):
            nchunks = (len(jlist) + 3) // 4
            for c in range(nchunks):
                chunk = jlist[4 * c:4 * c + 4]
                n = 128 * len(chunk)
                pS, bpS = psS.next()
                PT, bPT = PTs.next()

                def mmS(e, chunk=chunk, pS=pS):
                    ins = None
                    for jj, j in enumerate(chunk):
                        ins = e.matmul(pS[:, jj * 128:(jj + 1) * 128], KT[:, j * 128:(j + 1) * 128], qT, start=True,
                                       stop=(mask is None))
                        if mask is not None:
                            ins = e.matmul(pS[:, jj * 128:(jj + 1) * 128], expn[0:64, j * 128:(j + 1) * 128],
                                           mask[0][0:64, :], start=False, stop=True)
                    return ins

                rds = [bq, bK] + ([b_expn, mask[1]] if mask is not None else [])
                P.op("pe", mmS, r=rds, w=[bpS])
                P.op("act", lambda e, PT=PT, pS=pS, n=n: e.activation(out=PT[:, :n], in_=pS[:, :n], func=AF.Exp,
                                                                      scale=float(SCALE)), r=[bpS], w=[bPT])
                d0 = i - chunk[0]
                P.op("dve", lambda e, PT=PT, n=n, d0=d0: e.tensor_tensor(out=PT[:, :n], in0=PT[:, :n],
                                                                         in1=E[:, 128 * d0:128 * d0 + n], op=ALU.mult),
                     r=[bPT, bE], w=[bPT])

                def mmO(e, chunk=chunk, PT=PT, c=c):
                    ins = None
                    for jj, j in enumerate(chunk):
                        ins = e.matmul(po[:, :ncol], PT[:, jj * 128:(jj + 1) * 128], V1[:, j, :ncol],
                                       start=(c == 0 and jj == 0), stop=(c == nchunks - 1 and jj == len(chunk) - 1))
                    return ins

                P.op("pe", mmO, r=[bPT, bV], w=[bpo])

        def load_v1(dst, bdst, vh):
            P.dma("sp", dst[:, :, 0:128], vv[vh].rearrange("(i p) d -> p i d", p=128), r=[B_vv[vh]], w=[bdst])
            P.op("dve", lambda e: e.memset(dst[:, :, 128:129], 1.0), w=[bdst])

        def ecols(name):
            toff, pstep, width, ecol, start = PLAN[name]
            return ecol, width

        for k in range(2):
            A.off = mark1
            QT3 = A.alloc(3 * T).rearrange("p (h t) -> p h t", h=3)
            b_QT3 = Buf("QT3")
            Ks = A.alloc(T)
            Kw = A.alloc(T)
            b_Ks = Buf("Ks")
            b_Kw = Buf("Kw")
            Vs = A.alloc(NT * 130).rearrange("p (i d) -> p i d", d=130)
            Vw = A.alloc(NT * 130).rearrange("p (i d) -> p i d", d=130)
            b_Vs = Buf("Vs")
            b_Vw = Buf("Vw")
            Ec = A.alloc(3 * 6144).rearrange("p (h n) -> p h n", h=3)
            Esl = A.alloc(3 * 4096).rearrange("p (h n) -> p h n", h=3)
            Ew = A.alloc(3 * 640).rearrange("p (h n) -> p h n", h=3)
            b_E = Buf("E")
            accs = [(A.alloc(128, F32), Buf(f"acc{g}")) for g in range(3)]
            impa = A.alloc(64, F32)
            b_imp = Buf("imp")
            wk = A.alloc(64, F32)
            m8 = A.alloc(16, F32)
            Mb = A.alloc(64)
            b_M = Buf("Mb")
            MT = A.alloc(128)
            b_MT = Buf("MT")
            mixA = Rot([(A.alloc(384), Buf(f"mixA{i}")) for i in range(2)])
            for g in range(3):
                h = 3 * k + g
                P.dma("sp", QT3[:, g, :], qt[h], r=[B_qt[h]], w=[b_QT3])
                ec, w_ = ecols(("cmp0", h))
                P.dma("sp", Ec[:, g, 0:4096], etab[:, ec:ec + 4096], r=[B_etab], w=[b_E])
                ec, w_ = ecols(("cmp1", h))
                P.dma("sp", Ec[:, g, 4096:6144], etab[:, ec:ec + 2048], r=[B_etab], w=[b_E])
                ec, w_ = ecols(("slc", h))
                P.dma("sp", Esl[:, g, :], etab[:, ec:ec + 4096], r=[B_etab], w=[b_E])
                ec, w_ = ecols(("win", h))
                P.dma("sp", Ew[:, g, :], etab[:, ec:ec + 640], r=[B_etab], w=[b_E])
            P.dma("sp", Ks, kt[KT_KSLC + k], r=[B_kt[KT_KSLC + k]], w=[b_Ks])
            P.dma("sp", Kw, kt[KT_KWIN + k], r=[B_kt[KT_KWIN + k]], w=[b_Kw])
            load_v1(Vs, b_Vs, V_SLC + k)
            load_v1(Vw, b_Vw, V_WIN + k)

            for i in range(NT):
                tq = slice(i * 128, (i + 1) * 128)
                for g in range(3):
                    h = 3 * k + g
                    acc, bacc = accs[g]
                    po, bpo = psO.next()
                    na = 2 if i >= 16 else 1
                    pS, bpS = psS.next()
                    PT, bPT = PTs.next()

                    def mmS(e, pS=pS, g=g, na=na, tq=tq, k=k):
                        ins = None
                        for a in range(na):
                            ins = e.matmul(pS[:, a * 128:(a + 1) * 128], kcT[:, k, a * 128:(a + 1) * 128],
                                           QT3[:, g, tq], start=True, stop=True)
                        return ins

                    P.op("pe", mmS, r=[b_kcT, b_QT3], w=[bpS])
                    n = 128 * na
                    P.op("act", lambda e, PT=PT, pS=pS, n=n: e.activation(out=PT[:, :n], in_=pS[:, :n], func=AF.Exp,
                                                                          scale=float(SCALE)), r=[bpS], w=[bPT])
                    P.op("dve", lambda e, PT=PT, g=g, i=i: e.tensor_tensor(
                        out=PT[:, 0:128], in0=PT[:, 0:128], in1=Ec[:, g, i * 128:(i + 1) * 128], op=ALU.mult),
                         r=[bPT, b_E], w=[bPT])
                    if na == 2:
                        P.op("dve", lambda e, PT=PT, g=g, i=i: e.tensor_tensor(
                            out=PT[:, 128:256], in0=PT[:, 128:256],
                            in1=Ec[:, g, 4096 + (i - 16) * 128:4096 + (i - 15) * 128], op=ALU.mult),
                             r=[bPT, b_E], w=[bPT])

                    def mmO(e, PT=PT, po=po, na=na, k=k):
                        ins = None
                        for a in range(na):
                            ins = e.matmul(po[:, 0:193], PT[:, a * 128:(a + 1) * 128], VC1[:, k, a, 0:193],
                                           start=(a == 0), stop=(a == na - 1))
                        return ins

                    P.op("pe", mmO, r=[bPT, b_VC1], w=[bpo])
                    c0 = 4 * g
                    P.op("dve", lambda e, po=po, c0=c0: e.tensor_scalar(out=sm[:, c0:c0 + 1], in0=po[:, 128:129],
                                                                        scalar1=1e-30, scalar2=None, op0=ALU.max),
                         r=[bpo], w=[b_sm])
                    P.op("dve", lambda e, c0=c0: e.reciprocal(out=sm[:, c0:c0 + 1], in_=sm[:, c0:c0 + 1]),
                         r=[b_sm], w=[b_sm])
                    P.op("dve", lambda e, c0=c0, g=g, i=i, k=k: e.tensor_tensor(
                        out=sm[:, c0 + 1:c0 + 2], in0=sm[:, c0:c0 + 1], in1=gsb[:, i, 9 * k + 3 * g:9 * k + 3 * g + 1], op=ALU.mult),
                         r=[b_sm, b_gsb], w=[b_sm])
                    P.op("act", lambda e, acc=acc, po=po, c0=c0: e.activation(out=acc, in_=po[:, 0:128], func=AF.Copy,
                                                                              scale=sm[:, c0 + 1:c0 + 2]),
                         r=[bpo, b_sm], w=[bacc])
                    if g == 0:
                        P.op("dve", lambda e, po=po, c0=c0: e.tensor_scalar(out=impa, in0=po[:, 129:193],
                                                                            scalar1=sm[:, c0:c0 + 1], scalar2=None,
                                                                            op0=ALU.mult),
                             r=[bpo, b_sm], w=[b_imp])
                    else:
                        P.op("dve", lambda e, po=po, c0=c0: e.scalar_tensor_tensor(
                            out=impa, in0=po[:, 129:193], scalar=sm[:, c0:c0 + 1], in1=impa, op0=ALU.mult,
                            op1=ALU.add), r=[bpo, b_sm, b_imp], w=[b_imp])
                if dbg and k == 0:
                    P.dma("pool", dbgA[0, i * 128:(i + 1) * 128, :], accs[0][0], r=[accs[0][1]], w=[B_dbgA])
                P.op("dve", lambda e, i=i: e.tensor_tensor(out=impa, in0=impa, in1=keepm[:, i * 64:(i + 1) * 64],
                                                           op=ALU.mult), r=[b_imp, b_km], w=[b_imp])
                P.op("dve", lambda e, i=i: e.tensor_tensor(out=impa, in0=impa, in1=addm[:, i * 64:(i + 1) * 64],
                                                           op=ALU.add), r=[b_imp, b_km], w=[b_imp])
                P.op("dve", lambda e: e.max(out=m8[:, 0:8], in_=impa), r=[b_imp], w=[b_sm])
                P.op("dve", lambda e: e.match_replace(out=wk, in_to_replace=m8[:, 0:8], in_values=impa,
                                                      imm_value=-2.0e30), r=[b_imp, b_sm], w=[b_sm])
                P.op("dve", lambda e: e.max(out=m8[:, 8:16], in_=wk), r=[b_sm], w=[b_sm])
                P.op("dve", lambda e: e.tensor_scalar(out=Mb, in0=impa, scalar1=m8[:, 15:16], scalar2=30000.0,
                                                      op0=ALU.is_lt, op1=ALU.mult), r=[b_imp, b_sm], w=[b_M])
                pp, bpp = pst[0]
                P.op("pe", lambda e, pp=pp: e.transpose(pp[0:64, 0:128], Mb, ident[:]), r=[b_M, B_const], w=[bpp])
                P.op("act", lambda e, pp=pp: e.copy(out=MT[0:64, :], in_=pp[0:64, 0:128]), r=[bpp], w=[b_MT])
                for g in range(3):
                    acc, bacc = accs[g]
                    for br in (1, 2):
                        po, bpo = psO.next()
                        if br == 1:
                            jl = list(range(i, -1, -1))
                            attn(QT3[:, g, tq], b_QT3, Ks, b_Ks, Vs, b_Vs, Esl[:, g, :], b_E, i, jl, po, bpo, 129,
                                 mask=(MT, b_MT))
                        else:
                            jl = list(range(i, max(i - 4, 0) - 1, -1))
                            attn(QT3[:, g, tq], b_QT3, Kw, b_Kw, Vw, b_Vw, Ew[:, g, :], b_E, i, jl, po, bpo, 129)
                        c0 = 4 * g + 2
                        P.op("dve", lambda e, po=po, c0=c0: e.reciprocal(out=sm[:, c0:c0 + 1], in_=po[:, 128:129]),
                             r=[bpo], w=[b_sm])
                        P.op("dve", lambda e, c0=c0, g=g, i=i, br=br, k=k: e.tensor_tensor(
                            out=sm[:, c0:c0 + 1], in0=sm[:, c0:c0 + 1], in1=gsb[:, i, 9 * k + 3 * g + br:9 * k + 3 * g + br + 1],
                            op=ALU.mult), r=[b_sm, b_gsb], w=[b_sm])
                        P.op("dve", lambda e, po=po, c0=c0, acc=acc: e.scalar_tensor_tensor(
                            out=acc, in0=po[:, 0:128], scalar=sm[:, c0:c0 + 1], in1=acc, op0=ALU.mult, op1=ALU.add),
                             r=[bpo, b_sm, bacc], w=[bacc])
                        if dbg and k == 0 and g == 0:
                            P.dma("pool", dbgA[br, i * 128:(i + 1) * 128, :], acc, r=[bacc], w=[B_dbgA])
                mx, bmx = mixA.next()
                for g in range(3):
                    acc, bacc = accs[g]
                    P.op("act", lambda e, mx=mx, acc=acc, g=g: e.copy(out=mx[:, g * 128:(g + 1) * 128], in_=acc),
                         r=[bacc], w=[bmx])
                P.dma("pool", mix[i * 128:(i + 1) * 128, 384 * k:384 * (k + 1)], mx, r=[bmx], w=[B_mix])
            P.barrier()

        for kvh in range(2):
            A.off = mark1
            QT2 = A.alloc(2 * T).rearrange("p (h t) -> p h t", h=2)
            b_QT2 = Buf("QT2")
            Kb = A.alloc(T)
            b_Kb = Buf("Kb")
            Vb = A.alloc(NT * 130).rearrange("p (i d) -> p i d", d=130)
            b_Vb = Buf("Vb")
            Eb2 = A.alloc(2 * 256).rearrange("p (h n) -> p h n", h=2)
            b_Eb2 = Buf("Eb2")
            mixB = Rot([(A.alloc(256), Buf(f"mixB{i}")) for i in range(2)])
            for hh in range(2):
                h = 6 + 2 * kvh + hh
                P.dma("sp", QT2[:, hh, :], qt[h], r=[B_qt[h]], w=[b_QT2])
                ec, w_ = ecols(("b", 2 * kvh + hh))
                P.dma("sp", Eb2[:, hh, :], etab[:, ec:ec + 256], r=[B_etab], w=[b_Eb2])
            P.dma("sp", Kb, kt[KT_KB + kvh], r=[B_kt[KT_KB + kvh]], w=[b_Kb])
            load_v1(Vb, b_Vb, V_B + kvh)
            for i in range(NT):
                tq = slice(i * 128, (i + 1) * 128)
                mx, bmx = mixB.next()
                for hh in range(2):
                    po, bpo = psO.next()
                    jl = list(range(i, max(i - 1, 0) - 1, -1))
                    attn(QT2[:, hh, tq], b_QT2, Kb, b_Kb, Vb, b_Vb, Eb2[:, hh, :], b_Eb2, i, jl, po, bpo, 129)
                    c0 = 16 + hh
                    sc = 40 + 2 * kvh + hh
                    P.op("dve", lambda e, po=po, c0=c0, sc=sc: e.tensor_tensor(
                        out=sm[:, c0:c0 + 1], in0=po[:, 128:129], in1=prm[:, sc:sc + 1], op=ALU.add),
                         r=[bpo, B_prm], w=[b_sm])
                    P.op("dve", lambda e, c0=c0: e.reciprocal(out=sm[:, c0:c0 + 1], in_=sm[:, c0:c0 + 1]),
                         r=[b_sm], w=[b_sm])
                    P.op("act", lambda e, mx=mx, po=po, c0=c0, hh=hh: e.activation(
                        out=mx[:, hh * 128:(hh + 1) * 128], in_=po[:, 0:128], func=AF.Copy, scale=sm[:, c0:c0 + 1]),
                         r=[bpo, b_sm], w=[bmx])
                c1 = (6 + 2 * kvh) * 128
                P.dma("pool", mix[i * 128:(i + 1) * 128, c1:c1 + 256], mx, r=[bmx], w=[B_mix])
            P.barrier()

        for s in range(2):
            A.off = mark1
            QTc = A.alloc(3 * T).rearrange("p (h t) -> p h t", h=3)
            b_QTc = Buf("QTc")
            KTc = A.alloc(3 * T).rearrange("p (h t) -> p h t", h=3)
            b_KTc = Buf("KTc")
            Vc = [A.alloc(NT * 130).rearrange("p (i d) -> p i d", d=130) for g in range(3)]
            b_Vc = [Buf(f"Vc{g}") for g in range(3)]
            wtot = sum(128 * (C_NPREV[g] + 1) for g in range(3))
            Ecc = A.alloc(wtot)
            b_Ecc = Buf("Ecc")
            eoff = []
            o = 0
            for g in range(3):
                eoff.append(o)
                o += 128 * (C_NPREV[g] + 1)
            mixC = Rot([(A.alloc(384), Buf(f"mixC{i}")) for i in range(2)])
            for g in range(3):
                h = 10 + 2 * g + s
                P.dma("sp", QTc[:, g, :], qt[h], r=[B_qt[h]], w=[b_QTc])
                P.dma("sp", KTc[:, g, :], kt[KT_KC + g], r=[B_kt[KT_KC + g]], w=[b_KTc])
                load_v1(Vc[g], b_Vc[g], V_C + g)
                ec, w_ = ecols(("c", g, s))
                P.dma("sp", Ecc[:, eoff[g]:eoff[g] + w_], etab[:, ec:ec + w_], r=[B_etab], w=[b_Ecc])
            for i in range(NT):
                tq = slice(i * 128, (i + 1) * 128)
                pos = []
                for g in range(3):
                    po, bpo = psO.next()
                    pos.append((po, bpo))
                    jl = list(range(i, max(i - C_NPREV[g], 0) - 1, -1))
                    attn(QTc[:, g, tq], b_QTc, KTc[:, g, :], b_KTc, Vc[g], b_Vc[g],
                         Ecc[:, eoff[g]:eoff[g] + 128 * (C_NPREV[g] + 1)], b_Ecc, i, jl, po, bpo, 129)
                    P.op("dve", lambda e, po=po, g=g: e.tensor_copy(out=sm[:, 20 + g:21 + g], in_=po[:, 128:129]),
                         r=[bpo], w=[b_sm])
                P.op("dve", lambda e: e.tensor_tensor(out=sm[:, 23:24], in0=sm[:, 20:21], in1=sm[:, 21:22], op=ALU.add),
                     r=[b_sm], w=[b_sm])
                P.op("dve", lambda e: e.tensor_tensor(out=sm[:, 23:24], in0=sm[:, 23:24], in1=sm[:, 22:23], op=ALU.add),
                     r=[b_sm], w=[b_sm])
                P.op("dve", lambda e: e.reciprocal(out=sm[:, 24:25], in_=sm[:, 23:24]), r=[b_sm], w=[b_sm])
                mx, bmx = mixC.next()
                for g in range(3):
                    po, bpo = pos[g]
                    P.op("act", lambda e, mx=mx, po=po, g=g: e.activation(
                        out=mx[:, g * 128:(g + 1) * 128], in_=po[:, 0:128], func=AF.Copy, scale=sm[:, 24:25]),
                         r=[bpo, b_sm], w=[bmx])
                dst = mix[i * 128:(i + 1) * 128, 1280:2048].rearrange("p (g x) -> p g x", g=3)[:, :,
                                                                                                  s * 128:(s + 1) * 128]
                P.dma("pool", dst, mx.rearrange("p (g d) -> p g d", g=3), r=[bmx], w=[B_mix])
            P.barrier()

    def phase3(l, x_src, B_xsrc, x_dst, B_xdst):
        A.off = 0
        xts = [(A.alloc(D, F32), Buf(f"xt{j}")) for j in range(4)]
        xns = [(A.alloc(D), Buf(f"xn{j}")) for j in range(4)]
        junk = (A.alloc(D), Buf("junk"))
        ssb = (A.alloc(8, F32), Buf("ss"))
        hT = A.alloc(16 * 512).rearrange("p (c t) -> p c t", c=16)
        bhT = Buf("hT")
        actT = A.alloc(44 * 512).rearrange("p (c t) -> p c t", c=44)
        b_act = [Buf(f"act{c}") for c in range(44)]
        wbufs = Rot([(A.alloc(16 * 512).rearrange("p (c n) -> p c n", c=16), Buf(f"wb{i}")) for i in range(4)])
        sgs = Rot([(A.alloc(512, F32), Buf(f"sg{i}")) for i in range(2)])
        psG = Rot(ps[0:2])
        psU = Rot(ps[2:4])
        psW = Rot(ps[4:6])

        stages = []
        for g in range(NG):
            t0 = g * GT

            def ld_x(t0=t0):
                for j in range(4):
                    P.dma("sp", xns[j][0], mix[t0 + 128 * j:t0 + 128 * (j + 1), :], r=[B_mix], w=[xns[j][1]])

            def cp_x(_h, t0=t0):
                for j in range(4):
                    P.dma("sp", xts[j][0], x_src[t0 + 128 * j:t0 + 128 * (j + 1), :], r=[B_xsrc], w=[xts[j][1]])
                transpose_group([x[0] for x in xns], [x[1] for x in xns], hT, bhT, None)

            stages.append((ld_x, cp_x))
            for cb in range(4):
                def ld_wo(cb=cb):
                    wb, bwb = wbufs.next()
                    P.dma("sp", wb, wb_out[l, :, cb * 512:(cb + 1) * 512].rearrange("(c p) n -> p c n", p=128),
                          r=[B_wb[("out", l)]], w=[bwb])
                    return wb, bwb

                def cp_wo(hnd, cb=cb):
                    wb, bwb = hnd
                    for j in range(4):
                        pw, bpw = psW.next()

                        def mmw(e, pw=pw, wb=wb, j=j):
                            ins = None
                            for mc in range(16):
                                ins = e.matmul(pw, hT[:, mc, j * 128:(j + 1) * 128], wb[:, mc, :], start=(mc == 0),
                                               stop=(mc == 15))
                            return ins

                        P.op("pe", mmw, r=[bwb, bhT], w=[bpw])
                        xt, bxt = xts[j]
                        P.op("dve", lambda e, pw=pw, xt=xt, cb=cb: e.tensor_tensor(
                            out=xt[:, cb * 512:(cb + 1) * 512], in0=pw, in1=xt[:, cb * 512:(cb + 1) * 512], op=ALU.add),
                             r=[bpw, bxt], w=[bxt])

                stages.append((ld_wo, cp_wo))

            def ld_none():
                return None

            def cp_norm(_h):
                norm_transpose(xts, xns, junk, ssb, hT, bhT, 16)

            stages.append((ld_none, cp_norm))
            for fb in range(11):
                def ld_gu(fb=fb):
                    wg, bwg = wbufs.next()
                    P.dma("sp", wg, wb_g[l, :, fb * 512:(fb + 1) * 512].rearrange("(c p) n -> p c n", p=128),
                          r=[B_wb[("g", l)]], w=[bwg])
                    wu, bwu = wbufs.next()
                    P.dma("sp", wu, wb_u[l, :, fb * 512:(fb + 1) * 512].rearrange("(c p) n -> p c n", p=128),
                          r=[B_wb[("u", l)]], w=[bwu])
                    return wg, bwg, wu, bwu

                def cp_gu(hnd, fb=fb):
                    wg, bwg, wu, bwu = hnd
                    for q in range(4):
                        fc = 4 * fb + q
                        pg, bpg = psG.next()
                        pu, bpu = psU.next()

                        def mmg(e, pg=pg, wg=wg, q=q):
                            ins = None
                            for kc in range(16):
                                ins = e.matmul(pg, wg[:, kc, q * 128:(q + 1) * 128], hT[:, kc, :], start=(kc == 0),
                                               stop=(kc == 15))
                            return ins

                        def mmu(e, pu=pu, wu=wu, q=q):
                            ins = None
                            for kc in range(16):
                                ins = e.matmul(pu, wu[:, kc, q * 128:(q + 1) * 128], hT[:, kc, :], start=(kc == 0),
                                               stop=(kc == 15))
                            return ins

                        P.op("pe", mmg, r=[bwg, bhT], w=[bpg])
                        P.op("pe", mmu, r=[bwu, bhT], w=[bpu])
                        sg_, bsg = sgs.next()
                        P.op("act", lambda e, sg_=sg_, pg=pg: e.activation(out=sg_, in_=pg, func=AF.Silu), r=[bpg],
                             w=[bsg])
                        P.op("dve", lambda e, sg_=sg_, pu=pu, fc=fc: e.tensor_tensor(out=actT[:, fc, :], in0=pu,
                                                                                     in1=sg_, op=ALU.mult),
                             r=[bsg, bpu], w=[b_act[fc]])

                stages.append((ld_gu, cp_gu))
            for cb in range(4):
                for fg in range(3):
                    fc0 = 16 * fg
                    nfc = 16 if fg < 2 else 12

                    def ld_d(cb=cb, fc0=fc0, nfc=nfc):
                        wd, bwd = wbufs.next()
                        P.dma("sp", wd[:, :nfc, :],
                              wb_d[l, fc0 * 128:(fc0 + nfc) * 128, cb * 512:(cb + 1) * 512].rearrange(
                                  "(c p) n -> p c n", p=128), r=[B_wb[("d", l)]], w=[bwd])
                        return wd, bwd

                    def cp_d(hnd, cb=cb, fg=fg, fc0=fc0, nfc=nfc, t0=t0):
                        wd, bwd = hnd
                        for j in range(4):
                            pd, bpd = ps[j]

                            def mmd(e, pd=pd, wd=wd, j=j):
                                ins = None
                                for f in range(nfc):
                                    ins = e.matmul(pd, actT[:, fc0 + f, j * 128:(j + 1) * 128], wd[:, f, :],
                                                   start=(fg == 0 and f == 0), stop=(fg == 2 and f == nfc - 1))
                                return ins

                            P.op("pe", mmd, r=[bwd] + b_act[fc0:fc0 + nfc], w=[bpd])
                            if fg == 2:
                                xt, bxt = xts[j]
                                P.op("dve", lambda e, pd=pd, xt=xt: e.tensor_tensor(
                                    out=xt[:, cb * 512:(cb + 1) * 512], in0=pd, in1=xt[:, cb * 512:(cb + 1) * 512],
                                    op=ALU.add), r=[bpd, bxt], w=[bxt])
                                if cb == 3:
                                    P.dma("pool", x_dst[t0 + 128 * j:t0 + 128 * (j + 1), :], xt, r=[bxt], w=[B_xdst])

                    stages.append((ld_d, cp_d))
        run_pipeline(stages)
        P.barrier()

    B_x = Buf("x")
    if "p0" in phases:
        phase0()
    for l in range(n_layers):
        load_params(l)
        src, bsrc = (x_in, B_x) if l == 0 else (xres, B_xres)
        last = (l == n_layers - 1)
        dst, bdst = (y_out, B_y) if last else (xres, B_xres)
        if "p1" in phases:
            phase1(l, src, bsrc)
        if "p2" in phases:
            phase2(l)
        if "p3" in phases:
            phase3(l, src, bsrc, dst, bdst)
    P.barrier()
    P.emit()
    es.close()
    return nc


_CACHE = {}


def _host_inputs(inputs):
    consts = _build_consts()
    tabs = _build_tabs(np.asarray(inputs["rel_bias"], dtype=np.float32))
    shared = {
        "w_in": np.ascontiguousarray(inputs["w_in"], dtype=np.float32),
        "w_out": np.ascontiguousarray(inputs["w_out"], dtype=np.float32),
        "w_gate": np.ascontiguousarray(inputs["w_gate"], dtype=np.float32),
        "w_up": np.ascontiguousarray(inputs["w_up"], dtype=np.float32),
        "w_down": np.ascontiguousarray(inputs["w_down"], dtype=np.float32),
        "norm_attn": np.ascontiguousarray(inputs["norm_attn"], dtype=np.float32),
        "norm_ffn": np.ascontiguousarray(inputs["norm_ffn"], dtype=np.float32),
        "qk_gain": np.ascontiguousarray(inputs["qk_gain"], dtype=np.float32),
        "cmp_pe": np.ascontiguousarray(inputs["cmp_pe"], dtype=np.float32),
        "cmp_w1": np.ascontiguousarray(inputs["cmp_w1"], dtype=np.float32),
        "cmp_w2": np.ascontiguousarray(inputs["cmp_w2"], dtype=np.float32),
        "sinks": np.ascontiguousarray(inputs["sinks"], dtype=np.float32),
        "tabs": tabs,
        "c_ident": consts["ident"], "c_jrev": consts["jrev"], "c_ones": consts["ones"], "c_expn": consts["expn"],
        "c_ovl": consts["ovl"], "c_keepm": consts["keepm"], "c_addm": consts["addm"],
    }
    return shared


def kernel(x, norm_attn, w_in, qk_gain, cmp_pe, cmp_w1, cmp_w2, sinks, rel_bias, w_out, norm_ffn, w_gate, w_up,
           w_down):
    inputs = dict(x=x, norm_attn=norm_attn, w_in=w_in, qk_gain=qk_gain, cmp_pe=cmp_pe, cmp_w1=cmp_w1, cmp_w2=cmp_w2,
                  sinks=sinks, rel_bias=rel_bias, w_out=w_out, norm_ffn=norm_ffn, w_gate=w_gate, w_up=w_up,
                  w_down=w_down)
    shared = _host_inputs(inputs)
    x = np.asarray(x, dtype=np.float32)
    nc = build_program()
    in_maps = []
    for c in range(8):
        m = dict(shared)
        m["x"] = np.ascontiguousarray(x[c // 2])
        in_maps.append(m)
    res = run_bass_kernel_spmd(nc, in_maps, core_ids=list(range(8)))
    out = np.stack([np.asarray(res.results[2 * b]["y"], dtype=np.float32) for b in range(4)], axis=0)
    return out
```

```python
import math
from contextlib import ExitStack

import numpy as np
import ml_dtypes
import concourse.bass as bass
import concourse.mybir as mybir
from concourse.bass_utils import run_bass_kernel_spmd

F32 = mybir.dt.float32
BF16 = mybir.dt.bfloat16
AF = mybir.ActivationFunctionType
ALU = mybir.AluOpType

T = 4096
D = 2048
NT = T // 128
GT = 512
NG = T // GT
DFF = 5632
NIN = 4882
HD = 128
SCALE = HD ** -0.5
EPS = 1e-6
NEG = -30000.0

C_QA, C_KCA, C_VCA, C_KSA, C_VSA, C_KWA, C_VWA, C_GA = 0, 768, 1024, 1280, 1536, 1792, 2048, 2304
C_QB, C_KB, C_VB, C_QC, C_KC, C_VC = 2322, 2834, 3090, 3346, 4114, 4498
KT_KCMP, KT_VCMP, KT_KSLC, KT_KWIN, KT_KB, KT_KC = 0, 2, 4, 6, 8, 10
V_SLC, V_WIN, V_B, V_C = 0, 2, 4, 6
DIL = ((128, 1), (512, 4), (2048, 16))
C_NPREV = (1, 4, 16)


def _rel_bucket(dist):
    dist = np.maximum(dist, 0)
    far = np.maximum(dist, 16).astype(np.float32)
    lb = 16 + (np.log(far / np.float32(16)) / np.float32(math.log(2048 / 16)) * np.float32(16)).astype(np.int32)
    lb = np.minimum(lb, 31)
    return np.where(dist < 16, dist, lb).astype(np.int64)


def _table_plan():
    plan = {}
    tab_off = 0
    ecol = 0
    specs = []
    for h in range(6):
        specs.append((("slc", h), 127 + 4096, 1, 4096, 0))
    for h in range(6):
        specs.append((("win", h), 127 + 640, 1, 640, 0))
    for h in range(4):
        specs.append((("b", h), 127 + 256, 1, 256, 0))
    for g in range(3):
        for s in range(2):
            w = 128 * (C_NPREV[g] + 1)
            specs.append((("c", g, s), 127 + w, 1, w, 0))
    for h in range(6):
        specs.append((("cmp0", h), 8176, 16, 4096, 2048))
        specs.append((("cmp1", h), -1, 16, 2048, 2048))
    for (name, tlen, pstep, width, start) in specs:
        if tlen < 0:
            toff = plan[("cmp0", name[1])][0]
        else:
            toff = tab_off
            tab_off += tlen
        plan[name] = (toff, pstep, width, ecol, start)
        ecol += width
    return plan, tab_off, ecol


PLAN, TAB_LEN, ETOT = _table_plan()


def _build_tabs(rel_bias):
    tabs = np.full((TAB_LEN,), NEG, dtype=np.float32)

    def fill(name, col, dvals, valid):
        toff = PLAN[name][0]
        idx = _rel_bucket(np.where(valid, dvals, 0))
        vals = rel_bias[idx, col]
        seg = tabs[toff:toff + len(dvals)]
        seg[valid] = vals[valid]

    for h in range(6):
        d = np.arange(127 + 4096) - 127
        fill(("slc", h), h, d, d >= 0)
        d = np.arange(127 + 640) - 127
        fill(("win", h), h, d, (d >= 0) & (d <= 511))
        d = np.arange(8176) - 4111
        fill(("cmp0", h), h, d, d >= 0)
    for h in range(4):
        d = np.arange(127 + 256) - 127
        fill(("b", h), 6 + h, d, (d >= 0) & (d <= 127))
    for g in range(3):
        win, dil = DIL[g]
        for s in range(2):
            w = 128 * (C_NPREV[g] + 1)
            d = np.arange(127 + w) - 127
            fill(("c", g, s), 10 + 2 * g + s, d, (d >= 0) & (d <= win) & (d % dil == 0))
    return tabs


def _build_consts():
    bf = ml_dtypes.bfloat16
    ident = np.eye(128, dtype=np.float32).astype(bf)
    jrev = np.eye(128, dtype=np.float32)[::-1].copy().astype(bf)
    ones = np.ones((128, 128), dtype=np.float32).astype(bf)
    expn = np.zeros((64, 4096), dtype=np.float32)
    for j in range(64):
        expn[j, 64 * j:64 * j + 64] = -1.0
    expn = expn.astype(bf)
    c0 = np.arange(255)[:, None] * 16
    s0 = np.arange(64)[None, :] * 64
    ov = np.clip(np.minimum(c0 + 32, s0 + 64) - np.maximum(c0, s0), 0, None) / 32.0
    ovl = np.zeros((256, 64), dtype=np.float32)
    ovl[:255] = ov
    ovl = ovl.reshape(2, 128, 64).transpose(1, 0, 2).copy().astype(bf)
    tpos = np.arange(T)
    blk = np.arange(64)[None, :]
    cur = (tpos // 64)[:, None]
    forced = (blk == 0) | (blk == cur) | (blk == cur - 1)
    causal = blk * 64 <= tpos[:, None]
    keep = (causal & ~forced).astype(np.float32)
    add = np.where(causal, np.where(forced, 1.0e4, 0.0), -1.0e30).astype(np.float32)
    keep = keep.reshape(NT, 128, 64).transpose(1, 0, 2).reshape(128, NT * 64).copy()
    add = add.reshape(NT, 128, 64).transpose(1, 0, 2).reshape(128, NT * 64).copy()
    return dict(ident=ident, jrev=jrev, ones=ones, expn=expn, ovl=ovl, keepm=keep, addm=add)


class Buf:
    __slots__ = ("name", "w", "r", "excl")

    def __init__(self, name, excl=False):
        self.name = name
        self.w = {}
        self.r = {}
        self.excl = excl


class Prog:
    ENG = ("pe", "act", "dve", "pool", "sp")

    def __init__(self, nc, es):
        self.nc = nc
        self.eng = {"pe": nc.tensor, "act": nc.scalar, "dve": nc.vector, "pool": nc.gpsimd, "sp": nc.sync}
        self.streams = {k: [] for k in self.ENG}
        self.cnt = {}
        self.waited = {}
        self.sem = {}
        for k in list(self.ENG):
            self.sem[k] = es.enter_context(nc.semaphore("s_" + k))
            self.cnt[k] = 0
        self.NS = 24
        self.dnext = {"sp": 0, "pool": 0}
        for q in ("sp", "pool"):
            for i in range(self.NS):
                self.sem[(q, i)] = es.enter_context(nc.semaphore(f"d_{q}{i}"))
                self.cnt[(q, i)] = 0

    def _collect(self, me, reads, writes):
        need = {}
        for b in reads:
            for t, v in b.w.items():
                if v > need.get(t, 0):
                    need[t] = v
        for b in writes:
            for t, v in b.w.items():
                if v > need.get(t, 0):
                    need[t] = v
            for t, v in b.r.items():
                if v > need.get(t, 0):
                    need[t] = v
        out = []
        for t, v in need.items():
            if me == "pe" and t == "pe":
                continue
            if self.waited.get((me, t), 0) >= v:
                continue
            self.waited[(me, t)] = v
            out.append((t, v))
        return out

    def _mark(self, tgt, idx, reads, writes):
        for b in reads:
            if b.r.get(tgt, 0) < idx:
                b.r[tgt] = idx
        for b in writes:
            b.w = {tgt: idx}
            b.r = {}

    def op(self, e, fn, r=(), w=()):
        if any(b.excl for b in r):
            w = list(w) + [b for b in r if b.excl]
            r = [b for b in r if not b.excl]
        waits = self._collect(e, r, w)
        self.cnt[e] += 1
        idx = self.cnt[e]
        sem = self.sem

        def run(eng, waits=waits, fn=fn, s=sem[e]):
            for t, v in waits:
                eng.wait_ge(sem[t], v)
            fn(eng).then_inc(s, 1)

        self.streams[e].append(run)
        self._mark(e, idx, r, w)

    def dma(self, q, out, in_, r=(), w=(), merge=(), **kw):
        tgt = (q, self.dnext[q] % self.NS)
        self.dnext[q] += 1
        waits = self._collect(q, r, w)
        prev = self.cnt[tgt]
        if prev > 0 and self.waited.get((q, tgt), 0) < prev:
            self.waited[(q, tgt)] = prev
            waits.append((tgt, prev))
        self.cnt[tgt] += 16
        idx = self.cnt[tgt]
        sem = self.sem

        def run(eng, waits=waits, s=sem[tgt]):
            for t, v in waits:
                eng.wait_ge(sem[t], v)
            eng.dma_start(out=out, in_=in_, **kw).then_inc(s, 16)

        self.streams[q].append(run)
        self._mark(tgt, idx, r, w)
        for b in merge:
            b.w[tgt] = idx

    def barrier(self):
        for e in self.ENG:
            waits = []
            for t, v in self.cnt.items():
                if v == 0 or t == e:
                    continue
                if self.waited.get((e, t), 0) >= v:
                    continue
                self.waited[(e, t)] = v
                waits.append((t, v))
            if waits:
                sem = self.sem

                def run(eng, waits=waits):
                    for t, v in waits:
                        eng.wait_ge(sem[t], v)

                self.streams[e].append(run)

    def emit(self):
        with self.nc.Block() as block:
            @block.tensor
            def _(e):
                for f in self.streams["pe"]:
                    f(e)

            @block.scalar
            def _(e):
                for f in self.streams["act"]:
                    f(e)

            @block.vector
            def _(e):
                for f in self.streams["dve"]:
                    f(e)

            @block.gpsimd
            def _(e):
                for f in self.streams["pool"]:
                    f(e)

            @block.sync
            def _(e):
                for f in self.streams["sp"]:
                    f(e)


class Arena:
    def __init__(self, ap, nelem):
        self.ap = ap
        self.n = nelem
        self.off = 0

    def alloc(self, nfree, dtype=BF16):
        nb = nfree * (4 if dtype == F32 else 2)
        nb = (nb + 63) // 64 * 64
        ne = nb // 2
        assert self.off + ne <= self.n, ("arena overflow", self.off, ne, self.n)
        a = self.ap[:, self.off:self.off + ne]
        self.off += ne
        if dtype == F32:
            return a.bitcast(F32)[:, :nfree]
        return a[:, :nfree]


class Rot:
    def __init__(self, items):
        self.items = items
        self.i = 0

    def next(self):
        it = self.items[self.i % len(self.items)]
        self.i += 1
        return it


def run_pipeline(stages):
    handles = [None] * len(stages)
    if stages:
        handles[0] = stages[0][0]()
    for n in range(len(stages)):
        if n + 1 < len(stages):
            handles[n + 1] = stages[n + 1][0]()
        stages[n][1](handles[n])


def build_program(n_layers=2, dbg=False, phases=("p0", "p1", "p2", "p3")):
    nc = bass.Bass("TRN2", target_bir_lowering=False)
    es = ExitStack()
    P = Prog(nc, es)

    def din(name, shape, dt=F32):
        return nc.dram_tensor(name, list(shape), dt, kind="ExternalInput")

    x_in = din("x", [T, D]).ap()
    w_in_f = din("w_in", [2, D, NIN]).ap()
    w_out_f = din("w_out", [2, D, D]).ap()
    w_g_f = din("w_gate", [2, D, DFF]).ap()
    w_u_f = din("w_up", [2, D, DFF]).ap()
    w_d_f = din("w_down", [2, DFF, D]).ap()
    norm_attn = din("norm_attn", [2, D]).ap()
    norm_ffn = din("norm_ffn", [2, D]).ap()
    qk_gain = din("qk_gain", [2, 8, HD]).ap()
    cmp_pe = din("cmp_pe", [2, 2, 32, HD]).ap()
    cmp_w1 = din("cmp_w1", [2, 2, 4096, HD]).ap()
    cmp_w2 = din("cmp_w2", [2, 2, HD, HD]).ap()
    sinks = din("sinks", [2, 4]).ap()
    tabs_t = din("tabs", [TAB_LEN])
    c_ident = din("c_ident", [128, 128], BF16).ap()
    c_jrev = din("c_jrev", [128, 128], BF16).ap()
    c_ones = din("c_ones", [128, 128], BF16).ap()
    c_expn = din("c_expn", [64, 4096], BF16).ap()
    c_ovl = din("c_ovl", [128, 2, 64], BF16).ap()
    c_keep = din("c_keepm", [128, NT * 64]).ap()
    c_add = din("c_addm", [128, NT * 64]).ap()

    okind = "ExternalOutput" if dbg else "Internal"
    y_out = nc.dram_tensor("y", [T, D], F32, kind="ExternalOutput").ap()
    xres = nc.dram_tensor("xres", [T, D], F32, kind="Internal").ap()
    wb_in = nc.dram_tensor("wb_in", [2, D, NIN], BF16, kind="Internal").ap()
    wb_out = nc.dram_tensor("wb_out", [2, D, D], BF16, kind="Internal").ap()
    wb_g = nc.dram_tensor("wb_g", [2, D, DFF], BF16, kind="Internal").ap()
    wb_u = nc.dram_tensor("wb_u", [2, D, DFF], BF16, kind="Internal").ap()
    wb_d = nc.dram_tensor("wb_d", [2, DFF, D], BF16, kind="Internal").ap()
    etab = nc.dram_tensor("etab", [128, ETOT], BF16, kind="Internal").ap()
    qt = nc.dram_tensor("qt", [16, 128, T], BF16, kind=okind).ap()
    kt = nc.dram_tensor("kt", [13, 128, T], BF16, kind=okind).ap()
    vv = nc.dram_tensor("vv", [9, T, 128], BF16, kind=okind).ap()
    gates = nc.dram_tensor("gates", [T, 18], F32, kind=okind).ap()
    mix = nc.dram_tensor("mix", [T, D], BF16, kind=okind).ap()
    dbgA = nc.dram_tensor("dbgA", [3, T, 128], F32, kind=okind).ap() if dbg else None
    B_dbgA = Buf("dbgA")

    B_wb = {(n, l): Buf(f"{n}{l}") for n in ("in", "out", "g", "u", "d") for l in range(2)}
    B_etab = Buf("etab")
    B_qt = [Buf(f"qt{h}") for h in range(16)]
    B_kt = [Buf(f"kt{h}") for h in range(13)]
    B_vv = [Buf(f"vv{h}") for h in range(9)]
    B_gates = Buf("gates")
    B_mix = Buf("mix")
    B_xres = Buf("xres")
    B_y = Buf("y")

    ARENA_N = 100352
    arena_t = es.enter_context(nc.sbuf_tensor("arena", [128, ARENA_N], BF16))
    ident = es.enter_context(nc.sbuf_tensor("ident", [128, 128], BF16))
    jrev = es.enter_context(nc.sbuf_tensor("jrev", [128, 128], BF16))
    ones = es.enter_context(nc.sbuf_tensor("ones", [128, 128], BF16))
    prm = es.enter_context(nc.sbuf_tensor("prm", [128, 64], F32))
    B_const = Buf("const")
    B_prm = Buf("prm")
    A = Arena(arena_t[:], ARENA_N)

    ps_t = [es.enter_context(nc.psum_tensor(f"ps{i}", [128, 512], F32)) for i in range(6)]
    pst_t = [es.enter_context(nc.psum_tensor(f"pst{i}", [128, 1024], BF16)) for i in range(2)]
    ps = [(t[:], Buf(f"ps{i}", excl=True)) for i, t in enumerate(ps_t)]
    pst = [(pst_t[i][:, 0:512], Buf(f"pst{i}", excl=True)) for i in range(2)]

    P.dma("sp", ident[:], c_ident, w=[B_const])
    P.dma("sp", jrev[:], c_jrev, w=[B_const])
    P.dma("sp", ones[:], c_ones, w=[B_const])

    cast_thr = [Buf(f"castthr{i}") for i in range(2)]
    cast_n = [0]

    def cast_weights(l):
        for (nm, src, dst, rows) in (("in", w_in_f, wb_in, D), ("out", w_out_f, wb_out, D), ("g", w_g_f, wb_g, D),
                                     ("u", w_u_f, wb_u, D), ("d", w_d_f, wb_d, DFF)):
            for r0 in range(0, rows, 256):
                cast_n[0] += 1
                P.dma("pool", dst[l, r0:r0 + 256, :], src[l, r0:r0 + 256, :], w=[cast_thr[cast_n[0] % 2]],
                      merge=[B_wb[(nm, l)]])

    cast_weights(0)

    def phase0():
        A.off = 0
        CH = 2048
        Hs = [(A.alloc(CH, F32), Buf(f"H{i}")) for i in range(2)]
        Hbs = [(A.alloc(CH), Buf(f"Hb{i}")) for i in range(2)]
        Ebs = [(A.alloc(CH), Buf(f"Eb{i}")) for i in range(2)]
        psr = Rot(ps[0:4])
        k = 0
        for name, (toff, pstep, width, ecol, start) in PLAN.items():
            for c0 in range(0, width, CH):
                w = min(CH, width - c0)
                H, bH = Hs[k % 2]
                Hb, bHb = Hbs[k % 2]
                Eb, bEb = Ebs[k % 2]
                k += 1
                src = bass.AP(tabs_t, toff + start + c0, [[pstep, 128], [1, w]])
                P.dma("sp", H[:, :w], src, w=[bH])
                P.op("act", lambda e, H=H, Hb=Hb, w=w: e.activation(out=Hb[:, :w], in_=H[:, :w], func=AF.Exp),
                     r=[bH], w=[bHb])
                for cc in range(0, w, 512):
                    n = min(512, w - cc)
                    pp, bpp = psr.next()
                    P.op("pe", lambda e, pp=pp, Hb=Hb, cc=cc, n=n: e.matmul(pp[:, :n], jrev[:], Hb[:, cc:cc + n],
                                                                            start=True, stop=True),
                         r=[bHb, B_const], w=[bpp])
                    P.op("dve", lambda e, pp=pp, Eb=Eb, cc=cc, n=n: e.tensor_copy(out=Eb[:, cc:cc + n], in_=pp[:, :n]),
                         r=[bpp], w=[bEb])
                P.dma("sp", etab[:, ecol + c0:ecol + c0 + w], Eb[:, :w], r=[bEb], w=[B_etab])
        P.barrier()

    def load_params(l):
        P.dma("sp", prm[:, 0:16], norm_attn[l].rearrange("(c p) -> p c", p=128), w=[B_prm],
              allow_slow_non_contiguous=True)
        P.dma("sp", prm[:, 16:32], norm_ffn[l].rearrange("(c p) -> p c", p=128), w=[B_prm],
              allow_slow_non_contiguous=True)
        P.dma("sp", prm[:, 32:40], qk_gain[l].rearrange("n d -> d n"), w=[B_prm], allow_slow_non_contiguous=True)
        P.dma("sp", prm[:, 40:44], sinks[l].partition_broadcast(128), w=[B_prm], allow_slow_non_contiguous=True)
        P.op("dve", lambda e: e.tensor_scalar_mul(out=prm[:, 0:32], in0=prm[:, 0:32], scalar1=float(math.sqrt(D))),
             r=[B_prm], w=[B_prm])
        P.op("dve", lambda e: e.tensor_scalar_mul(out=prm[:, 32:40], in0=prm[:, 32:40], scalar1=float(math.sqrt(HD))),
             r=[B_prm], w=[B_prm])
        P.op("act", lambda e: e.activation(out=prm[:, 40:44], in_=prm[:, 40:44], func=AF.Exp), r=[B_prm], w=[B_prm])

    def norm_transpose(xts, xns, junk, ssb, hT, bhT, gcol):
        junk_ap, bjunk = junk
        ss_ap, bss = ssb
        P.op("dve", lambda e: e.memset(ss_ap, 0.0), w=[bss])
        for j in range(4):
            xt, bxt = xts[j]
            xn, bxn = xns[j]
            P.op("act", lambda e, xt=xt, j=j: e.activation(out=junk_ap, in_=xt, func=AF.Square,
                                                          accum_out=ss_ap[:, j:j + 1]),
                 r=[bxt], w=[bjunk, bss])
            P.op("act", lambda e, j=j: e.activation(out=ss_ap[:, 4 + j:5 + j], in_=ss_ap[:, j:j + 1], func=AF.Sqrt,
                                                    bias=float(D * EPS)), r=[bss], w=[bss])
            P.op("dve", lambda e, j=j: e.reciprocal(out=ss_ap[:, 4 + j:5 + j], in_=ss_ap[:, 4 + j:5 + j]),
                 r=[bss], w=[bss])
            P.op("dve", lambda e, xt=xt, xn=xn, j=j: e.tensor_scalar(out=xn, in0=xt, scalar1=ss_ap[:, 4 + j:5 + j],
                                                                     scalar2=None, op0=ALU.mult),
                 r=[bxt, bss], w=[bxn])
        transpose_group([x[0] for x in xns], [x[1] for x in xns], hT, bhT, gcol)

    def transpose_group(srcs, bsrcs, hT, bhT, gcol):
        for c in range(16):
            pp, bpp = pst[c % 2]

            def tr(e, pp=pp, c=c):
                ins = None
                for j in range(4):
                    ins = e.transpose(pp[:, j * 128:(j + 1) * 128], srcs[j][:, c * 128:(c + 1) * 128], ident[:])
                return ins

            P.op("pe", tr, r=list(bsrcs) + [B_const], w=[bpp])
            if gcol is None:
                if c % 2 == 0:
                    P.op("act", lambda e, pp=pp, c=c: e.copy(out=hT[:, c, :], in_=pp), r=[bpp], w=[bhT])
                else:
                    P.op("dve", lambda e, pp=pp, c=c: e.tensor_copy(out=hT[:, c, :], in_=pp), r=[bpp], w=[bhT])
            else:
                if c % 2 == 0:
                    P.op("act", lambda e, pp=pp, c=c: e.activation(out=hT[:, c, :], in_=pp, func=AF.Copy,
                                                                   scale=prm[:, gcol + c:gcol + c + 1]),
                         r=[bpp, B_prm], w=[bhT])
                else:
                    P.op("dve", lambda e, pp=pp, c=c: e.tensor_scalar(out=hT[:, c, :], in0=pp,
                                                                      scalar1=prm[:, gcol + c:gcol + c + 1],
                                                                      scalar2=None, op0=ALU.mult),
                         r=[bpp, B_prm], w=[bhT])

    def fm(tensor, bufs, h, gain):
        return (tensor, bufs, h, gain)

    BLOCKS = [
        (0, 512, "fm", [(qt, B_qt, 0, 0), (qt, B_qt, 1, 0), (qt, B_qt, 2, 0), (qt, B_qt, 3, 0)]),
        (512, 512, "fm", [(qt, B_qt, 4, 0), (qt, B_qt, 5, 0), (kt, B_kt, KT_KCMP, None), (kt, B_kt, KT_KCMP + 1, None)]),
        (1024, 512, "fm", [(kt, B_kt, KT_VCMP, None), (kt, B_kt, KT_VCMP + 1, None), (kt, B_kt, KT_KSLC, 2),
                           (kt, B_kt, KT_KSLC + 1, 2)]),
        (1536, 256, "tm", (V_SLC, 2, False)),
        (1792, 256, "fm", [(kt, B_kt, KT_KWIN, 3), (kt, B_kt, KT_KWIN + 1, 3)]),
        (2048, 274, "tm", (V_WIN, 2, True)),
        (2322, 512, "fm", [(qt, B_qt, 6, 4), (qt, B_qt, 7, 4), (qt, B_qt, 8, 4), (qt, B_qt, 9, 4)]),
        (2834, 256, "fm", [(kt, B_kt, KT_KB, 5), (kt, B_kt, KT_KB + 1, 5)]),
        (3090, 256, "tm", (V_B, 2, False)),
        (3346, 512, "fm", [(qt, B_qt, 10, 6), (qt, B_qt, 11, 6), (qt, B_qt, 12, 6), (qt, B_qt, 13, 6)]),
        (3858, 512, "fm", [(qt, B_qt, 14, 6), (qt, B_qt, 15, 6), (kt, B_kt, KT_KC, 7), (kt, B_kt, KT_KC + 1, 7)]),
        (4370, 128, "fm", [(kt, B_kt, KT_KC + 2, 7)]),
        (4498, 384, "tm", (V_C, 3, False)),
    ]

    def phase1(l, x_src, B_xsrc):
        A.off = 0
        xts = [(A.alloc(D, F32), Buf(f"xt{j}")) for j in range(4)]
        xns = [(A.alloc(D), Buf(f"xn{j}")) for j in range(4)]
        junk = (A.alloc(D), Buf("junk"))
        ssb = (A.alloc(8, F32), Buf("ss"))
        hT2 = A.alloc(16 * 512)
        hT = hT2.rearrange("p (c t) -> p c t", c=16)
        bhT = Buf("hT")
        wbufs = Rot([(A.alloc(16 * 512).rearrange("p (c n) -> p c n", c=16), Buf(f"wb{i}")) for i in range(3)])
        sqs = Rot([(A.alloc(512), Buf(f"sq{i}")) for i in range(2)])
        rrs = Rot([(A.alloc(512, F32), Buf(f"rr{i}")) for i in range(2)])
        ots = Rot([(A.alloc(512), Buf(f"ot{i}")) for i in range(4)])
        vts = Rot([(A.alloc(384), Buf(f"vt{i}")) for i in range(3)])
        gts = Rot([(A.alloc(18, F32), Buf(f"gt{i}")) for i in range(3)])
        psA = Rot(ps[0:2])
        psB = Rot(ps[2:4])
        psT = Rot(ps[4:6])

        stages = []
        for g in range(NG):
            t0 = g * GT

            def ld_x(t0=t0):
                for j in range(4):
                    P.dma("sp", xts[j][0], x_src[t0 + 128 * j:t0 + 128 * (j + 1), :], r=[B_xsrc], w=[xts[j][1]])

            def cp_x(_h):
                norm_transpose(xts, xns, junk, ssb, hT, bhT, 0)

            stages.append((ld_x, cp_x))
            for (c0, ncols, kind, items) in BLOCKS:
                def ld_w(c0=c0, ncols=ncols):
                    wb, bwb = wbufs.next()
                    P.dma("sp", wb[:, :, :ncols], wb_in[l, :, c0:c0 + ncols].rearrange("(c p) n -> p c n", p=128),
                          r=[B_wb[("in", l)]], w=[bwb])
                    return wb, bwb

                if kind == "fm":
                    def cp(hnd, items=items, t0=t0):
                        wb, bwb = hnd
                        for hi, (tens, tbufs, h, gain) in enumerate(items):
                            pa, bpa = psA.next()

                            def mmf(e, pa=pa, wb=wb, hi=hi):
                                ins = None
                                for kc in range(16):
                                    ins = e.matmul(pa, wb[:, kc, hi * 128:(hi + 1) * 128], hT[:, kc, :],
                                                   start=(kc == 0), stop=(kc == 15))
                                return ins

                            P.op("pe", mmf, r=[bwb, bhT], w=[bpa])
                            ot, bot = ots.next()
                            if gain is None:
                                P.op("act", lambda e, ot=ot, pa=pa: e.copy(out=ot, in_=pa), r=[bpa], w=[bot])
                            else:
                                sq, bsq = sqs.next()
                                rr, brr = rrs.next()
                                pb, bpb = psB.next()
                                P.op("act", lambda e, sq=sq, pa=pa: e.activation(out=sq, in_=pa, func=AF.Square),
                                     r=[bpa], w=[bsq])
                                P.op("pe", lambda e, pb=pb, sq=sq: e.matmul(pb, ones[:], sq, start=True, stop=True),
                                     r=[bsq, B_const], w=[bpb])
                                P.op("act", lambda e, rr=rr, pb=pb: e.activation(out=rr, in_=pb, func=AF.Sqrt,
                                                                                 bias=float(HD * EPS)),
                                     r=[bpb], w=[brr])
                                P.op("dve", lambda e, rr=rr: e.reciprocal(out=rr, in_=rr), r=[brr], w=[brr])
                                P.op("dve", lambda e, ot=ot, pa=pa, rr=rr, gain=gain: e.scalar_tensor_tensor(
                                    out=ot, in0=pa, scalar=prm[:, 32 + gain:33 + gain], in1=rr, op0=ALU.mult,
                                    op1=ALU.mult), r=[bpa, brr, B_prm], w=[bot])
                            P.dma("sp", tens[h, :, t0:t0 + GT], ot, r=[bot], w=[tbufs[h]])
                else:
                    def cp(hnd, items=items, t0=t0, ncols=ncols):
                        wb, bwb = hnd
                        vh0, nh, has_g = items
                        for j in range(4):
                            pt_, bpt = psT.next()

                            def mmt(e, pt_=pt_, wb=wb, j=j, ncols=ncols):
                                ins = None
                                for kc in range(16):
                                    ins = e.matmul(pt_[:, :ncols], hT[:, kc, j * 128:(j + 1) * 128], wb[:, kc, :ncols],
                                                   start=(kc == 0), stop=(kc == 15))
                                return ins

                            P.op("pe", mmt, r=[bwb, bhT], w=[bpt])
                            vt, bvt = vts.next()
                            P.op("act", lambda e, vt=vt, pt_=pt_, nh=nh: e.copy(out=vt[:, :nh * 128],
                                                                                in_=pt_[:, :nh * 128]),
                                 r=[bpt], w=[bvt])
                            r0 = t0 + 128 * j
                            P.dma("sp", vv[vh0:vh0 + nh, r0:r0 + 128, :].rearrange("h p d -> p h d"),
                                  vt[:, :nh * 128].rearrange("p (h d) -> p h d", h=nh), r=[bvt],
                                  w=[B_vv[vh0 + i] for i in range(nh)])
                            if has_g:
                                gt, bgt = gts.next()
                                P.op("act", lambda e, gt=gt, pt_=pt_: e.activation(out=gt, in_=pt_[:, 256:274],
                                                                                    func=AF.Sigmoid),
                                     r=[bpt], w=[bgt])
                                P.dma("sp", gates[r0:r0 + 128, :], gt, r=[bgt], w=[B_gates])
                stages.append((ld_w, cp))
        run_pipeline(stages)
        P.barrier()

    def phase2(l):
        A.off = 0
        kcT = A.alloc(2 * 256).rearrange("p (k n) -> p k n", k=2)
        b_kcT = Buf("kcT")
        VC1 = A.alloc(2 * 2 * 200).rearrange("p (k a n) -> p k a n", k=2, a=2)
        b_VC1 = Buf("VC1")
        expn = A.alloc(4096)
        b_expn = Buf("expn")
        keepm = A.alloc(NT * 64, F32)
        addm = A.alloc(NT * 64, F32)
        b_km = Buf("km")
        gsb = A.alloc(NT * 18, F32).rearrange("p (i c) -> p i c", c=18)
        b_gsb = Buf("gsb")
        mark0 = A.off
        P.dma("sp", expn[0:64, :], c_expn, w=[b_expn])
        P.dma("sp", keepm, c_keep, w=[b_km])
        P.dma("sp", addm, c_add, w=[b_km])
        P.dma("sp", gsb, gates.rearrange("(i p) c -> p i c", p=128), r=[B_gates], w=[b_gsb])

        XTs = [(A.alloc(T), Buf(f"XT{i}")) for i in range(2)]
        W1s = [(A.alloc(32 * 128).rearrange("p (l f) -> p l f", l=32), Buf(f"W1{i}")) for i in range(2)]
        peTf = A.alloc(32, F32)
        peT = A.alloc(32)
        b_peT = Buf("peT")
        W2 = A.alloc(128)
        b_W2 = Buf("W2")
        cb = A.alloc(1, F32)
        b_cb = Buf("cb")
        uu = A.alloc(256, F32)
        t1 = A.alloc(256, F32)
        sg = A.alloc(256, F32)
        b_u = Buf("u")
        GT_ = A.alloc(256)
        b_G = Buf("G")
        sq = A.alloc(256)
        b_sq = Buf("sqc")
        rr = A.alloc(256, F32)
        b_rr = Buf("rrc")
        P.op("dve", lambda e: e.memset(GT_, 0.0), w=[b_G])
        P.op("dve", lambda e: e.memset(VC1.rearrange("p k a n -> p (k a n)"), 0.0), w=[b_VC1])
        it = 0
        for k in range(2):
            for kv in range(2):
                XT, bXT = XTs[it % 2]
                W1, bW1 = W1s[it % 2]
                it += 1
                src_h = (KT_KCMP if kv == 0 else KT_VCMP) + k
                P.dma("sp", XT, kt[src_h], r=[B_kt[src_h]], w=[bXT])
                P.dma("pool", W1, cmp_w1[l, kv].rearrange("(l d) f -> d l f", d=128), w=[bW1])
                P.dma("pool", W2, cmp_w2[l, kv], w=[b_W2])
                P.dma("sp", peTf, cmp_pe[l, kv].rearrange("l d -> d l"), w=[b_peT], allow_slow_non_contiguous=True)
                P.op("dve", lambda e: e.tensor_copy(out=peT, in_=peTf), r=[b_peT], w=[b_peT])
                p0, bp0 = ps[0]
                p1, bp1 = ps[1]
                p2, bp2 = ps[2]

                def mm_h(e, XT=XT, W1=W1):
                    ins = None
                    for li in range(32):
                        ins = e.matmul(p0[:, 0:255], W1[:, li, :], XT[:, li:li + 16 * 254 + 1:16], start=(li == 0),
                                       stop=(li == 31))
                    return ins

                P.op("pe", mm_h, r=[bXT, bW1], w=[bp0])

                def mm_c(e, W1=W1):
                    ins = None
                    for li in range(32):
                        ins = e.matmul(p1[:, 0:1], W1[:, li, :], peT[:, li:li + 1], start=(li == 0), stop=(li == 31))
                    return ins

                P.op("pe", mm_c, r=[b_peT, bW1], w=[bp1])
                P.op("dve", lambda e: e.tensor_copy(out=cb, in_=p1[:, 0:1]), r=[bp1], w=[b_cb])
                P.op("act", lambda e: e.activation(out=uu[:, 0:255], in_=p0[:, 0:255], func=AF.Identity, bias=cb),
                     r=[bp0, b_cb], w=[b_u])
                P.op("dve", lambda e: e.tensor_tensor(out=t1[:, 0:255], in0=uu[:, 0:255], in1=uu[:, 0:255],
                                                      op=ALU.mult), r=[b_u], w=[b_u])
                P.op("dve", lambda e: e.tensor_scalar(out=t1[:, 0:255], in0=t1[:, 0:255], scalar1=0.044715, scalar2=1.0,
                                                      op0=ALU.mult, op1=ALU.add), r=[b_u], w=[b_u])
                P.op("dve", lambda e: e.tensor_tensor(out=t1[:, 0:255], in0=t1[:, 0:255], in1=uu[:, 0:255],
                                                      op=ALU.mult), r=[b_u], w=[b_u])
                P.op("act", lambda e: e.activation(out=sg[:, 0:255], in_=t1[:, 0:255], func=AF.Sigmoid,
                                                   scale=1.5957691216057308), r=[b_u], w=[b_u])
                P.op("dve", lambda e: e.tensor_tensor(out=GT_[:, 0:255], in0=uu[:, 0:255], in1=sg[:, 0:255],
                                                      op=ALU.mult), r=[b_u], w=[b_G])
                if kv == 0:
                    P.op("pe", lambda e: e.matmul(p2[:, 0:256], W2, GT_, start=True, stop=True), r=[b_W2, b_G],
                         w=[bp2])
                    P.op("act", lambda e: e.activation(out=sq, in_=p2[:, 0:256], func=AF.Square), r=[bp2], w=[b_sq])
                    P.op("pe", lambda e: e.matmul(p1[:, 0:256], ones[:], sq, start=True, stop=True),
                         r=[b_sq, B_const, b_cb], w=[bp1])
                    P.op("act", lambda e: e.activation(out=rr, in_=p1[:, 0:256], func=AF.Sqrt, bias=float(HD * EPS)),
                         r=[bp1], w=[b_rr])
                    P.op("dve", lambda e: e.reciprocal(out=rr, in_=rr), r=[b_rr], w=[b_rr])
                    P.op("dve", lambda e, k=k: e.scalar_tensor_tensor(out=kcT[:, k, :], in0=p2[:, 0:256],
                                                                      scalar=prm[:, 33:34], in1=rr, op0=ALU.mult,
                                                                      op1=ALU.mult),
                         r=[bp2, b_rr, B_prm], w=[b_kcT])
                else:
                    for a in range(2):
                        pa, bpa = ps[3 + a]
                        P.op("pe", lambda e, pa=pa, a=a: e.matmul(pa[:, 0:128], GT_[:, a * 128:(a + 1) * 128], W2,
                                                                  start=True, stop=True), r=[b_W2, b_G], w=[bpa])
                        P.op("act", lambda e, pa=pa, a=a, k=k: e.copy(out=VC1[:, k, a, 0:128], in_=pa[:, 0:128]),
                             r=[bpa], w=[b_VC1])
        for k in range(2):
            for a in range(2):
                P.op("dve", lambda e, k=k, a=a: e.memset(VC1[:, k, a, 128:129], 1.0), w=[b_VC1])
                P.dma("sp", VC1[:, k, a, 129:193], c_ovl[:, a, :], w=[b_VC1])
        P.barrier()

        A.off = mark0
        psS = Rot(ps[0:3] + [(pst_t[1][:, :].bitcast(F32), pst[1][1])])
        psO = Rot(ps[3:6])
        PTs = Rot([(A.alloc(512), Buf(f"PT{i}")) for i in range(6)])
        sm = A.alloc(64, F32)
        b_sm = Buf("sm")
        mark1 = A.off

        fifo = []
        DEPTH = 3

        def push(fn):
            fifo.append(fn)
            while len(fifo) > DEPTH:
                fifo.pop(0)()

        def flush():
            while fifo:
                fifo.pop(0)()

        def attn(qT, bq, KT, bK, V1, bV, E, bE, i, jlist, po, bpo, ncol, mask=None):
            nchunks = (len(jlist) + 3) // 4
            for c in range(nchunks):
                chunk = jlist[4 * c:4 * c + 4]
                n = 128 * len(chunk)
                pS, bpS = psS.next()
                PT, bPT = PTs.next()

                def mmS(e, chunk=chunk, pS=pS):
                    ins = None
                    for jj, j in enumerate(chunk):
                        ins = e.matmul(pS[:, jj * 128:(jj + 1) * 128], KT[:, j * 128:(j + 1) * 128], qT, start=True,
                                       stop=(mask is None))
                        if mask is not None:
                            ins = e.matmul(pS[:, jj * 128:(jj + 1) * 128], expn[0:64, j * 128:(j + 1) * 128],
                                           mask[0][0:64, :], start=False, stop=True)
                    return ins

                rds = [bq, bK] + ([b_expn, mask[1]] if mask is not None else [])
                P.op("pe", mmS, r=rds, w=[bpS])
                P.op("act", lambda e, PT=PT, pS=pS, n=n: e.activation(out=PT[:, :n], in_=pS[:, :n], func=AF.Exp,
                                                                      scale=float(SCALE)), r=[bpS], w=[bPT])
                d0 = i - chunk[0]
                P.op("dve", lambda e, PT=PT, n=n, d0=d0: e.tensor_tensor(out=PT[:, :n], in0=PT[:, :n],
                                                                         in1=E[:, 128 * d0:128 * d0 + n], op=ALU.mult),
                     r=[bPT, bE], w=[bPT])

                def mmO(e, chunk=chunk, PT=PT, c=c):
                    ins = None
                    for jj, j in enumerate(chunk):
                        ins = e.matmul(po[:, :ncol], PT[:, jj * 128:(jj + 1) * 128], V1[:, j, :ncol],
                                       start=(c == 0 and jj == 0), stop=(c == nchunks - 1 and jj == len(chunk) - 1))
                    return ins

                push(lambda mmO=mmO, bPT=bPT: P.op("pe", mmO, r=[bPT, bV], w=[bpo]))

        def load_v1(dst, bdst, vh):
            P.dma("sp", dst[:, :, 0:128], vv[vh].rearrange("(i p) d -> p i d", p=128), r=[B_vv[vh]], w=[bdst])
            P.op("dve", lambda e: e.memset(dst[:, :, 128:129], 1.0), w=[bdst])

        def ecols(name):
            toff, pstep, width, ecol, start = PLAN[name]
            return ecol, width

        for k in range(2):
            A.off = mark1
            QT3 = A.alloc(3 * T).rearrange("p (h t) -> p h t", h=3)
            b_QT3 = Buf("QT3")
            Ks = A.alloc(T)
            Kw = A.alloc(T)
            b_Ks = Buf("Ks")
            b_Kw = Buf("Kw")
            Vs = A.alloc(NT * 130).rearrange("p (i d) -> p i d", d=130)
            Vw = A.alloc(NT * 130).rearrange("p (i d) -> p i d", d=130)
            b_Vs = Buf("Vs")
            b_Vw = Buf("Vw")
            Ec = A.alloc(3 * 6144).rearrange("p (h n) -> p h n", h=3)
            Esl = A.alloc(3 * 4096).rearrange("p (h n) -> p h n", h=3)
            Ew = A.alloc(3 * 640).rearrange("p (h n) -> p h n", h=3)
            b_E = Buf("E")
            accs = [(A.alloc(128, F32), Buf(f"acc{g}")) for g in range(3)]
            impa = A.alloc(64, F32)
            b_imp = Buf("imp")
            wk = A.alloc(64, F32)
            m8 = A.alloc(16, F32)
            Mb = A.alloc(64)
            b_M = Buf("Mb")
            MT = A.alloc(128)
            b_MT = Buf("MT")
            mixA = Rot([(A.alloc(384), Buf(f"mixA{i}")) for i in range(2)])
            for g in range(3):
                h = 3 * k + g
                P.dma("sp", QT3[:, g, :], qt[h], r=[B_qt[h]], w=[b_QT3])
                ec, w_ = ecols(("cmp0", h))
                P.dma("sp", Ec[:, g, 0:4096], etab[:, ec:ec + 4096], r=[B_etab], w=[b_E])
                ec, w_ = ecols(("cmp1", h))
                P.dma("sp", Ec[:, g, 4096:6144], etab[:, ec:ec + 2048], r=[B_etab], w=[b_E])
                ec, w_ = ecols(("slc", h))
                P.dma("sp", Esl[:, g, :], etab[:, ec:ec + 4096], r=[B_etab], w=[b_E])
                ec, w_ = ecols(("win", h))
                P.dma("sp", Ew[:, g, :], etab[:, ec:ec + 640], r=[B_etab], w=[b_E])
            P.dma("sp", Ks, kt[KT_KSLC + k], r=[B_kt[KT_KSLC + k]], w=[b_Ks])
            P.dma("sp", Kw, kt[KT_KWIN + k], r=[B_kt[KT_KWIN + k]], w=[b_Kw])
            load_v1(Vs, b_Vs, V_SLC + k)
            load_v1(Vw, b_Vw, V_WIN + k)

            for i in range(NT):
                tq = slice(i * 128, (i + 1) * 128)
                for g in range(3):
                    h = 3 * k + g
                    acc, bacc = accs[g]
                    po, bpo = psO.next()
                    na = 2 if i >= 16 else 1
                    pS, bpS = psS.next()
                    PT, bPT = PTs.next()

                    def mmS(e, pS=pS, g=g, na=na, tq=tq, k=k):
                        ins = None
                        for a in range(na):
                            ins = e.matmul(pS[:, a * 128:(a + 1) * 128], kcT[:, k, a * 128:(a + 1) * 128],
                                           QT3[:, g, tq], start=True, stop=True)
                        return ins

                    P.op("pe", mmS, r=[b_kcT, b_QT3], w=[bpS])
                    n = 128 * na
                    P.op("act", lambda e, PT=PT, pS=pS, n=n: e.activation(out=PT[:, :n], in_=pS[:, :n], func=AF.Exp,
                                                                          scale=float(SCALE)), r=[bpS], w=[bPT])
                    P.op("dve", lambda e, PT=PT, g=g, i=i: e.tensor_tensor(
                        out=PT[:, 0:128], in0=PT[:, 0:128], in1=Ec[:, g, i * 128:(i + 1) * 128], op=ALU.mult),
                         r=[bPT, b_E], w=[bPT])
                    if na == 2:
                        P.op("dve", lambda e, PT=PT, g=g, i=i: e.tensor_tensor(
                            out=PT[:, 128:256], in0=PT[:, 128:256],
                            in1=Ec[:, g, 4096 + (i - 16) * 128:4096 + (i - 15) * 128], op=ALU.mult),
                             r=[bPT, b_E], w=[bPT])

                    def mmO(e, PT=PT, po=po, na=na, k=k):
                        ins = None
                        for a in range(na):
                            ins = e.matmul(po[:, 0:193], PT[:, a * 128:(a + 1) * 128], VC1[:, k, a, 0:193],
                                           start=(a == 0), stop=(a == na - 1))
                        return ins

                    def cmp_tail(mmO=mmO, bPT=bPT, po=po, bpo=bpo, acc=acc, bacc=bacc, g=g, i=i, k=k):
                        P.op("pe", mmO, r=[bPT, b_VC1], w=[bpo])
                        c0 = 4 * g
                        P.op("dve", lambda e: e.tensor_scalar(out=sm[:, c0:c0 + 1], in0=po[:, 128:129],
                                                              scalar1=1e-30, scalar2=None, op0=ALU.max),
                             r=[bpo], w=[b_sm])
                        P.op("dve", lambda e: e.reciprocal(out=sm[:, c0:c0 + 1], in_=sm[:, c0:c0 + 1]),
                             r=[b_sm], w=[b_sm])
                        P.op("dve", lambda e: e.tensor_tensor(
                            out=sm[:, c0 + 1:c0 + 2], in0=sm[:, c0:c0 + 1],
                            in1=gsb[:, i, 9 * k + 3 * g:9 * k + 3 * g + 1], op=ALU.mult),
                             r=[b_sm, b_gsb], w=[b_sm])
                        P.op("act", lambda e: e.activation(out=acc, in_=po[:, 0:128], func=AF.Copy,
                                                           scale=sm[:, c0 + 1:c0 + 2]),
                             r=[bpo, b_sm], w=[bacc])
                        if g == 0:
                            P.op("dve", lambda e: e.tensor_scalar(out=impa, in0=po[:, 129:193],
                                                                  scalar1=sm[:, c0:c0 + 1], scalar2=None,
                                                                  op0=ALU.mult),
                                 r=[bpo, b_sm], w=[b_imp])
                        else:
                            P.op("dve", lambda e: e.scalar_tensor_tensor(
                                out=impa, in0=po[:, 129:193], scalar=sm[:, c0:c0 + 1], in1=impa, op0=ALU.mult,
                                op1=ALU.add), r=[bpo, b_sm, b_imp], w=[b_imp])

                    push(cmp_tail)
                def branch(g, br, i=i, tq=tq, k=k):
                    acc, bacc = accs[g]
                    po, bpo = psO.next()
                    if br == 1:
                        jl = list(range(i, -1, -1))
                        attn(QT3[:, g, tq], b_QT3, Ks, b_Ks, Vs, b_Vs, Esl[:, g, :], b_E, i, jl, po, bpo, 129,
                             mask=(MT, b_MT))
                    else:
                        jl = list(range(i, max(i - 4, 0) - 1, -1))
                        attn(QT3[:, g, tq], b_QT3, Kw, b_Kw, Vw, b_Vw, Ew[:, g, :], b_E, i, jl, po, bpo, 129)

                    def br_tail(po=po, bpo=bpo, acc=acc, bacc=bacc, g=g, br=br):
                        c0 = 4 * g + 2
                        P.op("dve", lambda e: e.reciprocal(out=sm[:, c0:c0 + 1], in_=po[:, 128:129]),
                             r=[bpo], w=[b_sm])
                        P.op("dve", lambda e: e.tensor_tensor(
                            out=sm[:, c0:c0 + 1], in0=sm[:, c0:c0 + 1],
                            in1=gsb[:, i, 9 * k + 3 * g + br:9 * k + 3 * g + br + 1], op=ALU.mult),
                             r=[b_sm, b_gsb], w=[b_sm])
                        P.op("dve", lambda e: e.scalar_tensor_tensor(
                            out=acc, in0=po[:, 0:128], scalar=sm[:, c0:c0 + 1], in1=acc, op0=ALU.mult, op1=ALU.add),
                             r=[bpo, b_sm, bacc], w=[bacc])

                    push(br_tail)

                for g in range(3):
                    branch(g, 2)
                flush()
                P.op("dve", lambda e, i=i: e.tensor_tensor(out=impa, in0=impa, in1=keepm[:, i * 64:(i + 1) * 64],
                                                           op=ALU.mult), r=[b_imp, b_km], w=[b_imp])
                P.op("dve", lambda e, i=i: e.tensor_tensor(out=impa, in0=impa, in1=addm[:, i * 64:(i + 1) * 64],
                                                           op=ALU.add), r=[b_imp, b_km], w=[b_imp])
                P.op("dve", lambda e: e.max(out=m8[:, 0:8], in_=impa), r=[b_imp], w=[b_sm])
                P.op("dve", lambda e: e.match_replace(out=wk, in_to_replace=m8[:, 0:8], in_values=impa,
                                                      imm_value=-2.0e30), r=[b_imp, b_sm], w=[b_sm])
                P.op("dve", lambda e: e.max(out=m8[:, 8:16], in_=wk), r=[b_sm], w=[b_sm])
                P.op("dve", lambda e: e.tensor_scalar(out=Mb, in0=impa, scalar1=m8[:, 15:16], scalar2=30000.0,
                                                      op0=ALU.is_lt, op1=ALU.mult), r=[b_imp, b_sm], w=[b_M])
                pp, bpp = pst[0]
                P.op("pe", lambda e, pp=pp: e.transpose(pp[0:64, 0:128], Mb, ident[:]), r=[b_M, B_const], w=[bpp])
                P.op("act", lambda e, pp=pp: e.copy(out=MT[0:64, :], in_=pp[0:64, 0:128]), r=[bpp], w=[b_MT])
                for g in range(3):
                    branch(g, 1)

                def mix_tail(i=i, k=k):
                    mx, bmx = mixA.next()
                    for g in range(3):
                        acc, bacc = accs[g]
                        P.op("act", lambda e, mx=mx, acc=acc, g=g: e.copy(out=mx[:, g * 128:(g + 1) * 128], in_=acc),
                             r=[bacc], w=[bmx])
                    P.dma("sp", mix[i * 128:(i + 1) * 128, 384 * k:384 * (k + 1)], mx, r=[bmx], w=[B_mix])

                push(mix_tail)
            flush()
            P.barrier()

        for kvh in range(2):
            A.off = mark1
            QT2 = A.alloc(2 * T).rearrange("p (h t) -> p h t", h=2)
            b_QT2 = Buf("QT2")
            Kb = A.alloc(T)
            b_Kb = Buf("Kb")
            Vb = A.alloc(NT * 130).rearrange("p (i d) -> p i d", d=130)
            b_Vb = Buf("Vb")
            Eb2 = A.alloc(2 * 256).rearrange("p (h n) -> p h n", h=2)
            b_Eb2 = Buf("Eb2")
            mixB = Rot([(A.alloc(256), Buf(f"mixB{i}")) for i in range(2)])
            for hh in range(2):
                h = 6 + 2 * kvh + hh
                P.dma("sp", QT2[:, hh, :], qt[h], r=[B_qt[h]], w=[b_QT2])
                ec, w_ = ecols(("b", 2 * kvh + hh))
                P.dma("sp", Eb2[:, hh, :], etab[:, ec:ec + 256], r=[B_etab], w=[b_Eb2])
            P.dma("sp", Kb, kt[KT_KB + kvh], r=[B_kt[KT_KB + kvh]], w=[b_Kb])
            load_v1(Vb, b_Vb, V_B + kvh)
            for i in range(NT):
                tq = slice(i * 128, (i + 1) * 128)
                mx, bmx = mixB.next()
                for hh in range(2):
                    po, bpo = psO.next()
                    jl = list(range(i, max(i - 1, 0) - 1, -1))
                    attn(QT2[:, hh, tq], b_QT2, Kb, b_Kb, Vb, b_Vb, Eb2[:, hh, :], b_Eb2, i, jl, po, bpo, 129)
                    def b_tail(po=po, bpo=bpo, mx=mx, bmx=bmx, hh=hh, kvh=kvh):
                        c0 = 16 + hh
                        sc = 40 + 2 * kvh + hh
                        P.op("dve", lambda e: e.tensor_tensor(
                            out=sm[:, c0:c0 + 1], in0=po[:, 128:129], in1=prm[:, sc:sc + 1], op=ALU.add),
                             r=[bpo, B_prm], w=[b_sm])
                        P.op("dve", lambda e: e.reciprocal(out=sm[:, c0:c0 + 1], in_=sm[:, c0:c0 + 1]),
                             r=[b_sm], w=[b_sm])
                        P.op("act", lambda e: e.activation(
                            out=mx[:, hh * 128:(hh + 1) * 128], in_=po[:, 0:128], func=AF.Copy,
                            scale=sm[:, c0:c0 + 1]), r=[bpo, b_sm], w=[bmx])

                    push(b_tail)

                def bmix_tail(i=i, kvh=kvh, mx=mx, bmx=bmx):
                    c1 = (6 + 2 * kvh) * 128
                    P.dma("sp", mix[i * 128:(i + 1) * 128, c1:c1 + 256], mx, r=[bmx], w=[B_mix])

                push(bmix_tail)
            flush()
            P.barrier()

        for s in range(2):
            A.off = mark1
            QTc = A.alloc(3 * T).rearrange("p (h t) -> p h t", h=3)
            b_QTc = Buf("QTc")
            KTc = A.alloc(3 * T).rearrange("p (h t) -> p h t", h=3)
            b_KTc = Buf("KTc")
            Vc = [A.alloc(NT * 130).rearrange("p (i d) -> p i d", d=130) for g in range(3)]
            b_Vc = [Buf(f"Vc{g}") for g in range(3)]
            wtot = sum(128 * (C_NPREV[g] + 1) for g in range(3))
            Ecc = A.alloc(wtot)
            b_Ecc = Buf("Ecc")
            eoff = []
            o = 0
            for g in range(3):
                eoff.append(o)
                o += 128 * (C_NPREV[g] + 1)
            mixC = Rot([(A.alloc(384), Buf(f"mixC{i}")) for i in range(2)])
            for g in range(3):
                h = 10 + 2 * g + s
                P.dma("sp", QTc[:, g, :], qt[h], r=[B_qt[h]], w=[b_QTc])
                P.dma("sp", KTc[:, g, :], kt[KT_KC + g], r=[B_kt[KT_KC + g]], w=[b_KTc])
                load_v1(Vc[g], b_Vc[g], V_C + g)
                ec, w_ = ecols(("c", g, s))
                P.dma("sp", Ecc[:, eoff[g]:eoff[g] + w_], etab[:, ec:ec + w_], r=[B_etab], w=[b_Ecc])
            for i in range(NT):
                tq = slice(i * 128, (i + 1) * 128)
                pos = []
                for g in range(3):
                    po, bpo = psO.next()
                    pos.append((po, bpo))
                    jl = list(range(i, max(i - C_NPREV[g], 0) - 1, -1))
                    attn(QTc[:, g, tq], b_QTc, KTc[:, g, :], b_KTc, Vc[g], b_Vc[g],
                         Ecc[:, eoff[g]:eoff[g] + 128 * (C_NPREV[g] + 1)], b_Ecc, i, jl, po, bpo, 129)
                    def c_den(po=po, bpo=bpo, g=g):
                        P.op("dve", lambda e: e.tensor_copy(out=sm[:, 20 + g:21 + g], in_=po[:, 128:129]),
                             r=[bpo], w=[b_sm])

                    push(c_den)

                def c_tail(pos=pos, i=i, s=s):
                    P.op("dve", lambda e: e.tensor_tensor(out=sm[:, 23:24], in0=sm[:, 20:21], in1=sm[:, 21:22],
                                                          op=ALU.add), r=[b_sm], w=[b_sm])
                    P.op("dve", lambda e: e.tensor_tensor(out=sm[:, 23:24], in0=sm[:, 23:24], in1=sm[:, 22:23],
                                                          op=ALU.add), r=[b_sm], w=[b_sm])
                    P.op("dve", lambda e: e.reciprocal(out=sm[:, 24:25], in_=sm[:, 23:24]), r=[b_sm], w=[b_sm])
                    mx, bmx = mixC.next()
                    for g in range(3):
                        po, bpo = pos[g]
                        P.op("act", lambda e, mx=mx, po=po, g=g: e.activation(
                            out=mx[:, g * 128:(g + 1) * 128], in_=po[:, 0:128], func=AF.Copy, scale=sm[:, 24:25]),
                             r=[bpo, b_sm], w=[bmx])
                    dst = mix[i * 128:(i + 1) * 128, 1280:2048].rearrange("p (g x) -> p g x", g=3)[:, :,
                                                                                                    s * 128:(s + 1) * 128]
                    P.dma("sp", dst, mx.rearrange("p (g d) -> p g d", g=3), r=[bmx], w=[B_mix])

                push(c_tail)
            flush()
            P.barrier()

    def phase3(l, x_src, B_xsrc, x_dst, B_xdst):
        A.off = 0
        xts = [(A.alloc(D, F32), Buf(f"xt{j}")) for j in range(4)]
        xns = [(A.alloc(D), Buf(f"xn{j}")) for j in range(4)]
        junk = (A.alloc(D), Buf("junk"))
        ssb = (A.alloc(8, F32), Buf("ss"))
        hT = A.alloc(16 * 512).rearrange("p (c t) -> p c t", c=16)
        bhT = Buf("hT")
        actT = A.alloc(44 * 512).rearrange("p (c t) -> p c t", c=44)
        b_act = [Buf(f"act{c}") for c in range(44)]
        wbufs = Rot([(A.alloc(16 * 512).rearrange("p (c n) -> p c n", c=16), Buf(f"wb{i}")) for i in range(4)])
        sgs = Rot([(A.alloc(512, F32), Buf(f"sg{i}")) for i in range(2)])
        psG = Rot(ps[0:2])
        psU = Rot(ps[2:4])
        psW = Rot(ps[4:6])

        stages = []
        for g in range(NG):
            t0 = g * GT

            def ld_x(t0=t0):
                for j in range(4):
                    P.dma("sp", xns[j][0], mix[t0 + 128 * j:t0 + 128 * (j + 1), :], r=[B_mix], w=[xns[j][1]])

            def cp_x(_h, t0=t0):
                for j in range(4):
                    P.dma("sp", xts[j][0], x_src[t0 + 128 * j:t0 + 128 * (j + 1), :], r=[B_xsrc], w=[xts[j][1]])
                transpose_group([x[0] for x in xns], [x[1] for x in xns], hT, bhT, None)

            stages.append((ld_x, cp_x))
            for cb in range(4):
                def ld_wo(cb=cb):
                    wb, bwb = wbufs.next()
                    P.dma("sp", wb, wb_out[l, :, cb * 512:(cb + 1) * 512].rearrange("(c p) n -> p c n", p=128),
                          r=[B_wb[("out", l)]], w=[bwb])
                    return wb, bwb

                def cp_wo(hnd, cb=cb):
                    wb, bwb = hnd
                    for j in range(4):
                        pw, bpw = psW.next()

                        def mmw(e, pw=pw, wb=wb, j=j):
                            ins = None
                            for mc in range(16):
                                ins = e.matmul(pw, hT[:, mc, j * 128:(j + 1) * 128], wb[:, mc, :], start=(mc == 0),
                                               stop=(mc == 15))
                            return ins

                        P.op("pe", mmw, r=[bwb, bhT], w=[bpw])
                        xt, bxt = xts[j]
                        P.op("dve", lambda e, pw=pw, xt=xt, cb=cb: e.tensor_tensor(
                            out=xt[:, cb * 512:(cb + 1) * 512], in0=pw, in1=xt[:, cb * 512:(cb + 1) * 512], op=ALU.add),
                             r=[bpw, bxt], w=[bxt])

                stages.append((ld_wo, cp_wo))

            def ld_none():
                return None

            def cp_norm(_h):
                norm_transpose(xts, xns, junk, ssb, hT, bhT, 16)

            stages.append((ld_none, cp_norm))
            for fb in range(11):
                def ld_gu(fb=fb):
                    wg, bwg = wbufs.next()
                    P.dma("sp", wg, wb_g[l, :, fb * 512:(fb + 1) * 512].rearrange("(c p) n -> p c n", p=128),
                          r=[B_wb[("g", l)]], w=[bwg])
                    wu, bwu = wbufs.next()
                    P.dma("sp", wu, wb_u[l, :, fb * 512:(fb + 1) * 512].rearrange("(c p) n -> p c n", p=128),
                          r=[B_wb[("u", l)]], w=[bwu])
                    return wg, bwg, wu, bwu

                def cp_gu(hnd, fb=fb):
                    wg, bwg, wu, bwu = hnd
                    for q in range(4):
                        fc = 4 * fb + q
                        pg, bpg = psG.next()
                        pu, bpu = psU.next()

                        def mmg(e, pg=pg, wg=wg, q=q):
                            ins = None
                            for kc in range(16):
                                ins = e.matmul(pg, wg[:, kc, q * 128:(q + 1) * 128], hT[:, kc, :], start=(kc == 0),
                                               stop=(kc == 15))
                            return ins

                        def mmu(e, pu=pu, wu=wu, q=q):
                            ins = None
                            for kc in range(16):
                                ins = e.matmul(pu, wu[:, kc, q * 128:(q + 1) * 128], hT[:, kc, :], start=(kc == 0),
                                               stop=(kc == 15))
                            return ins

                        P.op("pe", mmg, r=[bwg, bhT], w=[bpg])
                        P.op("pe", mmu, r=[bwu, bhT], w=[bpu])
                        sg_, bsg = sgs.next()
                        P.op("act", lambda e, sg_=sg_, pg=pg: e.activation(out=sg_, in_=pg, func=AF.Silu), r=[bpg],
                             w=[bsg])
                        P.op("dve", lambda e, sg_=sg_, pu=pu, fc=fc: e.tensor_tensor(out=actT[:, fc, :], in0=pu,
                                                                                     in1=sg_, op=ALU.mult),
                             r=[bsg, bpu], w=[b_act[fc]])

                stages.append((ld_gu, cp_gu))
            for cb in range(4):
                for fg in range(3):
                    fc0 = 16 * fg
                    nfc = 16 if fg < 2 else 12

                    def ld_d(cb=cb, fc0=fc0, nfc=nfc):
                        wd, bwd = wbufs.next()
                        P.dma("sp", wd[:, :nfc, :],
                              wb_d[l, fc0 * 128:(fc0 + nfc) * 128, cb * 512:(cb + 1) * 512].rearrange(
                                  "(c p) n -> p c n", p=128), r=[B_wb[("d", l)]], w=[bwd])
                        return wd, bwd

                    def cp_d(hnd, cb=cb, fg=fg, fc0=fc0, nfc=nfc, t0=t0):
                        wd, bwd = hnd
                        for j in range(4):
                            pd, bpd = ps[j]

                            def mmd(e, pd=pd, wd=wd, j=j):
                                ins = None
                                for f in range(nfc):
                                    ins = e.matmul(pd, actT[:, fc0 + f, j * 128:(j + 1) * 128], wd[:, f, :],
                                                   start=(fg == 0 and f == 0), stop=(fg == 2 and f == nfc - 1))
                                return ins

                            P.op("pe", mmd, r=[bwd] + b_act[fc0:fc0 + nfc], w=[bpd])
                            if fg == 2:
                                xt, bxt = xts[j]
                                P.op("dve", lambda e, pd=pd, xt=xt: e.tensor_tensor(
                                    out=xt[:, cb * 512:(cb + 1) * 512], in0=pd, in1=xt[:, cb * 512:(cb + 1) * 512],
                                    op=ALU.add), r=[bpd, bxt], w=[bxt])
                                if cb == 3:
                                    P.dma("pool", x_dst[t0 + 128 * j:t0 + 128 * (j + 1), :], xt, r=[bxt], w=[B_xdst])

                    stages.append((ld_d, cp_d))
        run_pipeline(stages)
        P.barrier()

    B_x = Buf("x")
    if "p0" in phases:
        phase0()
    for l in range(n_layers):
        load_params(l)
        src, bsrc = (x_in, B_x) if l == 0 else (xres, B_xres)
        last = (l == n_layers - 1)
        dst, bdst = (y_out, B_y) if last else (xres, B_xres)
        if "p1" in phases:
            phase1(l, src, bsrc)
        if l + 1 < n_layers:
            cast_weights(l + 1)
        if "p2" in phases:
            phase2(l)
        if "p3" in phases:
            phase3(l, src, bsrc, dst, bdst)
    P.barrier()
    P.emit()
    es.close()
    return nc


_CACHE = {}


def _host_inputs(inputs):
    consts = _build_consts()
    tabs = _build_tabs(np.asarray(inputs["rel_bias"], dtype=np.float32))
    shared = {
        "w_in": np.ascontiguousarray(inputs["w_in"], dtype=np.float32),
        "w_out": np.ascontiguousarray(inputs["w_out"], dtype=np.float32),
        "w_gate": np.ascontiguousarray(inputs["w_gate"], dtype=np.float32),
        "w_up": np.ascontiguousarray(inputs["w_up"], dtype=np.float32),
        "w_down": np.ascontiguousarray(inputs["w_down"], dtype=np.float32),
        "norm_attn": np.ascontiguousarray(inputs["norm_attn"], dtype=np.float32),
        "norm_ffn": np.ascontiguousarray(inputs["norm_ffn"], dtype=np.float32),
        "qk_gain": np.ascontiguousarray(inputs["qk_gain"], dtype=np.float32),
        "cmp_pe": np.ascontiguousarray(inputs["cmp_pe"], dtype=np.float32),
        "cmp_w1": np.ascontiguousarray(inputs["cmp_w1"], dtype=np.float32),
        "cmp_w2": np.ascontiguousarray(inputs["cmp_w2"], dtype=np.float32),
        "sinks": np.ascontiguousarray(inputs["sinks"], dtype=np.float32),
        "tabs": tabs,
        "c_ident": consts["ident"], "c_jrev": consts["jrev"], "c_ones": consts["ones"], "c_expn": consts["expn"],
        "c_ovl": consts["ovl"], "c_keepm": consts["keepm"], "c_addm": consts["addm"],
    }
    return shared


def kernel(x, norm_attn, w_in, qk_gain, cmp_pe, cmp_w1, cmp_w2, sinks, rel_bias, w_out, norm_ffn, w_gate, w_up,
           w_down):
    inputs = dict(x=x, norm_attn=norm_attn, w_in=w_in, qk_gain=qk_gain, cmp_pe=cmp_pe, cmp_w1=cmp_w1, cmp_w2=cmp_w2,
                  sinks=sinks, rel_bias=rel_bias, w_out=w_out, norm_ffn=norm_ffn, w_gate=w_gate, w_up=w_up,
                  w_down=w_down)
    shared = _host_inputs(inputs)
    x = np.asarray(x, dtype=np.float32)
    nc = build_program()
    in_maps = []
    for c in range(8):
        m = dict(shared)
        m["x"] = np.ascontiguousarray(x[c // 2])
        in_maps.append(m)
    res = run_bass_kernel_spmd(nc, in_maps, core_ids=list(range(8)))
    out = np.stack([np.asarray(res.results[2 * b]["y"], dtype=np.float32) for b in range(4)], axis=0)
    return out
```

```python
import math
from contextlib import ExitStack

import numpy as np
import ml_dtypes
import concourse.bass as bass
import concourse.mybir as mybir
from concourse.bass_utils import run_bass_kernel_spmd

F32 = mybir.dt.float32
BF16 = mybir.dt.bfloat16
AF = mybir.ActivationFunctionType
ALU = mybir.AluOpType

T = 4096
D = 2048
NT = T // 128
GT = 512
NG = T // GT
DFF = 5632
NIN = 4882
HD = 128
SCALE = HD ** -0.5
EPS = 1e-6
NEG = -30000.0

C_QA, C_KCA, C_VCA, C_KSA, C_VSA, C_KWA, C_VWA, C_GA = 0, 768, 1024, 1280, 1536, 1792, 2048, 2304
C_QB, C_KB, C_VB, C_QC, C_KC, C_VC = 2322, 2834, 3090, 3346, 4114, 4498
KT_KCMP, KT_VCMP, KT_KSLC, KT_KWIN, KT_KB, KT_KC = 0, 2, 4, 6, 8, 10
V_SLC, V_WIN, V_B, V_C = 0, 2, 4, 6
DIL = ((128, 1), (512, 4), (2048, 16))
C_NPREV = (1, 4, 16)


def _rel_bucket(dist):
    dist = np.maximum(dist, 0)
    far = np.maximum(dist, 16).astype(np.float32)
    lb = 16 + (np.log(far / np.float32(16)) / np.float32(math.log(2048 / 16)) * np.float32(16)).astype(np.int32)
    lb = np.minimum(lb, 31)
    return np.where(dist < 16, dist, lb).astype(np.int64)


def _table_plan():
    plan = {}
    tab_off = 0
    ecol = 0
    specs = []
    for h in range(6):
        specs.append((("slc", h), 127 + 4096, 1, 4096, 0))
    for h in range(6):
        specs.append((("win", h), 127 + 640, 1, 640, 0))
    for h in range(4):
        specs.append((("b", h), 127 + 256, 1, 256, 0))
    for g in range(3):
        for s in range(2):
            w = 128 * (C_NPREV[g] + 1)
            specs.append((("c", g, s), 127 + w, 1, w, 0))
    for h in range(6):
        specs.append((("cmp0", h), 8176, 16, 4096, 2048))
        specs.append((("cmp1", h), -1, 16, 2048, 2048))
    for (name, tlen, pstep, width, start) in specs:
        if tlen < 0:
            toff = plan[("cmp0", name[1])][0]
        else:
            toff = tab_off
            tab_off += tlen
        plan[name] = (toff, pstep, width, ecol, start)
        ecol += width
    return plan, tab_off, ecol


PLAN, TAB_LEN, ETOT = _table_plan()


def _build_tabs(rel_bias):
    tabs = np.full((TAB_LEN,), NEG, dtype=np.float32)

    def fill(name, col, dvals, valid):
        toff = PLAN[name][0]
        idx = _rel_bucket(np.where(valid, dvals, 0))
        vals = rel_bias[idx, col]
        seg = tabs[toff:toff + len(dvals)]
        seg[valid] = vals[valid]

    for h in range(6):
        d = np.arange(127 + 4096) - 127
        fill(("slc", h), h, d, d >= 0)
        d = np.arange(127 + 640) - 127
        fill(("win", h), h, d, (d >= 0) & (d <= 511))
        d = np.arange(8176) - 4111
        fill(("cmp0", h), h, d, d >= 0)
    for h in range(4):
        d = np.arange(127 + 256) - 127
        fill(("b", h), 6 + h, d, (d >= 0) & (d <= 127))
    for g in range(3):
        win, dil = DIL[g]
        for s in range(2):
            w = 128 * (C_NPREV[g] + 1)
            d = np.arange(127 + w) - 127
            fill(("c", g, s), 10 + 2 * g + s, d, (d >= 0) & (d <= win) & (d % dil == 0))
    return tabs


def _build_consts():
    bf = ml_dtypes.bfloat16
    ident = np.eye(128, dtype=np.float32).astype(bf)
    jrev = np.eye(128, dtype=np.float32)[::-1].copy().astype(bf)
    ones = np.ones((128, 128), dtype=np.float32).astype(bf)
    expn = np.zeros((64, 4096), dtype=np.float32)
    for j in range(64):
        expn[j, 64 * j:64 * j + 64] = -1.0
    expn = expn.astype(bf)
    c0 = np.arange(255)[:, None] * 16
    s0 = np.arange(64)[None, :] * 64
    ov = np.clip(np.minimum(c0 + 32, s0 + 64) - np.maximum(c0, s0), 0, None) / 32.0
    ovl = np.zeros((256, 64), dtype=np.float32)
    ovl[:255] = ov
    ovl = ovl.reshape(2, 128, 64).transpose(1, 0, 2).copy().astype(bf)
    tpos = np.arange(T)
    blk = np.arange(64)[None, :]
    cur = (tpos // 64)[:, None]
    forced = (blk == 0) | (blk == cur) | (blk == cur - 1)
    causal = blk * 64 <= tpos[:, None]
    keep = (causal & ~forced).astype(np.float32)
    add = np.where(causal, np.where(forced, 1.0e4, 0.0), -1.0e30).astype(np.float32)
    keep = keep.reshape(NT, 128, 64).transpose(1, 0, 2).reshape(128, NT * 64).copy()
    add = add.reshape(NT, 128, 64).transpose(1, 0, 2).reshape(128, NT * 64).copy()
    return dict(ident=ident, jrev=jrev, ones=ones, expn=expn, ovl=ovl, keepm=keep, addm=add)


class Buf:
    __slots__ = ("name", "w", "r", "excl")

    def __init__(self, name, excl=False):
        self.name = name
        self.w = {}
        self.r = {}
        self.excl = excl


class Prog:
    ENG = ("pe", "act", "dve", "pool", "sp")

    def __init__(self, nc, es):
        self.nc = nc
        self.eng = {"pe": nc.tensor, "act": nc.scalar, "dve": nc.vector, "pool": nc.gpsimd, "sp": nc.sync}
        self.streams = {k: [] for k in self.ENG}
        self.cnt = {}
        self.waited = {}
        self.sem = {}
        for k in list(self.ENG):
            self.sem[k] = es.enter_context(nc.semaphore("s_" + k))
            self.cnt[k] = 0
        self.NS = 24
        self.dnext = {"sp": 0, "pool": 0}
        for q in ("sp", "pool"):
            for i in range(self.NS):
                self.sem[(q, i)] = es.enter_context(nc.semaphore(f"d_{q}{i}"))
                self.cnt[(q, i)] = 0

    def _collect(self, me, reads, writes):
        need = {}
        for b in reads:
            for t, v in b.w.items():
                if v > need.get(t, 0):
                    need[t] = v
        for b in writes:
            for t, v in b.w.items():
                if v > need.get(t, 0):
                    need[t] = v
            for t, v in b.r.items():
                if v > need.get(t, 0):
                    need[t] = v
        out = []
        for t, v in need.items():
            if me == "pe" and t == "pe":
                continue
            if self.waited.get((me, t), 0) >= v:
                continue
            self.waited[(me, t)] = v
            out.append((t, v))
        return out

    def _mark(self, tgt, idx, reads, writes):
        for b in reads:
            if b.r.get(tgt, 0) < idx:
                b.r[tgt] = idx
        for b in writes:
            b.w = {tgt: idx}
            b.r = {}

    def op(self, e, fn, r=(), w=()):
        if any(b.excl for b in r):
            w = list(w) + [b for b in r if b.excl]
            r = [b for b in r if not b.excl]
        waits = self._collect(e, r, w)
        self.cnt[e] += 1
        idx = self.cnt[e]
        sem = self.sem

        def run(eng, waits=waits, fn=fn, s=sem[e]):
            for t, v in waits:
                eng.wait_ge(sem[t], v)
            fn(eng).then_inc(s, 1)

        self.streams[e].append(run)
        self._mark(e, idx, r, w)

    def dma(self, q, out, in_, r=(), w=(), merge=(), **kw):
        tgt = (q, self.dnext[q] % self.NS)
        self.dnext[q] += 1
        waits = self._collect(q, r, w)
        prev = self.cnt[tgt]
        if prev > 0 and self.waited.get((q, tgt), 0) < prev:
            self.waited[(q, tgt)] = prev
            waits.append((tgt, prev))
        self.cnt[tgt] += 16
        idx = self.cnt[tgt]
        sem = self.sem

        def run(eng, waits=waits, s=sem[tgt]):
            for t, v in waits:
                eng.wait_ge(sem[t], v)
            eng.dma_start(out=out, in_=in_, **kw).then_inc(s, 16)

        self.streams[q].append(run)
        self._mark(tgt, idx, r, w)
        for b in merge:
            b.w[tgt] = idx

    def barrier(self):
        for e in self.ENG:
            waits = []
            for t, v in self.cnt.items():
                if v == 0 or t == e:
                    continue
                if self.waited.get((e, t), 0) >= v:
                    continue
                self.waited[(e, t)] = v
                waits.append((t, v))
            if waits:
                sem = self.sem

                def run(eng, waits=waits):
                    for t, v in waits:
                        eng.wait_ge(sem[t], v)

                self.streams[e].append(run)

    def emit(self):
        with self.nc.Block() as block:
            @block.tensor
            def _(e):
                for f in self.streams["pe"]:
                    f(e)

            @block.scalar
            def _(e):
                for f in self.streams["act"]:
                    f(e)

            @block.vector
            def _(e):
                for f in self.streams["dve"]:
                    f(e)

            @block.gpsimd
            def _(e):
                for f in self.streams["pool"]:
                    f(e)

            @block.sync
            def _(e):
                for f in self.streams["sp"]:
                    f(e)


class Arena:
    def __init__(self, ap, nelem):
        self.ap = ap
        self.n = nelem
        self.off = 0

    def alloc(self, nfree, dtype=BF16):
        nb = nfree * (4 if dtype == F32 else 2)
        nb = (nb + 63) // 64 * 64
        ne = nb // 2
        assert self.off + ne <= self.n, ("arena overflow", self.off, ne, self.n)
        a = self.ap[:, self.off:self.off + ne]
        self.off += ne
        if dtype == F32:
            return a.bitcast(F32)[:, :nfree]
        return a[:, :nfree]


class Rot:
    def __init__(self, items):
        self.items = items
        self.i = 0

    def next(self):
        it = self.items[self.i % len(self.items)]
        self.i += 1
        return it


def run_pipeline(stages):
    handles = [None] * len(stages)
    if stages:
        handles[0] = stages[0][0]()
    for n in range(len(stages)):
        if n + 1 < len(stages):
            handles[n + 1] = stages[n + 1][0]()
        stages[n][1](handles[n])


def build_program(n_layers=2, dbg=False, phases=("p0", "p1", "p2", "p3")):
    nc = bass.Bass("TRN2", target_bir_lowering=False)
    es = ExitStack()
    P = Prog(nc, es)

    def din(name, shape, dt=F32):
        return nc.dram_tensor(name, list(shape), dt, kind="ExternalInput")

    x_in = din("x", [T, D]).ap()
    w_in_f = din("w_in", [2, D, NIN]).ap()
    w_out_f = din("w_out", [2, D, D]).ap()
    w_g_f = din("w_gate", [2, D, DFF]).ap()
    w_u_f = din("w_up", [2, D, DFF]).ap()
    w_d_f = din("w_down", [2, DFF, D]).ap()
    norm_attn = din("norm_attn", [2, D]).ap()
    norm_ffn = din("norm_ffn", [2, D]).ap()
    qk_gain = din("qk_gain", [2, 8, HD]).ap()
    cmp_pe = din("cmp_pe", [2, 2, 32, HD]).ap()
    cmp_w1 = din("cmp_w1", [2, 2, 4096, HD]).ap()
    cmp_w2 = din("cmp_w2", [2, 2, HD, HD]).ap()
    sinks = din("sinks", [2, 4]).ap()
    tabs_t = din("tabs", [TAB_LEN])
    c_ident = din("c_ident", [128, 128], BF16).ap()
    c_jrev = din("c_jrev", [128, 128], BF16).ap()
    c_ones = din("c_ones", [128, 128], BF16).ap()
    c_expn = din("c_expn", [64, 4096], BF16).ap()
    c_ovl = din("c_ovl", [128, 2, 64], BF16).ap()
    c_keep = din("c_keepm", [128, NT * 64]).ap()
    c_add = din("c_addm", [128, NT * 64]).ap()

    okind = "ExternalOutput" if dbg else "Internal"
    y_out = nc.dram_tensor("y", [T, D], F32, kind="ExternalOutput").ap()
    xres = nc.dram_tensor("xres", [T, D], F32, kind="Internal").ap()
    wb_in = nc.dram_tensor("wb_in", [2, D, NIN], BF16, kind="Internal").ap()
    wb_out = nc.dram_tensor("wb_out", [2, D, D], BF16, kind="Internal").ap()
    wb_g = nc.dram_tensor("wb_g", [2, D, DFF], BF16, kind="Internal").ap()
    wb_u = nc.dram_tensor("wb_u", [2, D, DFF], BF16, kind="Internal").ap()
    wb_d = nc.dram_tensor("wb_d", [2, DFF, D], BF16, kind="Internal").ap()
    etab = nc.dram_tensor("etab", [128, ETOT], BF16, kind="Internal").ap()
    qt = nc.dram_tensor("qt", [16, 128, T], BF16, kind=okind).ap()
    kt = nc.dram_tensor("kt", [13, 128, T], BF16, kind=okind).ap()
    vv = nc.dram_tensor("vv", [9, T, 128], BF16, kind=okind).ap()
    gates = nc.dram_tensor("gates", [T, 18], F32, kind=okind).ap()
    mix = nc.dram_tensor("mix", [T, D], BF16, kind=okind).ap()
    dbgA = nc.dram_tensor("dbgA", [3, T, 128], F32, kind=okind).ap() if dbg else None
    B_dbgA = Buf("dbgA")

    B_wb = {(n, l): Buf(f"{n}{l}") for n in ("in", "out", "g", "u", "d") for l in range(2)}
    B_etab = Buf("etab")
    B_qt = [Buf(f"qt{h}") for h in range(16)]
    B_kt = [Buf(f"kt{h}") for h in range(13)]
    B_vv = [Buf(f"vv{h}") for h in range(9)]
    B_gates = Buf("gates")
    B_mix = Buf("mix")
    B_xres = Buf("xres")
    B_y = Buf("y")

    ARENA_N = 100352
    arena_t = es.enter_context(nc.sbuf_tensor("arena", [128, ARENA_N], BF16))
    ident = es.enter_context(nc.sbuf_tensor("ident", [128, 128], BF16))
    jrev = es.enter_context(nc.sbuf_tensor("jrev", [128, 128], BF16))
    ones = es.enter_context(nc.sbuf_tensor("ones", [128, 128], BF16))
    prm = es.enter_context(nc.sbuf_tensor("prm", [128, 64], F32))
    B_const = Buf("const")
    B_prm = Buf("prm")
    A = Arena(arena_t[:], ARENA_N)

    ps_t = [es.enter_context(nc.psum_tensor(f"ps{i}", [128, 512], F32)) for i in range(6)]
    pst_t = [es.enter_context(nc.psum_tensor(f"pst{i}", [128, 1024], BF16)) for i in range(2)]
    ps = [(t[:], Buf(f"ps{i}", excl=True)) for i, t in enumerate(ps_t)]
    pst = [(pst_t[i][:, 0:512], Buf(f"pst{i}", excl=True)) for i in range(2)]

    P.dma("sp", ident[:], c_ident, w=[B_const])
    P.dma("sp", jrev[:], c_jrev, w=[B_const])
    P.dma("sp", ones[:], c_ones, w=[B_const])

    cast_thr = [Buf(f"castthr{i}") for i in range(2)]
    cast_n = [0]

    def cast_weights(l):
        for (nm, src, dst, rows) in (("in", w_in_f, wb_in, D), ("out", w_out_f, wb_out, D), ("g", w_g_f, wb_g, D),
                                     ("u", w_u_f, wb_u, D), ("d", w_d_f, wb_d, DFF)):
            for r0 in range(0, rows, 256):
                cast_n[0] += 1
                P.dma("pool", dst[l, r0:r0 + 256, :], src[l, r0:r0 + 256, :], w=[cast_thr[cast_n[0] % 2]],
                      merge=[B_wb[(nm, l)]])

    cast_weights(0)

    def phase0():
        A.off = 0
        CH = 2048
        Hs = [(A.alloc(CH, F32), Buf(f"H{i}")) for i in range(2)]
        Hbs = [(A.alloc(CH), Buf(f"Hb{i}")) for i in range(2)]
        Ebs = [(A.alloc(CH), Buf(f"Eb{i}")) for i in range(2)]
        psr = Rot(ps[0:4])
        k = 0
        for name, (toff, pstep, width, ecol, start) in PLAN.items():
            for c0 in range(0, width, CH):
                w = min(CH, width - c0)
                H, bH = Hs[k % 2]
                Hb, bHb = Hbs[k % 2]
                Eb, bEb = Ebs[k % 2]
                k += 1
                src = bass.AP(tabs_t, toff + start + c0, [[pstep, 128], [1, w]])
                P.dma("sp", H[:, :w], src, w=[bH])
                P.op("act", lambda e, H=H, Hb=Hb, w=w: e.activation(out=Hb[:, :w], in_=H[:, :w], func=AF.Exp),
                     r=[bH], w=[bHb])
                for cc in range(0, w, 512):
                    n = min(512, w - cc)
                    pp, bpp = psr.next()
                    P.op("pe", lambda e, pp=pp, Hb=Hb, cc=cc, n=n: e.matmul(pp[:, :n], jrev[:], Hb[:, cc:cc + n],
                                                                            start=True, stop=True),
                         r=[bHb, B_const], w=[bpp])
                    P.op("dve", lambda e, pp=pp, Eb=Eb, cc=cc, n=n: e.tensor_copy(out=Eb[:, cc:cc + n], in_=pp[:, :n]),
                         r=[bpp], w=[bEb])
                P.dma("sp", etab[:, ecol + c0:ecol + c0 + w], Eb[:, :w], r=[bEb], w=[B_etab])
        P.barrier()

    def load_params(l):
        P.dma("sp", prm[:, 0:16], norm_attn[l].rearrange("(c p) -> p c", p=128), w=[B_prm],
              allow_slow_non_contiguous=True)
        P.dma("sp", prm[:, 16:32], norm_ffn[l].rearrange("(c p) -> p c", p=128), w=[B_prm],
              allow_slow_non_contiguous=True)
        P.dma("sp", prm[:, 32:40], qk_gain[l].rearrange("n d -> d n"), w=[B_prm], allow_slow_non_contiguous=True)
        P.dma("sp", prm[:, 40:44], sinks[l].partition_broadcast(128), w=[B_prm], allow_slow_non_contiguous=True)
        P.op("dve", lambda e: e.tensor_scalar_mul(out=prm[:, 0:32], in0=prm[:, 0:32], scalar1=float(math.sqrt(D))),
             r=[B_prm], w=[B_prm])
        P.op("dve", lambda e: e.tensor_scalar_mul(out=prm[:, 32:40], in0=prm[:, 32:40], scalar1=float(math.sqrt(HD))),
             r=[B_prm], w=[B_prm])
        P.op("act", lambda e: e.activation(out=prm[:, 40:44], in_=prm[:, 40:44], func=AF.Exp), r=[B_prm], w=[B_prm])

    def norm_transpose(xts, xns, junk, ssb, hT, bhT, gcol):
        junk_ap, bjunk = junk
        ss_ap, bss = ssb
        P.op("dve", lambda e: e.memset(ss_ap, 0.0), w=[bss])
        for j in range(4):
            xt, bxt = xts[j]
            xn, bxn = xns[j]
            P.op("act", lambda e, xt=xt, j=j: e.activation(out=junk_ap, in_=xt, func=AF.Square,
                                                          accum_out=ss_ap[:, j:j + 1]),
                 r=[bxt], w=[bjunk, bss])
            P.op("act", lambda e, j=j: e.activation(out=ss_ap[:, 4 + j:5 + j], in_=ss_ap[:, j:j + 1], func=AF.Sqrt,
                                                    bias=float(D * EPS)), r=[bss], w=[bss])
            P.op("dve", lambda e, j=j: e.reciprocal(out=ss_ap[:, 4 + j:5 + j], in_=ss_ap[:, 4 + j:5 + j]),
                 r=[bss], w=[bss])
            P.op("dve", lambda e, xt=xt, xn=xn, j=j: e.tensor_scalar(out=xn, in0=xt, scalar1=ss_ap[:, 4 + j:5 + j],
                                                                     scalar2=None, op0=ALU.mult),
                 r=[bxt, bss], w=[bxn])
        transpose_group([x[0] for x in xns], [x[1] for x in xns], hT, bhT, gcol)

    def transpose_group(srcs, bsrcs, hT, bhT, gcol):
        for c in range(16):
            pp, bpp = pst[c % 2]

            def tr(e, pp=pp, c=c):
                ins = None
                for j in range(4):
                    ins = e.transpose(pp[:, j * 128:(j + 1) * 128], srcs[j][:, c * 128:(c + 1) * 128], ident[:])
                return ins

            P.op("pe", tr, r=list(bsrcs) + [B_const], w=[bpp])
            if gcol is None:
                if c % 2 == 0:
                    P.op("act", lambda e, pp=pp, c=c: e.copy(out=hT[:, c, :], in_=pp), r=[bpp], w=[bhT])
                else:
                    P.op("dve", lambda e, pp=pp, c=c: e.tensor_copy(out=hT[:, c, :], in_=pp), r=[bpp], w=[bhT])
            else:
                if c % 2 == 0:
                    P.op("act", lambda e, pp=pp, c=c: e.activation(out=hT[:, c, :], in_=pp, func=AF.Copy,
                                                                   scale=prm[:, gcol + c:gcol + c + 1]),
                         r=[bpp, B_prm], w=[bhT])
                else:
                    P.op("dve", lambda e, pp=pp, c=c: e.tensor_scalar(out=hT[:, c, :], in0=pp,
                                                                      scalar1=prm[:, gcol + c:gcol + c + 1],
                                                                      scalar2=None, op0=ALU.mult),
                         r=[bpp, B_prm], w=[bhT])

    def fm(tensor, bufs, h, gain):
        return (tensor, bufs, h, gain)

    BLOCKS = [
        (0, 512, "fm", [(qt, B_qt, 0, 0), (qt, B_qt, 1, 0), (qt, B_qt, 2, 0), (qt, B_qt, 3, 0)]),
        (512, 512, "fm", [(qt, B_qt, 4, 0), (qt, B_qt, 5, 0), (kt, B_kt, KT_KCMP, None), (kt, B_kt, KT_KCMP + 1, None)]),
        (1024, 512, "fm", [(kt, B_kt, KT_VCMP, None), (kt, B_kt, KT_VCMP + 1, None), (kt, B_kt, KT_KSLC, 2),
                           (kt, B_kt, KT_KSLC + 1, 2)]),
        (1536, 256, "tm", (V_SLC, 2, False)),
        (1792, 256, "fm", [(kt, B_kt, KT_KWIN, 3), (kt, B_kt, KT_KWIN + 1, 3)]),
        (2048, 274, "tm", (V_WIN, 2, True)),
        (2322, 512, "fm", [(qt, B_qt, 6, 4), (qt, B_qt, 7, 4), (qt, B_qt, 8, 4), (qt, B_qt, 9, 4)]),
        (2834, 256, "fm", [(kt, B_kt, KT_KB, 5), (kt, B_kt, KT_KB + 1, 5)]),
        (3090, 256, "tm", (V_B, 2, False)),
        (3346, 512, "fm", [(qt, B_qt, 10, 6), (qt, B_qt, 11, 6), (qt, B_qt, 12, 6), (qt, B_qt, 13, 6)]),
        (3858, 512, "fm", [(qt, B_qt, 14, 6), (qt, B_qt, 15, 6), (kt, B_kt, KT_KC, 7), (kt, B_kt, KT_KC + 1, 7)]),
        (4370, 128, "fm", [(kt, B_kt, KT_KC + 2, 7)]),
        (4498, 384, "tm", (V_C, 3, False)),
    ]

    def phase1(l, x_src, B_xsrc):
        A.off = 0
        xts = [(A.alloc(D, F32), Buf(f"xt{j}")) for j in range(4)]
        xns = [(A.alloc(D), Buf(f"xn{j}")) for j in range(4)]
        junk = (A.alloc(D), Buf("junk"))
        ssb = (A.alloc(8, F32), Buf("ss"))
        hT2 = A.alloc(16 * 512)
        hT = hT2.rearrange("p (c t) -> p c t", c=16)
        bhT = Buf("hT")
        wbufs = Rot([(A.alloc(16 * 512).rearrange("p (c n) -> p c n", c=16), Buf(f"wb{i}")) for i in range(3)])
        sqs = Rot([(A.alloc(512), Buf(f"sq{i}")) for i in range(2)])
        rrs = Rot([(A.alloc(512, F32), Buf(f"rr{i}")) for i in range(2)])
        ots = Rot([(A.alloc(512), Buf(f"ot{i}")) for i in range(4)])
        vts = Rot([(A.alloc(384), Buf(f"vt{i}")) for i in range(3)])
        gts = Rot([(A.alloc(18, F32), Buf(f"gt{i}")) for i in range(3)])
        psA = Rot(ps[0:2])
        psB = Rot(ps[2:4])
        psT = Rot(ps[4:6])

        stages = []
        for g in range(NG):
            t0 = g * GT

            def ld_x(t0=t0):
                for j in range(4):
                    P.dma("sp", xts[j][0], x_src[t0 + 128 * j:t0 + 128 * (j + 1), :], r=[B_xsrc], w=[xts[j][1]])

            def cp_x(_h):
                norm_transpose(xts, xns, junk, ssb, hT, bhT, 0)

            stages.append((ld_x, cp_x))
            for (c0, ncols, kind, items) in BLOCKS:
                def ld_w(c0=c0, ncols=ncols):
                    wb, bwb = wbufs.next()
                    P.dma("sp", wb[:, :, :ncols], wb_in[l, :, c0:c0 + ncols].rearrange("(c p) n -> p c n", p=128),
                          r=[B_wb[("in", l)]], w=[bwb])
                    return wb, bwb

                if kind == "fm":
                    def cp(hnd, items=items, t0=t0):
                        wb, bwb = hnd
                        for hi, (tens, tbufs, h, gain) in enumerate(items):
                            pa, bpa = psA.next()

                            def mmf(e, pa=pa, wb=wb, hi=hi):
                                ins = None
                                for kc in range(16):
                                    ins = e.matmul(pa, wb[:, kc, hi * 128:(hi + 1) * 128], hT[:, kc, :],
                                                   start=(kc == 0), stop=(kc == 15))
                                return ins

                            P.op("pe", mmf, r=[bwb, bhT], w=[bpa])
                            ot, bot = ots.next()
                            if gain is None:
                                P.op("act", lambda e, ot=ot, pa=pa: e.copy(out=ot, in_=pa), r=[bpa], w=[bot])
                            else:
                                sq, bsq = sqs.next()
                                rr, brr = rrs.next()
                                pb, bpb = psB.next()
                                P.op("act", lambda e, sq=sq, pa=pa: e.activation(out=sq, in_=pa, func=AF.Square),
                                     r=[bpa], w=[bsq])
                                P.op("pe", lambda e, pb=pb, sq=sq: e.matmul(pb, ones[:], sq, start=True, stop=True),
                                     r=[bsq, B_const], w=[bpb])
                                P.op("act", lambda e, rr=rr, pb=pb: e.activation(out=rr, in_=pb, func=AF.Sqrt,
                                                                                 bias=float(HD * EPS)),
                                     r=[bpb], w=[brr])
                                P.op("dve", lambda e, rr=rr: e.reciprocal(out=rr, in_=rr), r=[brr], w=[brr])
                                P.op("dve", lambda e, ot=ot, pa=pa, rr=rr, gain=gain: e.scalar_tensor_tensor(
                                    out=ot, in0=pa, scalar=prm[:, 32 + gain:33 + gain], in1=rr, op0=ALU.mult,
                                    op1=ALU.mult), r=[bpa, brr, B_prm], w=[bot])
                            P.dma("sp", tens[h, :, t0:t0 + GT], ot, r=[bot], w=[tbufs[h]])
                else:
                    def cp(hnd, items=items, t0=t0, ncols=ncols):
                        wb, bwb = hnd
                        vh0, nh, has_g = items
                        for j in range(4):
                            pt_, bpt = psT.next()

                            def mmt(e, pt_=pt_, wb=wb, j=j, ncols=ncols):
                                ins = None
                                for kc in range(16):
                                    ins = e.matmul(pt_[:, :ncols], hT[:, kc, j * 128:(j + 1) * 128], wb[:, kc, :ncols],
                                                   start=(kc == 0), stop=(kc == 15))
                                return ins

                            P.op("pe", mmt, r=[bwb, bhT], w=[bpt])
                            vt, bvt = vts.next()
                            P.op("act", lambda e, vt=vt, pt_=pt_, nh=nh: e.copy(out=vt[:, :nh * 128],
                                                                                in_=pt_[:, :nh * 128]),
                                 r=[bpt], w=[bvt])
                            r0 = t0 + 128 * j
                            P.dma("sp", vv[vh0:vh0 + nh, r0:r0 + 128, :].rearrange("h p d -> p h d"),
                                  vt[:, :nh * 128].rearrange("p (h d) -> p h d", h=nh), r=[bvt],
                                  w=[B_vv[vh0 + i] for i in range(nh)])
                            if has_g:
                                gt, bgt = gts.next()
                                P.op("act", lambda e, gt=gt, pt_=pt_: e.activation(out=gt, in_=pt_[:, 256:274],
                                                                                    func=AF.Sigmoid),
                                     r=[bpt], w=[bgt])
                                P.dma("sp", gates[r0:r0 + 128, :], gt, r=[bgt], w=[B_gates])
                stages.append((ld_w, cp))
        run_pipeline(stages)
        P.barrier()

    def phase2(l, after_loads=None):
        A.off = 0
        kcT = A.alloc(2 * 256).rearrange("p (k n) -> p k n", k=2)
        b_kcT = Buf("kcT")
        VC1 = A.alloc(2 * 2 * 200).rearrange("p (k a n) -> p k a n", k=2, a=2)
        b_VC1 = Buf("VC1")
        expn = A.alloc(4096)
        b_expn = Buf("expn")
        keepm = A.alloc(NT * 64, F32)
        addm = A.alloc(NT * 64, F32)
        b_km = Buf("km")
        gsb = A.alloc(NT * 18, F32).rearrange("p (i c) -> p i c", c=18)
        b_gsb = Buf("gsb")
        mark0 = A.off
        P.dma("sp", expn[0:64, :], c_expn, w=[b_expn])
        P.dma("sp", keepm, c_keep, w=[b_km])
        P.dma("sp", addm, c_add, w=[b_km])
        P.dma("sp", gsb, gates.rearrange("(i p) c -> p i c", p=128), r=[B_gates], w=[b_gsb])

        XTs = [(A.alloc(T), Buf(f"XT{i}")) for i in range(2)]
        W1s = [(A.alloc(32 * 128).rearrange("p (l f) -> p l f", l=32), Buf(f"W1{i}")) for i in range(2)]
        peTf = A.alloc(32, F32)
        peT = A.alloc(32)
        b_peT = Buf("peT")
        W2 = A.alloc(128)
        b_W2 = Buf("W2")
        cb = A.alloc(1, F32)
        b_cb = Buf("cb")
        uu = A.alloc(256, F32)
        t1 = A.alloc(256, F32)
        sg = A.alloc(256, F32)
        b_u = Buf("u")
        GT_ = A.alloc(256)
        b_G = Buf("G")
        sq = A.alloc(256)
        b_sq = Buf("sqc")
        rr = A.alloc(256, F32)
        b_rr = Buf("rrc")
        P.op("dve", lambda e: e.memset(GT_, 0.0), w=[b_G])
        P.op("dve", lambda e: e.memset(VC1.rearrange("p k a n -> p (k a n)"), 0.0), w=[b_VC1])
        it = 0
        for k in range(2):
            for kv in range(2):
                XT, bXT = XTs[it % 2]
                W1, bW1 = W1s[it % 2]
                it += 1
                src_h = (KT_KCMP if kv == 0 else KT_VCMP) + k
                P.dma("sp", XT, kt[src_h], r=[B_kt[src_h]], w=[bXT])
                P.dma("pool", W1, cmp_w1[l, kv].rearrange("(l d) f -> d l f", d=128), w=[bW1])
                P.dma("pool", W2, cmp_w2[l, kv], w=[b_W2])
                P.dma("sp", peTf, cmp_pe[l, kv].rearrange("l d -> d l"), w=[b_peT], allow_slow_non_contiguous=True)
                P.op("dve", lambda e: e.tensor_copy(out=peT, in_=peTf), r=[b_peT], w=[b_peT])
                p0, bp0 = ps[0]
                p1, bp1 = ps[1]
                p2, bp2 = ps[2]

                def mm_h(e, XT=XT, W1=W1):
                    ins = None
                    for li in range(32):
                        ins = e.matmul(p0[:, 0:255], W1[:, li, :], XT[:, li:li + 16 * 254 + 1:16], start=(li == 0),
                                       stop=(li == 31))
                    return ins

                P.op("pe", mm_h, r=[bXT, bW1], w=[bp0])

                def mm_c(e, W1=W1):
                    ins = None
                    for li in range(32):
                        ins = e.matmul(p1[:, 0:1], W1[:, li, :], peT[:, li:li + 1], start=(li == 0), stop=(li == 31))
                    return ins

                P.op("pe", mm_c, r=[b_peT, bW1], w=[bp1])
                P.op("dve", lambda e: e.tensor_copy(out=cb, in_=p1[:, 0:1]), r=[bp1], w=[b_cb])
                P.op("act", lambda e: e.activation(out=uu[:, 0:255], in_=p0[:, 0:255], func=AF.Identity, bias=cb),
                     r=[bp0, b_cb], w=[b_u])
                P.op("dve", lambda e: e.tensor_tensor(out=t1[:, 0:255], in0=uu[:, 0:255], in1=uu[:, 0:255],
                                                      op=ALU.mult), r=[b_u], w=[b_u])
                P.op("dve", lambda e: e.tensor_scalar(out=t1[:, 0:255], in0=t1[:, 0:255], scalar1=0.044715, scalar2=1.0,
                                                      op0=ALU.mult, op1=ALU.add), r=[b_u], w=[b_u])
                P.op("dve", lambda e: e.tensor_tensor(out=t1[:, 0:255], in0=t1[:, 0:255], in1=uu[:, 0:255],
                                                      op=ALU.mult), r=[b_u], w=[b_u])
                P.op("act", lambda e: e.activation(out=sg[:, 0:255], in_=t1[:, 0:255], func=AF.Sigmoid,
                                                   scale=1.5957691216057308), r=[b_u], w=[b_u])
                P.op("dve", lambda e: e.tensor_tensor(out=GT_[:, 0:255], in0=uu[:, 0:255], in1=sg[:, 0:255],
                                                      op=ALU.mult), r=[b_u], w=[b_G])
                if kv == 0:
                    P.op("pe", lambda e: e.matmul(p2[:, 0:256], W2, GT_, start=True, stop=True), r=[b_W2, b_G],
                         w=[bp2])
                    P.op("act", lambda e: e.activation(out=sq, in_=p2[:, 0:256], func=AF.Square), r=[bp2], w=[b_sq])
                    P.op("pe", lambda e: e.matmul(p1[:, 0:256], ones[:], sq, start=True, stop=True),
                         r=[b_sq, B_const, b_cb], w=[bp1])
                    P.op("act", lambda e: e.activation(out=rr, in_=p1[:, 0:256], func=AF.Sqrt, bias=float(HD * EPS)),
                         r=[bp1], w=[b_rr])
                    P.op("dve", lambda e: e.reciprocal(out=rr, in_=rr), r=[b_rr], w=[b_rr])
                    P.op("dve", lambda e, k=k: e.scalar_tensor_tensor(out=kcT[:, k, :], in0=p2[:, 0:256],
                                                                      scalar=prm[:, 33:34], in1=rr, op0=ALU.mult,
                                                                      op1=ALU.mult),
                         r=[bp2, b_rr, B_prm], w=[b_kcT])
                else:
                    for a in range(2):
                        pa, bpa = ps[3 + a]
                        P.op("pe", lambda e, pa=pa, a=a: e.matmul(pa[:, 0:128], GT_[:, a * 128:(a + 1) * 128], W2,
                                                                  start=True, stop=True), r=[b_W2, b_G], w=[bpa])
                        P.op("act", lambda e, pa=pa, a=a, k=k: e.copy(out=VC1[:, k, a, 0:128], in_=pa[:, 0:128]),
                             r=[bpa], w=[b_VC1])
        for k in range(2):
            for a in range(2):
                P.op("dve", lambda e, k=k, a=a: e.memset(VC1[:, k, a, 128:129], 1.0), w=[b_VC1])
                P.dma("sp", VC1[:, k, a, 129:193], c_ovl[:, a, :], w=[b_VC1])
        P.barrier()

        A.off = mark0
        psS = Rot(ps[0:3] + [(pst_t[1][:, :].bitcast(F32), pst[1][1])])
        psO = Rot(ps[3:6])
        PTs = Rot([(A.alloc(512), Buf(f"PT{i}")) for i in range(6)])
        sm = A.alloc(64, F32)
        b_sm = Buf("sm")
        mark1 = A.off

        fifo = []
        DEPTH = 3

        def push(fn):
            fifo.append(fn)
            while len(fifo) > DEPTH:
                fifo.pop(0)()

        def flush():
            while fifo:
                fifo.pop(0)()

        def attn(qT, bq, KT, bK, V1, bV, E, bE, i, jlist, po, bpo, ncol, mask=None):
            nchunks = (len(jlist) + 3) // 4
            for c in range(nchunks):
                chunk = jlist[4 * c:4 * c + 4]
                n = 128 * len(chunk)
                pS, bpS = psS.next()
                PT, bPT = PTs.next()

                def mmS(e, chunk=chunk, pS=pS):
                    ins = None
                    for jj, j in enumerate(chunk):
                        ins = e.matmul(pS[:, jj * 128:(jj + 1) * 128], KT[:, j * 128:(j + 1) * 128], qT, start=True,
                                       stop=(mask is None))
                        if mask is not None:
                            ins = e.matmul(pS[:, jj * 128:(jj + 1) * 128], expn[0:64, j * 128:(j + 1) * 128],
                                           mask[0][0:64, :], start=False, stop=True)
                    return ins

                rds = [bq, bK] + ([b_expn, mask[1]] if mask is not None else [])
                P.op("pe", mmS, r=rds, w=[bpS])
                P.op("act", lambda e, PT=PT, pS=pS, n=n: e.activation(out=PT[:, :n], in_=pS[:, :n], func=AF.Exp,
                                                                      scale=float(SCALE)), r=[bpS], w=[bPT])
                d0 = i - chunk[0]
                P.op("dve", lambda e, PT=PT, n=n, d0=d0: e.tensor_tensor(out=PT[:, :n], in0=PT[:, :n],
                                                                         in1=E[:, 128 * d0:128 * d0 + n], op=ALU.mult),
                     r=[bPT, bE], w=[bPT])

                def mmO(e, chunk=chunk, PT=PT, c=c):
                    ins = None
                    for jj, j in enumerate(chunk):
                        ins = e.matmul(po[:, :ncol], PT[:, jj * 128:(jj + 1) * 128], V1[:, j, :ncol],
                                       start=(c == 0 and jj == 0), stop=(c == nchunks - 1 and jj == len(chunk) - 1))
                    return ins

                push(lambda mmO=mmO, bPT=bPT: P.op("pe", mmO, r=[bPT, bV], w=[bpo]))

        def load_v1(dst, bdst, vh):
            P.dma("sp", dst[:, :, 0:128], vv[vh].rearrange("(i p) d -> p i d", p=128), r=[B_vv[vh]], w=[bdst])
            P.op("dve", lambda e: e.memset(dst[:, :, 128:129], 1.0), w=[bdst])

        def ecols(name):
            toff, pstep, width, ecol, start = PLAN[name]
            return ecol, width

        for k in range(2):
            A.off = mark1
            QT3 = A.alloc(3 * T).rearrange("p (h t) -> p h t", h=3)
            b_QT3 = Buf("QT3")
            Ks = A.alloc(T)
            Kw = A.alloc(T)
            b_Ks = Buf("Ks")
            b_Kw = Buf("Kw")
            Vs = A.alloc(NT * 130).rearrange("p (i d) -> p i d", d=130)
            Vw = A.alloc(NT * 130).rearrange("p (i d) -> p i d", d=130)
            b_Vs = Buf("Vs")
            b_Vw = Buf("Vw")
            Ec = A.alloc(3 * 6144).rearrange("p (h n) -> p h n", h=3)
            Esl = A.alloc(3 * 4096).rearrange("p (h n) -> p h n", h=3)
            Ew = A.alloc(3 * 640).rearrange("p (h n) -> p h n", h=3)
            b_E = Buf("E")
            accs = [(A.alloc(128, F32), Buf(f"acc{g}")) for g in range(3)]
            impa = A.alloc(64, F32)
            b_imp = Buf("imp")
            wk = A.alloc(64, F32)
            m8 = A.alloc(16, F32)
            Mb = A.alloc(64)
            b_M = Buf("Mb")
            MT = A.alloc(128)
            b_MT = Buf("MT")
            mixA = Rot([(A.alloc(384), Buf(f"mixA{i}")) for i in range(2)])
            for g in range(3):
                h = 3 * k + g
                P.dma("sp", QT3[:, g, :], qt[h], r=[B_qt[h]], w=[b_QT3])
                ec, w_ = ecols(("cmp0", h))
                P.dma("sp", Ec[:, g, 0:4096], etab[:, ec:ec + 4096], r=[B_etab], w=[b_E])
                ec, w_ = ecols(("cmp1", h))
                P.dma("sp", Ec[:, g, 4096:6144], etab[:, ec:ec + 2048], r=[B_etab], w=[b_E])
                ec, w_ = ecols(("slc", h))
                P.dma("sp", Esl[:, g, :], etab[:, ec:ec + 4096], r=[B_etab], w=[b_E])
                ec, w_ = ecols(("win", h))
                P.dma("sp", Ew[:, g, :], etab[:, ec:ec + 640], r=[B_etab], w=[b_E])
            P.dma("sp", Ks, kt[KT_KSLC + k], r=[B_kt[KT_KSLC + k]], w=[b_Ks])
            P.dma("sp", Kw, kt[KT_KWIN + k], r=[B_kt[KT_KWIN + k]], w=[b_Kw])
            load_v1(Vs, b_Vs, V_SLC + k)
            load_v1(Vw, b_Vw, V_WIN + k)
            if k == 0 and after_loads is not None:
                after_loads()

            for i in range(NT):
                tq = slice(i * 128, (i + 1) * 128)
                for g in range(3):
                    h = 3 * k + g
                    acc, bacc = accs[g]
                    po, bpo = psO.next()
                    na = 2 if i >= 16 else 1
                    pS, bpS = psS.next()
                    PT, bPT = PTs.next()

                    def mmS(e, pS=pS, g=g, na=na, tq=tq, k=k):
                        ins = None
                        for a in range(na):
                            ins = e.matmul(pS[:, a * 128:(a + 1) * 128], kcT[:, k, a * 128:(a + 1) * 128],
                                           QT3[:, g, tq], start=True, stop=True)
                        return ins

                    P.op("pe", mmS, r=[b_kcT, b_QT3], w=[bpS])
                    n = 128 * na
                    P.op("act", lambda e, PT=PT, pS=pS, n=n: e.activation(out=PT[:, :n], in_=pS[:, :n], func=AF.Exp,
                                                                          scale=float(SCALE)), r=[bpS], w=[bPT])
                    P.op("dve", lambda e, PT=PT, g=g, i=i: e.tensor_tensor(
                        out=PT[:, 0:128], in0=PT[:, 0:128], in1=Ec[:, g, i * 128:(i + 1) * 128], op=ALU.mult),
                         r=[bPT, b_E], w=[bPT])
                    if na == 2:
                        P.op("dve", lambda e, PT=PT, g=g, i=i: e.tensor_tensor(
                            out=PT[:, 128:256], in0=PT[:, 128:256],
                            in1=Ec[:, g, 4096 + (i - 16) * 128:4096 + (i - 15) * 128], op=ALU.mult),
                             r=[bPT, b_E], w=[bPT])

                    def mmO(e, PT=PT, po=po, na=na, k=k):
                        ins = None
                        for a in range(na):
                            ins = e.matmul(po[:, 0:193], PT[:, a * 128:(a + 1) * 128], VC1[:, k, a, 0:193],
                                           start=(a == 0), stop=(a == na - 1))
                        return ins

                    def cmp_tail(mmO=mmO, bPT=bPT, po=po, bpo=bpo, acc=acc, bacc=bacc, g=g, i=i, k=k):
                        P.op("pe", mmO, r=[bPT, b_VC1], w=[bpo])
                        c0 = 4 * g
                        P.op("dve", lambda e: e.tensor_scalar(out=sm[:, c0:c0 + 1], in0=po[:, 128:129],
                                                              scalar1=1e-30, scalar2=None, op0=ALU.max),
                             r=[bpo], w=[b_sm])
                        P.op("dve", lambda e: e.reciprocal(out=sm[:, c0:c0 + 1], in_=sm[:, c0:c0 + 1]),
                             r=[b_sm], w=[b_sm])
                        P.op("dve", lambda e: e.tensor_tensor(
                            out=sm[:, c0 + 1:c0 + 2], in0=sm[:, c0:c0 + 1],
                            in1=gsb[:, i, 9 * k + 3 * g:9 * k + 3 * g + 1], op=ALU.mult),
                             r=[b_sm, b_gsb], w=[b_sm])
                        P.op("act", lambda e: e.activation(out=acc, in_=po[:, 0:128], func=AF.Copy,
                                                           scale=sm[:, c0 + 1:c0 + 2]),
                             r=[bpo, b_sm], w=[bacc])
                        if g == 0:
                            P.op("dve", lambda e: e.tensor_scalar(out=impa, in0=po[:, 129:193],
                                                                  scalar1=sm[:, c0:c0 + 1], scalar2=None,
                                                                  op0=ALU.mult),
                                 r=[bpo, b_sm], w=[b_imp])
                        else:
                            P.op("dve", lambda e: e.scalar_tensor_tensor(
                                out=impa, in0=po[:, 129:193], scalar=sm[:, c0:c0 + 1], in1=impa, op0=ALU.mult,
                                op1=ALU.add), r=[bpo, b_sm, b_imp], w=[b_imp])

                    push(cmp_tail)
                def branch(g, br, i=i, tq=tq, k=k):
                    acc, bacc = accs[g]
                    po, bpo = psO.next()
                    if br == 1:
                        jl = list(range(i, -1, -1))
                        attn(QT3[:, g, tq], b_QT3, Ks, b_Ks, Vs, b_Vs, Esl[:, g, :], b_E, i, jl, po, bpo, 129,
                             mask=(MT, b_MT))
                    else:
                        jl = list(range(i, max(i - 4, 0) - 1, -1))
                        attn(QT3[:, g, tq], b_QT3, Kw, b_Kw, Vw, b_Vw, Ew[:, g, :], b_E, i, jl, po, bpo, 129)

                    def br_tail(po=po, bpo=bpo, acc=acc, bacc=bacc, g=g, br=br):
                        c0 = 4 * g + 2
                        P.op("dve", lambda e: e.reciprocal(out=sm[:, c0:c0 + 1], in_=po[:, 128:129]),
                             r=[bpo], w=[b_sm])
                        P.op("dve", lambda e: e.tensor_tensor(
                            out=sm[:, c0:c0 + 1], in0=sm[:, c0:c0 + 1],
                            in1=gsb[:, i, 9 * k + 3 * g + br:9 * k + 3 * g + br + 1], op=ALU.mult),
                             r=[b_sm, b_gsb], w=[b_sm])
                        P.op("dve", lambda e: e.scalar_tensor_tensor(
                            out=acc, in0=po[:, 0:128], scalar=sm[:, c0:c0 + 1], in1=acc, op0=ALU.mult, op1=ALU.add),
                             r=[bpo, b_sm, bacc], w=[bacc])

                    push(br_tail)

                for g in range(3):
                    branch(g, 2)
                flush()
                P.op("dve", lambda e, i=i: e.tensor_tensor(out=impa, in0=impa, in1=keepm[:, i * 64:(i + 1) * 64],
                                                           op=ALU.mult), r=[b_imp, b_km], w=[b_imp])
                P.op("dve", lambda e, i=i: e.tensor_tensor(out=impa, in0=impa, in1=addm[:, i * 64:(i + 1) * 64],
                                                           op=ALU.add), r=[b_imp, b_km], w=[b_imp])
                P.op("dve", lambda e: e.max(out=m8[:, 0:8], in_=impa), r=[b_imp], w=[b_sm])
                P.op("dve", lambda e: e.match_replace(out=wk, in_to_replace=m8[:, 0:8], in_values=impa,
                                                      imm_value=-2.0e30), r=[b_imp, b_sm], w=[b_sm])
                P.op("dve", lambda e: e.max(out=m8[:, 8:16], in_=wk), r=[b_sm], w=[b_sm])
                P.op("dve", lambda e: e.tensor_scalar(out=Mb, in0=impa, scalar1=m8[:, 15:16], scalar2=30000.0,
                                                      op0=ALU.is_lt, op1=ALU.mult), r=[b_imp, b_sm], w=[b_M])
                pp, bpp = pst[0]
                P.op("pe", lambda e, pp=pp: e.transpose(pp[0:64, 0:128], Mb, ident[:]), r=[b_M, B_const], w=[bpp])
                P.op("act", lambda e, pp=pp: e.copy(out=MT[0:64, :], in_=pp[0:64, 0:128]), r=[bpp], w=[b_MT])
                for g in range(3):
                    branch(g, 1)

                def mix_tail(i=i, k=k):
                    mx, bmx = mixA.next()
                    for g in range(3):
                        acc, bacc = accs[g]
                        P.op("act", lambda e, mx=mx, acc=acc, g=g: e.copy(out=mx[:, g * 128:(g + 1) * 128], in_=acc),
                             r=[bacc], w=[bmx])
                    P.dma("sp", mix[i * 128:(i + 1) * 128, 384 * k:384 * (k + 1)], mx, r=[bmx], w=[B_mix])

                push(mix_tail)
            flush()
            P.barrier()

        for kvh in range(2):
            A.off = mark1
            QT2 = A.alloc(2 * T).rearrange("p (h t) -> p h t", h=2)
            b_QT2 = Buf("QT2")
            Kb = A.alloc(T)
            b_Kb = Buf("Kb")
            Vb = A.alloc(NT * 130).rearrange("p (i d) -> p i d", d=130)
            b_Vb = Buf("Vb")
            Eb2 = A.alloc(2 * 256).rearrange("p (h n) -> p h n", h=2)
            b_Eb2 = Buf("Eb2")
            mixB = Rot([(A.alloc(256), Buf(f"mixB{i}")) for i in range(2)])
            for hh in range(2):
                h = 6 + 2 * kvh + hh
                P.dma("sp", QT2[:, hh, :], qt[h], r=[B_qt[h]], w=[b_QT2])
                ec, w_ = ecols(("b", 2 * kvh + hh))
                P.dma("sp", Eb2[:, hh, :], etab[:, ec:ec + 256], r=[B_etab], w=[b_Eb2])
            P.dma("sp", Kb, kt[KT_KB + kvh], r=[B_kt[KT_KB + kvh]], w=[b_Kb])
            load_v1(Vb, b_Vb, V_B + kvh)
            for i in range(NT):
                tq = slice(i * 128, (i + 1) * 128)
                mx, bmx = mixB.next()
                for hh in range(2):
                    po, bpo = psO.next()
                    jl = list(range(i, max(i - 1, 0) - 1, -1))
                    attn(QT2[:, hh, tq], b_QT2, Kb, b_Kb, Vb, b_Vb, Eb2[:, hh, :], b_Eb2, i, jl, po, bpo, 129)
                    def b_tail(po=po, bpo=bpo, mx=mx, bmx=bmx, hh=hh, kvh=kvh):
                        c0 = 16 + hh
                        sc = 40 + 2 * kvh + hh
                        P.op("dve", lambda e: e.tensor_tensor(
                            out=sm[:, c0:c0 + 1], in0=po[:, 128:129], in1=prm[:, sc:sc + 1], op=ALU.add),
                             r=[bpo, B_prm], w=[b_sm])
                        P.op("dve", lambda e: e.reciprocal(out=sm[:, c0:c0 + 1], in_=sm[:, c0:c0 + 1]),
                             r=[b_sm], w=[b_sm])
                        P.op("act", lambda e: e.activation(
                            out=mx[:, hh * 128:(hh + 1) * 128], in_=po[:, 0:128], func=AF.Copy,
                            scale=sm[:, c0:c0 + 1]), r=[bpo, b_sm], w=[bmx])

                    push(b_tail)

                def bmix_tail(i=i, kvh=kvh, mx=mx, bmx=bmx):
                    c1 = (6 + 2 * kvh) * 128
                    P.dma("sp", mix[i * 128:(i + 1) * 128, c1:c1 + 256], mx, r=[bmx], w=[B_mix])

                push(bmix_tail)
            flush()
            P.barrier()

        for s in range(2):
            A.off = mark1
            QTc = A.alloc(3 * T).rearrange("p (h t) -> p h t", h=3)
            b_QTc = Buf("QTc")
            KTc = A.alloc(3 * T).rearrange("p (h t) -> p h t", h=3)
            b_KTc = Buf("KTc")
            Vc = [A.alloc(NT * 130).rearrange("p (i d) -> p i d", d=130) for g in range(3)]
            b_Vc = [Buf(f"Vc{g}") for g in range(3)]
            wtot = sum(128 * (C_NPREV[g] + 1) for g in range(3))
            Ecc = A.alloc(wtot)
            b_Ecc = Buf("Ecc")
            eoff = []
            o = 0
            for g in range(3):
                eoff.append(o)
                o += 128 * (C_NPREV[g] + 1)
            mixC = Rot([(A.alloc(384), Buf(f"mixC{i}")) for i in range(2)])
            for g in range(3):
                h = 10 + 2 * g + s
                P.dma("sp", QTc[:, g, :], qt[h], r=[B_qt[h]], w=[b_QTc])
                P.dma("sp", KTc[:, g, :], kt[KT_KC + g], r=[B_kt[KT_KC + g]], w=[b_KTc])
                load_v1(Vc[g], b_Vc[g], V_C + g)
                ec, w_ = ecols(("c", g, s))
                P.dma("sp", Ecc[:, eoff[g]:eoff[g] + w_], etab[:, ec:ec + w_], r=[B_etab], w=[b_Ecc])
            for i in range(NT):
                tq = slice(i * 128, (i + 1) * 128)
                pos = []
                for g in range(3):
                    po, bpo = psO.next()
                    pos.append((po, bpo))
                    jl = list(range(i, max(i - C_NPREV[g], 0) - 1, -1))
                    attn(QTc[:, g, tq], b_QTc, KTc[:, g, :], b_KTc, Vc[g], b_Vc[g],
                         Ecc[:, eoff[g]:eoff[g] + 128 * (C_NPREV[g] + 1)], b_Ecc, i, jl, po, bpo, 129)
                    def c_den(po=po, bpo=bpo, g=g):
                        P.op("dve", lambda e: e.tensor_copy(out=sm[:, 20 + g:21 + g], in_=po[:, 128:129]),
                             r=[bpo], w=[b_sm])

                    push(c_den)

                def c_tail(pos=pos, i=i, s=s):
                    P.op("dve", lambda e: e.tensor_tensor(out=sm[:, 23:24], in0=sm[:, 20:21], in1=sm[:, 21:22],
                                                          op=ALU.add), r=[b_sm], w=[b_sm])
                    P.op("dve", lambda e: e.tensor_tensor(out=sm[:, 23:24], in0=sm[:, 23:24], in1=sm[:, 22:23],
                                                          op=ALU.add), r=[b_sm], w=[b_sm])
                    P.op("dve", lambda e: e.reciprocal(out=sm[:, 24:25], in_=sm[:, 23:24]), r=[b_sm], w=[b_sm])
                    mx, bmx = mixC.next()
                    for g in range(3):
                        po, bpo = pos[g]
                        P.op("act", lambda e, mx=mx, po=po, g=g: e.activation(
                            out=mx[:, g * 128:(g + 1) * 128], in_=po[:, 0:128], func=AF.Copy, scale=sm[:, 24:25]),
                             r=[bpo, b_sm], w=[bmx])
                    dst = mix[i * 128:(i + 1) * 128, 1280:2048].rearrange("p (g x) -> p g x", g=3)[:, :,
                                                                                                    s * 128:(s + 1) * 128]
                    P.dma("sp", dst, mx.rearrange("p (g d) -> p g d", g=3), r=[bmx], w=[B_mix])

                push(c_tail)
            flush()
            P.barrier()

    def phase3(l, x_src, B_xsrc, x_dst, B_xdst):
        A.off = 0
        xts = [(A.alloc(D, F32), Buf(f"xt{j}")) for j in range(4)]
        xns = [(A.alloc(D), Buf(f"xn{j}")) for j in range(4)]
        junk = (A.alloc(D), Buf("junk"))
        ssb = (A.alloc(8, F32), Buf("ss"))
        hT = A.alloc(16 * 512).rearrange("p (c t) -> p c t", c=16)
        bhT = Buf("hT")
        actT = A.alloc(44 * 512).rearrange("p (c t) -> p c t", c=44)
        b_act = [Buf(f"act{c}") for c in range(44)]
        wbufs = Rot([(A.alloc(16 * 512).rearrange("p (c n) -> p c n", c=16), Buf(f"wb{i}")) for i in range(4)])
        sgs = Rot([(A.alloc(512, F32), Buf(f"sg{i}")) for i in range(2)])
        psG = Rot(ps[0:2])
        psU = Rot(ps[2:4])
        psW = Rot(ps[4:6])

        stages = []
        for g in range(NG):
            t0 = g * GT

            def ld_x(t0=t0):
                for j in range(4):
                    P.dma("sp", xns[j][0], mix[t0 + 128 * j:t0 + 128 * (j + 1), :], r=[B_mix], w=[xns[j][1]])

            def cp_x(_h, t0=t0):
                for j in range(4):
                    P.dma("sp", xts[j][0], x_src[t0 + 128 * j:t0 + 128 * (j + 1), :], r=[B_xsrc], w=[xts[j][1]])
                transpose_group([x[0] for x in xns], [x[1] for x in xns], hT, bhT, None)

            stages.append((ld_x, cp_x))
            for cb in range(4):
                def ld_wo(cb=cb):
                    wb, bwb = wbufs.next()
                    P.dma("sp", wb, wb_out[l, :, cb * 512:(cb + 1) * 512].rearrange("(c p) n -> p c n", p=128),
                          r=[B_wb[("out", l)]], w=[bwb])
                    return wb, bwb

                def cp_wo(hnd, cb=cb):
                    wb, bwb = hnd
                    for j in range(4):
                        pw, bpw = psW.next()

                        def mmw(e, pw=pw, wb=wb, j=j):
                            ins = None
                            for mc in range(16):
                                ins = e.matmul(pw, hT[:, mc, j * 128:(j + 1) * 128], wb[:, mc, :], start=(mc == 0),
                                               stop=(mc == 15))
                            return ins

                        P.op("pe", mmw, r=[bwb, bhT], w=[bpw])
                        xt, bxt = xts[j]
                        P.op("dve", lambda e, pw=pw, xt=xt, cb=cb: e.tensor_tensor(
                            out=xt[:, cb * 512:(cb + 1) * 512], in0=pw, in1=xt[:, cb * 512:(cb + 1) * 512], op=ALU.add),
                             r=[bpw, bxt], w=[bxt])

                stages.append((ld_wo, cp_wo))

            def ld_none():
                return None

            def cp_norm(_h):
                norm_transpose(xts, xns, junk, ssb, hT, bhT, 16)

            stages.append((ld_none, cp_norm))
            for fb in range(11):
                def ld_gu(fb=fb):
                    wg, bwg = wbufs.next()
                    P.dma("sp", wg, wb_g[l, :, fb * 512:(fb + 1) * 512].rearrange("(c p) n -> p c n", p=128),
                          r=[B_wb[("g", l)]], w=[bwg])
                    wu, bwu = wbufs.next()
                    P.dma("sp", wu, wb_u[l, :, fb * 512:(fb + 1) * 512].rearrange("(c p) n -> p c n", p=128),
                          r=[B_wb[("u", l)]], w=[bwu])
                    return wg, bwg, wu, bwu

                def cp_gu(hnd, fb=fb):
                    wg, bwg, wu, bwu = hnd
                    for q in range(4):
                        fc = 4 * fb + q
                        pg, bpg = psG.next()
                        pu, bpu = psU.next()

                        def mmg(e, pg=pg, wg=wg, q=q):
                            ins = None
                            for kc in range(16):
                                ins = e.matmul(pg, wg[:, kc, q * 128:(q + 1) * 128], hT[:, kc, :], start=(kc == 0),
                                               stop=(kc == 15))
                            return ins

                        def mmu(e, pu=pu, wu=wu, q=q):
                            ins = None
                            for kc in range(16):
                                ins = e.matmul(pu, wu[:, kc, q * 128:(q + 1) * 128], hT[:, kc, :], start=(kc == 0),
                                               stop=(kc == 15))
                            return ins

                        P.op("pe", mmg, r=[bwg, bhT], w=[bpg])
                        P.op("pe", mmu, r=[bwu, bhT], w=[bpu])
                        sg_, bsg = sgs.next()
                        P.op("act", lambda e, sg_=sg_, pg=pg: e.activation(out=sg_, in_=pg, func=AF.Silu), r=[bpg],
                             w=[bsg])
                        P.op("dve", lambda e, sg_=sg_, pu=pu, fc=fc: e.tensor_tensor(out=actT[:, fc, :], in0=pu,
                                                                                     in1=sg_, op=ALU.mult),
                             r=[bsg, bpu], w=[b_act[fc]])

                stages.append((ld_gu, cp_gu))
            for cb in range(4):
                for fg in range(3):
                    fc0 = 16 * fg
                    nfc = 16 if fg < 2 else 12

                    def ld_d(cb=cb, fc0=fc0, nfc=nfc):
                        wd, bwd = wbufs.next()
                        P.dma("sp", wd[:, :nfc, :],
                              wb_d[l, fc0 * 128:(fc0 + nfc) * 128, cb * 512:(cb + 1) * 512].rearrange(
                                  "(c p) n -> p c n", p=128), r=[B_wb[("d", l)]], w=[bwd])
                        return wd, bwd

                    def cp_d(hnd, cb=cb, fg=fg, fc0=fc0, nfc=nfc, t0=t0):
                        wd, bwd = hnd
                        for j in range(4):
                            pd, bpd = ps[j]

                            def mmd(e, pd=pd, wd=wd, j=j):
                                ins = None
                                for f in range(nfc):
                                    ins = e.matmul(pd, actT[:, fc0 + f, j * 128:(j + 1) * 128], wd[:, f, :],
                                                   start=(fg == 0 and f == 0), stop=(fg == 2 and f == nfc - 1))
                                return ins

                            P.op("pe", mmd, r=[bwd] + b_act[fc0:fc0 + nfc], w=[bpd])
                            if fg == 2:
                                xt, bxt = xts[j]
                                P.op("dve", lambda e, pd=pd, xt=xt: e.tensor_tensor(
                                    out=xt[:, cb * 512:(cb + 1) * 512], in0=pd, in1=xt[:, cb * 512:(cb + 1) * 512],
                                    op=ALU.add), r=[bpd, bxt], w=[bxt])
                                if cb == 3:
                                    P.dma("pool", x_dst[t0 + 128 * j:t0 + 128 * (j + 1), :], xt, r=[bxt], w=[B_xdst])

                    stages.append((ld_d, cp_d))
        run_pipeline(stages)
        P.barrier()

    B_x = Buf("x")
    for l in range(n_layers):
        load_params(l)
        src, bsrc = (x_in, B_x) if l == 0 else (xres, B_xres)
        last = (l == n_layers - 1)
        dst, bdst = (y_out, B_y) if last else (xres, B_xres)
        if "p1" in phases:
            phase1(l, src, bsrc)
        if l == 0 and "p0" in phases:
            phase0()
        if "p2" in phases:
            phase2(l, (lambda l=l: cast_weights(l + 1)) if l + 1 < n_layers else None)
        elif l + 1 < n_layers:
            cast_weights(l + 1)
        if "p3" in phases:
            phase3(l, src, bsrc, dst, bdst)
    P.barrier()
    P.emit()
    es.close()
    return nc


_CACHE = {}


def _host_inputs(inputs):
    consts = _build_consts()
    tabs = _build_tabs(np.asarray(inputs["rel_bias"], dtype=np.float32))
    shared = {
        "w_in": np.ascontiguousarray(inputs["w_in"], dtype=np.float32),
        "w_out": np.ascontiguousarray(inputs["w_out"], dtype=np.float32),
        "w_gate": np.ascontiguousarray(inputs["w_gate"], dtype=np.float32),
        "w_up": np.ascontiguousarray(inputs["w_up"], dtype=np.float32),
        "w_down": np.ascontiguousarray(inputs["w_down"], dtype=np.float32),
        "norm_attn": np.ascontiguousarray(inputs["norm_attn"], dtype=np.float32),
        "norm_ffn": np.ascontiguousarray(inputs["norm_ffn"], dtype=np.float32),
        "qk_gain": np.ascontiguousarray(inputs["qk_gain"], dtype=np.float32),
        "cmp_pe": np.ascontiguousarray(inputs["cmp_pe"], dtype=np.float32),
        "cmp_w1": np.ascontiguousarray(inputs["cmp_w1"], dtype=np.float32),
        "cmp_w2": np.ascontiguousarray(inputs["cmp_w2"], dtype=np.float32),
        "sinks": np.ascontiguousarray(inputs["sinks"], dtype=np.float32),
        "tabs": tabs,
        "c_ident": consts["ident"], "c_jrev": consts["jrev"], "c_ones": consts["ones"], "c_expn": consts["expn"],
        "c_ovl": consts["ovl"], "c_keepm": consts["keepm"], "c_addm": consts["addm"],
    }
    return shared


def kernel(x, norm_attn, w_in, qk_gain, cmp_pe, cmp_w1, cmp_w2, sinks, rel_bias, w_out, norm_ffn, w_gate, w_up,
           w_down):
    inputs = dict(x=x, norm_attn=norm_attn, w_in=w_in, qk_gain=qk_gain, cmp_pe=cmp_pe, cmp_w1=cmp_w1, cmp_w2=cmp_w2,
                  sinks=sinks, rel_bias=rel_bias, w_out=w_out, norm_ffn=norm_ffn, w_gate=w_gate, w_up=w_up,
                  w_down=w_down)
    shared = _host_inputs(inputs)
    x = np.asarray(x, dtype=np.float32)
    nc = build_program()
    in_maps = []
    for c in range(8):
        m = dict(shared)
        m["x"] = np.ascontiguousarray(x[c // 2])
        in_maps.append(m)
    res = run_bass_kernel_spmd(nc, in_maps, core_ids=list(range(8)))
    out = np.stack([np.asarray(res.results[2 * b]["y"], dtype=np.float32) for b in range(4)], axis=0)
    return out
```
